# Optimizing a Trainium2 kernel written in Bass

```python
import math
import jax, jax.numpy as jnp
from jax import lax
import numpy as np

D_MODEL = 1024
BATCH = 32
SEQ = 256
DEPTH = 2
DEC_BATCH = 4
DEC_SEQ = 2048
PAST_LEN = 256

GRID_W = 64
Q_BLOCK = 128
SSD_CHUNK = 128
ROPE_THETA = 10000.0
EPS = 1e-6

HEAD_DIM = 64
A_Q_HEADS = 6
A_KV_HEADS = 2
A_GROUP = A_Q_HEADS // A_KV_HEADS
A_WIDTH = A_Q_HEADS * HEAD_DIM
B_HEADS = 4
B_HALF = 32
B_WIDTH = B_HEADS * 2 * B_HALF
C_HEADS = 6
C_HEADDIM = 64
C_INNER = C_HEADS * C_HEADDIM
C_GROUPS = 2
C_STATE = 128
C_CONV = 3
C_CONV_CH = C_INNER + 2 * C_GROUPS * C_STATE

MIX_WIDTH = A_WIDTH + B_WIDTH + C_INNER
D_FF = 2816
FFN_CONV = 3

PROJ_SIZES = (A_WIDTH, A_KV_HEADS * HEAD_DIM, A_KV_HEADS * HEAD_DIM,
              B_WIDTH, B_WIDTH, B_WIDTH,
              C_INNER, C_INNER, C_GROUPS * C_STATE, C_GROUPS * C_STATE, 2 * C_HEADS)
PROJ_WIDTH = A_WIDTH + 4 * HEAD_DIM + 3 * B_WIDTH + 2 * C_INNER + 4 * C_STATE + 2 * C_HEADS

kernel_name = "hybrid_diffusion_prefix_context_step"


def rmsnorm(x, w=None):
    xf = x.astype(jnp.float32)
    y = xf * lax.rsqrt(jnp.mean(xf * xf, axis=-1, keepdims=True) + EPS)
    if w is not None:
        y = y * w.astype(jnp.float32)
    return y.astype(x.dtype)


def adanorm(x, shift, scale):
    return rmsnorm(x) * (1 + scale) + shift


def modulation(cvec, w, b):
    m = jax.nn.silu(cvec) @ w + b
    return [t[:, None, :] for t in jnp.split(m, 6, axis=-1)]


def axial_rope_tables(L, d):
    rows = L // GRID_W
    row = jnp.repeat(jnp.arange(rows), GRID_W).astype(jnp.float32)
    col = jnp.tile(jnp.arange(GRID_W), rows).astype(jnp.float32)
    quarter = d // 4
    inv = ROPE_THETA ** (-jnp.arange(quarter, dtype=jnp.float32) / quarter)
    ang_r = row[:, None] * inv[None, :]
    ang_c = col[:, None] * inv[None, :]
    ang = jnp.concatenate([ang_r, ang_r, ang_c, ang_c], axis=-1)
    return jnp.cos(ang), jnp.sin(ang)


def apply_rope(x, cos, sin):
    xf = x.astype(jnp.float32)
    q1, q2, q3, q4 = jnp.split(xf, 4, axis=-1)
    rot = jnp.concatenate([-q2, q1, -q4, q3], axis=-1)
    return (xf * cos + rot * sin).astype(x.dtype)


def dwconv(x, w, b):
    K = w.shape[0]
    y = lax.conv_general_dilated(x, w[:, None, :].astype(x.dtype), window_strides=(1,),
                                 padding=[(K // 2, K // 2)],
                                 dimension_numbers=('NWC', 'WIO', 'NWC'),
                                 feature_group_count=x.shape[-1])
    return y + b.astype(x.dtype)


def sweep_query_blocks(fn, q):
    b, L = q.shape[0], q.shape[1]
    nb = L // Q_BLOCK
    qb = jnp.moveaxis(q.reshape((b, nb, Q_BLOCK) + q.shape[2:]), 1, 0)
    ob = lax.map(fn, qb)
    return jnp.moveaxis(ob, 0, 1).reshape((b, L) + ob.shape[3:])


def gqa_attend(q, k, v):
    scale = q.shape[-1] ** -0.5

    def block(qb):
        s = jnp.einsum('bqhgd,bkhd->bhgqk', qb, k).astype(jnp.float32) * scale
        p = jax.nn.softmax(s, axis=-1).astype(v.dtype)
        return jnp.einsum('bhgqk,bkhd->bqhgd', p, v)

    return sweep_query_blocks(block, q)


def diff_attend(q, k, v, lam):
    scale = q.shape[-1] ** -0.5

    def block(qb):
        s = jnp.einsum('bqhmd,bkhmd->bhmqk', qb, k).astype(jnp.float32) * scale
        p = jax.nn.softmax(s, axis=-1)
        a = (p[:, :, 0] - lam * p[:, :, 1]).astype(v.dtype)
        return jnp.einsum('bhqk,bkhe->bqhe', a, v)

    return sweep_query_blocks(block, q)


def ssd_scan(x, dt, A, Bm, Cm, init):
    f32 = jnp.float32
    b, L, h, p = x.shape
    g, n = Bm.shape[2], Bm.shape[3]
    nc = L // SSD_CHUNK
    rep = h // g
    dtf = dt.astype(f32)
    Bh = jnp.repeat(Bm.astype(f32), rep, axis=2).reshape(b, nc, SSD_CHUNK, h, n)
    Ch = jnp.repeat(Cm.astype(f32), rep, axis=2).reshape(b, nc, SSD_CHUNK, h, n)
    xdt = (x.astype(f32) * dtf[..., None]).reshape(b, nc, SSD_CHUNK, h, p)
    cs = jnp.cumsum((dtf * A).reshape(b, nc, SSD_CHUNK, h), axis=2)
    mask = jnp.tril(jnp.ones((SSD_CHUNK, SSD_CHUNK), dtype=bool))[None, None, :, :, None]
    decay_in = jnp.exp(jnp.where(mask, cs[:, :, :, None, :] - cs[:, :, None, :, :], -jnp.inf))
    scores = jnp.einsum('bclhn,bcshn->bclsh', Ch, Bh) * decay_in
    y_diag = jnp.einsum('bclsh,bcshp->bclhp', scores, xdt)
    decay_to_end = jnp.exp(cs[:, :, -1:, :] - cs)
    chunk_states = jnp.einsum('bclhn,bclh,bclhp->bchpn', Bh, decay_to_end, xdt)
    chunk_decay = jnp.exp(cs[:, :, -1, :])

    def step(state, inp):
        dec, new = inp
        return state * dec[:, :, None, None] + new, state

    final, entering = lax.scan(step, init.astype(f32),
                               (jnp.moveaxis(chunk_decay, 1, 0), jnp.moveaxis(chunk_states, 1, 0)))
    entering = jnp.moveaxis(entering, 0, 1)
    y_off = jnp.einsum('bclhn,bchpn,bclh->bclhp', Ch, entering, jnp.exp(cs))
    return (y_diag + y_off).reshape(b, L, h, p), final


def mixer(h, lw, lam_init, rope=None, ctx=None):
    nb, L, _ = h.shape
    offsets = [int(v) for v in np.cumsum(PROJ_SIZES)[:-1]]
    aq, ak, av, bq, bk, bv, cx, cz, cB, cC, cdt = jnp.split(h @ lw['w_in'], offsets, axis=-1)
    aq = rmsnorm(aq.reshape(nb, L, A_Q_HEADS, HEAD_DIM), lw['a_q_norm'])
    ak = rmsnorm(ak.reshape(nb, L, A_KV_HEADS, HEAD_DIM), lw['a_k_norm'])
    av = av.reshape(nb, L, A_KV_HEADS, HEAD_DIM)
    bq = bq.reshape(nb, L, B_HEADS, 2, B_HALF)
    bk = bk.reshape(nb, L, B_HEADS, 2, B_HALF)
    bv = bv.reshape(nb, L, B_HEADS, 2 * B_HALF)
    lv = lw['b_lambda'].astype(jnp.float32)
    lam = jnp.exp(jnp.sum(lv[0] * lv[1])) - jnp.exp(jnp.sum(lv[2] * lv[3])) + lam_init
    if rope is not None:
        cos_a, sin_a, cos_b, sin_b = rope
        aq = apply_rope(aq, cos_a[:, None, :], sin_a[:, None, :])
        ak = apply_rope(ak, cos_a[:, None, :], sin_a[:, None, :])
        bq = apply_rope(bq, cos_b[:, None, None, :], sin_b[:, None, None, :])
        bk = apply_rope(bk, cos_b[:, None, None, :], sin_b[:, None, None, :])
    if ctx is None:
        keys_a, vals_a, keys_b, vals_b = ak, av, bk, bv
        init = jnp.zeros((nb, 2, C_HEADS, C_HEADDIM, C_STATE), jnp.float32)
    else:
        cak, cav, cbk, cbv, init = ctx
        keys_a = jnp.concatenate([ak, cak.astype(ak.dtype)], axis=1)
        vals_a = jnp.concatenate([av, cav.astype(av.dtype)], axis=1)
        keys_b = jnp.concatenate([bk, cbk.astype(bk.dtype)], axis=1)
        vals_b = jnp.concatenate([bv, cbv.astype(bv.dtype)], axis=1)
    a_out = gqa_attend(aq.reshape(nb, L, A_KV_HEADS, A_GROUP, HEAD_DIM), keys_a, vals_a)
    a_out = a_out.reshape(nb, L, A_WIDTH)
    b_o = diff_attend(bq, keys_b, vals_b, lam)
    b_out = (rmsnorm(b_o, lw['b_subln']) * (1.0 - lam_init)).reshape(nb, L, B_WIDTH)
    xbc = jax.nn.silu(dwconv(jnp.concatenate([cx, cB, cC], axis=-1), lw['ssm_conv_w'], lw['ssm_conv_b']))
    cx, cB, cC = jnp.split(xbc, [C_INNER, C_INNER + C_GROUPS * C_STATE], axis=-1)
    xs = cx.reshape(nb, L, C_HEADS, C_HEADDIM)
    Bm = cB.reshape(nb, L, C_GROUPS, C_STATE)
    Cm = cC.reshape(nb, L, C_GROUPS, C_STATE)
    dt = jax.nn.softplus(cdt.reshape(nb, L, 2, C_HEADS).astype(jnp.float32)
                         + lw['ssm_dt_bias'].astype(jnp.float32))
    A = -jnp.exp(lw['ssm_A_log'].astype(jnp.float32))
    y_f, s_f = ssd_scan(xs, dt[:, :, 0], A[0], Bm, Cm, init[:, 0])
    y_b, s_b = ssd_scan(jnp.flip(xs, 1), jnp.flip(dt[:, :, 1], 1), A[1],
                        jnp.flip(Bm, 1), jnp.flip(Cm, 1), init[:, 1])
    y = y_f + jnp.flip(y_b, 1) + xs.astype(jnp.float32) * lw['ssm_D'].astype(jnp.float32)[:, None]
    y = y.reshape(nb, L, C_INNER).astype(h.dtype)
    c_out = rmsnorm(y * jax.nn.silu(cz), lw['ssm_norm_w'])
    out = jnp.concatenate([a_out, b_out, c_out], axis=-1) @ lw['w_out']
    if ctx is None:
        return out, (ak, av, bk, bv, jnp.stack([s_f, s_b], axis=1))
    return out


def conv_ffn(h, lw):
    u = dwconv(h @ lw['ffn_up'], lw['ffn_conv_w'], lw['ffn_conv_b'])
    g, v = jnp.split(u, 2, axis=-1)
    return (jax.nn.silu(g) * v) @ lw['ffn_down']


def setup_inputs(seed: int = 0) -> dict:
    key = jax.random.key(seed)
    ks = jax.random.split(key, 32)

    def nrm(k, shape, s):
        return jax.random.normal(k, shape, jnp.float32) * s

    x_prompt = nrm(ks[0], (BATCH, SEQ, D_MODEL), 1.0)
    x_sample = nrm(ks[1], (DEC_BATCH, DEC_SEQ, D_MODEL), 1.0)
    cache_a_k = nrm(ks[2], (DEC_BATCH, DEPTH, PAST_LEN, A_KV_HEADS, HEAD_DIM), 1.0)
    cache_a_v = nrm(ks[3], (DEC_BATCH, DEPTH, PAST_LEN, A_KV_HEADS, HEAD_DIM), 1.0)
    cache_b_k = nrm(ks[4], (DEC_BATCH, DEPTH, PAST_LEN, B_HEADS, 2, B_HALF), 1.0)
    cache_b_v = nrm(ks[5], (DEC_BATCH, DEPTH, PAST_LEN, B_HEADS, 2 * B_HALF), 1.0)
    state_ssm = nrm(ks[6], (DEC_BATCH, DEPTH, 2, C_HEADS, C_HEADDIM, C_STATE), 0.1)
    c = nrm(ks[7], (DEC_BATCH, D_MODEL), 1.0)
    c_ctx = nrm(ks[8], (D_MODEL,), 1.0)
    ada_w = nrm(ks[9], (DEPTH, D_MODEL, 6 * D_MODEL), 0.5 * D_MODEL ** -0.5)
    ada_b = nrm(ks[10], (DEPTH, 6 * D_MODEL), 0.01)
    w_in = nrm(ks[11], (DEPTH, D_MODEL, PROJ_WIDTH), D_MODEL ** -0.5)
    a_q_norm = 1.0 + nrm(ks[12], (DEPTH, HEAD_DIM), 0.1)
    a_k_norm = 1.0 + nrm(ks[13], (DEPTH, HEAD_DIM), 0.1)
    b_lambda = nrm(ks[14], (DEPTH, 4, B_HALF), 0.1)
    b_subln = 1.0 + nrm(ks[15], (DEPTH, 2 * B_HALF), 0.1)
    ssm_conv_w = nrm(ks[16], (DEPTH, C_CONV, C_CONV_CH), C_CONV ** -0.5)
    ssm_conv_b = nrm(ks[17], (DEPTH, C_CONV_CH), 0.01)
    ssm_A_log = jnp.log(jax.random.uniform(ks[18], (DEPTH, 2, C_HEADS), jnp.float32, 1.0, 16.0))
    dt0 = jnp.exp(jax.random.uniform(ks[19], (DEPTH, 2, C_HEADS), jnp.float32,
                                     math.log(1e-3), math.log(1e-1)))
    ssm_dt_bias = dt0 + jnp.log(-jnp.expm1(-dt0))
    ssm_D = 1.0 + nrm(ks[20], (DEPTH, C_HEADS), 0.1)
    ssm_norm_w = 1.0 + nrm(ks[21], (DEPTH, C_INNER), 0.1)
    w_out = nrm(ks[22], (DEPTH, MIX_WIDTH, D_MODEL), MIX_WIDTH ** -0.5)
    ffn_up = nrm(ks[23], (DEPTH, D_MODEL, 2 * D_FF), D_MODEL ** -0.5)
    ffn_conv_w = nrm(ks[24], (DEPTH, FFN_CONV, 2 * D_FF), FFN_CONV ** -0.5)
    ffn_conv_b = nrm(ks[25], (DEPTH, 2 * D_FF), 0.01)
    ffn_down = nrm(ks[26], (DEPTH, D_FF, D_MODEL), D_FF ** -0.5)
    final_norm_w = 1.0 + nrm(ks[27], (D_MODEL,), 0.1)
    return {"x_prompt": x_prompt, "x_sample": x_sample,
            "cache_a_k": cache_a_k, "cache_a_v": cache_a_v,
            "cache_b_k": cache_b_k, "cache_b_v": cache_b_v, "state_ssm": state_ssm,
            "c": c, "c_ctx": c_ctx, "ada_w": ada_w, "ada_b": ada_b, "w_in": w_in,
            "a_q_norm": a_q_norm, "a_k_norm": a_k_norm, "b_lambda": b_lambda, "b_subln": b_subln,
            "ssm_conv_w": ssm_conv_w, "ssm_conv_b": ssm_conv_b, "ssm_A_log": ssm_A_log,
            "ssm_dt_bias": ssm_dt_bias, "ssm_D": ssm_D, "ssm_norm_w": ssm_norm_w,
            "w_out": w_out, "ffn_up": ffn_up, "ffn_conv_w": ffn_conv_w, "ffn_conv_b": ffn_conv_b,
            "ffn_down": ffn_down, "final_norm_w": final_norm_w}


def reference(x_prompt, x_sample, cache_a_k, cache_a_v, cache_b_k, cache_b_v, state_ssm, c,
              c_ctx, ada_w, ada_b, w_in, a_q_norm, a_k_norm, b_lambda, b_subln,
              ssm_conv_w, ssm_conv_b, ssm_A_log, ssm_dt_bias, ssm_D, ssm_norm_w,
              w_out, ffn_up, ffn_conv_w, ffn_conv_b, ffn_down, final_norm_w):
    L_lat = x_sample.shape[1]
    cos_a, sin_a = axial_rope_tables(L_lat, HEAD_DIM)
    cos_b, sin_b = axial_rope_tables(L_lat, B_HALF)
    rope = (cos_a, sin_a, cos_b, sin_b)
    yp, ys = x_prompt, x_sample
    ak_l, av_l, bk_l, bv_l, st_l = [], [], [], [], []
    for l in range(DEPTH):
        lam_init = 0.8 - 0.6 * math.exp(-0.3 * l)
        lw = {'w_in': w_in[l], 'a_q_norm': a_q_norm[l], 'a_k_norm': a_k_norm[l],
              'b_lambda': b_lambda[l], 'b_subln': b_subln[l],
              'ssm_conv_w': ssm_conv_w[l], 'ssm_conv_b': ssm_conv_b[l], 'ssm_A_log': ssm_A_log[l],
              'ssm_dt_bias': ssm_dt_bias[l], 'ssm_D': ssm_D[l], 'ssm_norm_w': ssm_norm_w[l],
              'w_out': w_out[l], 'ffn_up': ffn_up[l], 'ffn_conv_w': ffn_conv_w[l],
              'ffn_conv_b': ffn_conv_b[l], 'ffn_down': ffn_down[l]}
        sh1, sc1, g1, sh2, sc2, g2 = modulation(c_ctx[None, :], ada_w[l], ada_b[l])
        mix, (ka, va, kb, vb, st) = mixer(adanorm(yp, sh1, sc1), lw, lam_init)
        yp = yp + g1 * mix
        yp = yp + g2 * conv_ffn(adanorm(yp, sh2, sc2), lw)
        ak_l.append(ka)
        av_l.append(va)
        bk_l.append(kb)
        bv_l.append(vb)
        st_l.append(st)
        sh1, sc1, g1, sh2, sc2, g2 = modulation(c, ada_w[l], ada_b[l])
        ctx = (cache_a_k[:, l], cache_a_v[:, l], cache_b_k[:, l], cache_b_v[:, l], state_ssm[:, l])
        ys = ys + g1 * mixer(adanorm(ys, sh1, sc1), lw, lam_init, rope, ctx)
        ys = ys + g2 * conv_ffn(adanorm(ys, sh2, sc2), lw)
    y_prompt = rmsnorm(yp, final_norm_w)
    y_sample = rmsnorm(ys, final_norm_w)
    new_a_k = jnp.stack(ak_l, axis=1)
    new_a_v = jnp.stack(av_l, axis=1)
    new_b_k = jnp.stack(bk_l, axis=1)
    new_b_v = jnp.stack(bv_l, axis=1)
    new_ssm = jnp.stack(st_l, axis=1)
    return (y_prompt, y_sample, new_a_k, new_a_v, new_b_k, new_b_v, new_ssm)
```

```python
import math
import numpy as np
import concourse.bass as bass
import concourse.mybir as mybir
from concourse.bass_utils import run_bass_kernel_spmd

F32 = mybir.dt.float32
BF16 = mybir.dt.bfloat16
AF = mybir.ActivationFunctionType
ALU = mybir.AluOpType
AX = mybir.AxisListType
_DSZ = {F32: 4, BF16: 2}


def dsz(dt):
    return _DSZ[dt]


class _Op:
    __slots__ = ("eng", "fn", "deps", "gidx", "idx", "is_dma", "defer", "sig", "tok", "dsem", "dval")

    def __init__(self, eng, fn, deps, gidx, idx, is_dma, defer):
        self.eng = eng
        self.fn = fn
        self.deps = deps
        self.gidx = gidx
        self.idx = idx
        self.is_dma = is_dma
        self.defer = defer
        self.sig = False
        self.tok = None
        self.dsem = None
        self.dval = None


class Prog:
    ENGS = ("pe", "act", "dve", "pool", "sp")
    NDS = 8

    def __init__(self, nc):
        self.nc = nc
        self.ops = {e: [] for e in self.ENGS}
        self.all_ops = []
        self.last_w = {}
        self.readers = {}
        self.chunk = {}
        self.rowbytes = {}
        self._cm = []
        self.psum_names = set()

    def sbuf(self, name, shape, dtype, chunk=None):
        g = self.nc.sbuf_tensor(name, list(shape), dtype)
        t = g.__enter__()
        self._cm.append(g)
        rb = int(np.prod(shape[1:])) * dsz(dtype)
        self.rowbytes[name] = rb
        self.chunk[name] = chunk if chunk else rb
        return t

    def psum(self, name, shape, dtype=F32):
        g = self.nc.psum_tensor(name, list(shape), dtype)
        t = g.__enter__()
        self._cm.append(g)
        rb = int(np.prod(shape[1:])) * dsz(dtype)
        self.rowbytes[name] = rb
        self.chunk[name] = 2048
        self.psum_names.add(name)
        return t

    def res(self, ap):
        sp = str(ap.space)
        if "DRAM" in sp.upper():
            return []
        name = ap.tensor.name
        rb = self.rowbytes[name]
        es = dsz(ap.dtype)
        ch = self.chunk[name]
        base = (int(ap.offset) * es) % rb
        dims = [(abs(st), cnt) for (st, cnt) in ap.ap[1:] if cnt > 1]
        if not dims:
            return [(name, base // ch)]
        dims.sort()
        inner_st, inner_cnt = dims[0]
        outer = dims[1:]
        nout = 1
        for (_, c) in outer:
            nout *= c
        keys = set()
        if nout > 256:
            span = sum((c - 1) * s for (s, c) in dims)
            lo, hi = base, base + (span + 1) * es
            return [(name, c) for c in range(lo // ch, (hi - 1) // ch + 1)]
        offs = [0]
        for (s, c) in outer:
            offs = [o + i * s for o in offs for i in range(c)]
        ispan = ((inner_cnt - 1) * inner_st + 1) * es
        for o in offs:
            lo = base + o * es
            hi = lo + ispan
            for c in range(lo // ch, (hi - 1) // ch + 1):
                keys.add((name, c))
        return list(keys)

    limit = None

    def op(self, eng, fn, reads=(), writes=(), is_dma=False, defer=False):
        if self.limit is not None and len(self.all_ops) >= self.limit:
            return None
        rk = set()
        for a in reads:
            rk.update(self.res(a))
        wk = set()
        for a in writes:
            wk.update(self.res(a))
        deps = set()
        for r in rk:
            w = self.last_w.get(r)
            if w is not None:
                deps.add(w)
            if r[0] in self.psum_names:
                for t in self.readers.get(r, ()):
                    if t.eng != eng:
                        deps.add(t)
        for r in wk:
            w = self.last_w.get(r)
            if w is not None:
                deps.add(w)
            for t in self.readers.get(r, ()):
                deps.add(t)
        o = _Op(eng, fn, deps, len(self.all_ops), len(self.ops[eng]), is_dma, defer)
        self.ops[eng].append(o)
        self.all_ops.append(o)
        for r in rk:
            if r not in wk:
                self.readers.setdefault(r, []).append(o)
        for r in wk:
            self.last_w[r] = o
            self.readers[r] = []
        return o

    def mm(self, out, lhsT, rhs, start=True, stop=True, defer=None, **kw):
        if defer is None:
            defer = not stop
        return self.op("pe", lambda e: e.matmul(out, lhsT, rhs, start=start, stop=stop, **kw),
                       reads=[lhsT, rhs], writes=[out], defer=defer)

    def tr(self, out, in_, ident, defer=False):
        return self.op("pe", lambda e: e.transpose(out, in_, ident), reads=[in_, ident], writes=[out], defer=defer)

    def act(self, out, in_, func, bias=None, scale=None):
        kw = {}
        reads = [in_]
        if bias is not None:
            kw["bias"] = bias
            if not isinstance(bias, (int, float)):
                reads.append(bias)
        if scale is not None:
            kw["scale"] = scale
            if not isinstance(scale, (int, float)):
                reads.append(scale)
        return self.op("act", lambda e: e.activation(out, in_, func, **kw), reads=reads, writes=[out])

    def tt(self, out, in0, in1, op, eng="dve"):
        return self.op(eng, lambda e: e.tensor_tensor(out, in0, in1, op), reads=[in0, in1], writes=[out])

    def ts(self, out, in0, s1, s2, op0, op1=None, eng="dve"):
        reads = [in0]
        for s in (s1, s2):
            if s is not None and not isinstance(s, (int, float)):
                reads.append(s)
        if op1 is None:
            return self.op(eng, lambda e: e.tensor_scalar(out, in0, s1, s2, op0), reads=reads, writes=[out])
        return self.op(eng, lambda e: e.tensor_scalar(out, in0, s1, s2, op0, op1), reads=reads, writes=[out])

    def stt(self, out, in0, scalar, in1, op0, op1, eng="dve"):
        reads = [in0, in1]
        if not isinstance(scalar, (int, float)):
            reads.append(scalar)
        return self.op(eng, lambda e: e.scalar_tensor_tensor(out, in0, scalar, in1, op0, op1),
                       reads=reads, writes=[out])

    def copy(self, out, in_, eng="dve"):
        if eng == "act":
            return self.op("act", lambda e: e.copy(out, in_), reads=[in_], writes=[out])
        return self.op(eng, lambda e: e.tensor_copy(out, in_), reads=[in_], writes=[out])

    def memset(self, ap, val, eng="pool"):
        return self.op(eng, lambda e: e.memset(ap, val), writes=[ap])

    def dma(self, out, in_, q="sp"):
        return self.op(q, lambda e: e.dma_start(out=out, in_=in_), reads=[in_], writes=[out], is_dma=True)

    def finalize(self):
        nc = self.nc
        nxt = {}
        for e in self.ENGS:
            lst = self.ops[e]
            cur = None
            for k in range(len(lst) - 1, -1, -1):
                o = lst[k]
                if not o.is_dma and not o.defer:
                    cur = o
                nxt[o] = cur
        resolve = {}
        for o in self.all_ops:
            for d in o.deps:
                if d.is_dma or (o.eng == "pe" and d.eng == "pe"):
                    continue
                t = d
                if d.defer:
                    t2 = nxt[d]
                    if t2 is not None and t2.gidx < o.gidx:
                        t = t2
                resolve[(d, o)] = t
                t.sig = True
        sems = {e: nc.alloc_semaphore(name="c_" + e) for e in self.ENGS}
        dsems = {}
        for e in self.ENGS:
            if any(o.is_dma for o in self.ops[e]):
                dsems[e] = [nc.alloc_semaphore(name="d_%s_%d" % (e, i)) for i in range(self.NDS)]
        for e in self.ENGS:
            c = 0
            j = 0
            for o in self.ops[e]:
                if o.is_dma:
                    o.dsem = dsems[e][j % self.NDS]
                    o.dval = 16 * (j // self.NDS + 1)
                    j += 1
                elif o.sig:
                    c += 1
                    o.tok = c
        self.sig_counts = {e: sum(1 for o in self.ops[e] if o.sig) for e in self.ENGS}
        progs = self

        def emit(e, eng):
            waited = {}
            lst = progs.ops[e]
            for o in lst:
                need = {}
                for d in o.deps:
                    if e == "pe" and d.eng == "pe":
                        continue
                    if d.is_dma:
                        key, val = d.dsem, d.dval
                    else:
                        t = resolve[(d, o)]
                        key, val = sems[t.eng], t.tok
                    if need.get(key, 0) < val:
                        need[key] = val
                if o.is_dma and o.dval > 16:
                    key, val = o.dsem, o.dval - 16
                    if need.get(key, 0) < val:
                        need[key] = val
                for key, val in need.items():
                    if waited.get(key, 0) < val:
                        eng.wait_ge(key, val)
                        waited[key] = val
                ins = o.fn(eng)
                if o.is_dma:
                    ins.then_inc(o.dsem, 16)
                elif o.sig:
                    ins.then_inc(sems[e], 1)
            if e in dsems:
                last = {}
                for o in lst:
                    if o.is_dma:
                        last[o.dsem] = o.dval
                for key, val in last.items():
                    if waited.get(key, 0) < val:
                        eng.wait_ge(key, val)

        with nc.Block() as block:
            if self.ops["pe"]:
                @block.tensor
                def _(eng):
                    emit("pe", eng)
            if self.ops["act"]:
                @block.scalar
                def _(eng):
                    emit("act", eng)
            if self.ops["dve"]:
                @block.vector
                def _(eng):
                    emit("dve", eng)
            if self.ops["pool"]:
                @block.gpsimd
                def _(eng):
                    emit("pool", eng)
            if self.ops["sp"]:
                @block.sync
                def _(eng):
                    emit("sp", eng)
        return nc


T = 2048
D = 1024
KD = 8
NL = 2
NKEY = 2304
NKT = 18
EPS = 1e-6
D_FF = 2816
NFF = 22
C_AQ, C_AK, C_BQ, C_BK, C_CX, C_CB, C_CC = 0, 384, 512, 768, 1024, 1408, 1664
C_AV, C_BV, C_CZ, C_DT = 1920, 2048, 2304, 2688
MASKV = 2048.0
NEG = -30000.0


class Arena:
    def __init__(self, P, name, nbytes):
        self.P = P
        self.t = P.sbuf(name, [128, nbytes // 4], F32, chunk=256)
        self.n = nbytes
        self.ptr = 0
        self.peak = 0

    def mark(self):
        return self.ptr

    def reset(self, m=0):
        self.ptr = m

    def alloc(self, shape, dtype):
        nb = int(np.prod(shape[1:])) * dsz(dtype)
        nb = (nb + 255) // 256 * 256
        off = self.ptr
        self.ptr += nb
        self.peak = max(self.peak, self.ptr)
        assert self.ptr <= self.n, "arena overflow %d > %d" % (self.ptr, self.n)
        n_el = int(np.prod(shape[1:]))
        v = self.t[:, off // 4:(off + nb) // 4]
        if dtype != F32:
            v = v.bitcast(dtype)
        v = v[:, 0:n_el]
        if len(shape) == 3:
            v = v.rearrange("p (a b) -> p a b", b=shape[2])
        elif len(shape) == 4:
            v = v.rearrange("p (a b c) -> p a b c", b=shape[2], c=shape[3])
        return v


def colgroups(n):
    out = []
    j = 0
    while j < n:
        w = min(512, n - j)
        out.append((j, w))
        j += w
    return out


def build(stop_after=None, dbg=None, limit=None):
    dbg = dbg or {}
    nc = bass.Bass("TRN2", target_bir_lowering=False)
    P = Prog(nc)
    P.limit = limit

    def din(name, shape):
        return nc.dram_tensor(name, list(shape), F32, kind="ExternalInput").ap()

    def dout(name, shape):
        return nc.dram_tensor(name, list(shape), F32, kind="ExternalOutput").ap()

    x_d = din("x", [T, D])
    cvec_d = din("cvec", [128, 8])
    cak_d = din("cak", [NL, 256, 128])
    cav_d = din("cav", [NL, 256, 128])
    cbk_d = din("cbk", [NL, 256, 256])
    cbv_d = din("cbv", [NL, 256, 256])
    ssm0_d = din("ssm0", [NL, 2, 384, 128])
    mq_d = din("mq", [9, T])
    mk_d = din("mk", [9, NKEY])
    rope_d = [din(n, [128, T]) for n in ("cosA", "sinA", "cosB", "sinB")]
    pflag_d = din("pflag", [128, 1])
    seqf_d = din("seqf", [128, 16])
    identf_d = din("identf", [128, 128])
    trif_d = din("trif", [128, 128])
    trib_d = din("trib", [128, 128])
    mnegf_d = din("mnegf", [128, 128])
    mnegb_d = din("mnegb", [128, 128])
    permA_d = din("permA", [128, 128])
    permB_d = din("permB", [128, 128])
    bones_d = din("bones", [128, 128])
    ada_w_d = din("ada_w", [NL, D, 6 * D])
    ada_b_d = din("ada_b", [NL, 128, 48])
    w_in_d = din("w_in", [NL, D, 2700])
    aqn_d = din("aqn", [NL, 128, 1])
    akn_d = din("akn", [NL, 128, 1])
    blam_d = din("blam", [NL, 128])
    bsub_d = din("bsub", [NL, 128, 1])
    scw_d = din("scw", [NL, 128, 7, 3])
    scb_d = din("scb", [NL, 128, 7])
    alog_d = din("alog", [NL, 12])
    dtb_d = din("dtb", [NL, 12])
    ssmD_d = din("ssmD", [NL, 6])
    snw_d = din("snw", [NL, 384])
    w_out_d = din("w_out", [NL, D, D])
    ffn_up_d = din("ffn_up", [NL, D, 2 * D_FF])
    fcw_d = din("fcw", [NL, 128, 44, 3])
    fcb_d = din("fcb", [NL, 128, 44])
    ffn_down_d = din("ffn_down", [NL, D_FF, D])
    fnw_d = din("fnw", [128, 8])

    y_o = dout("y", [T, D])
    nak_o = dout("nak", [NL, T, 128])
    nav_o = dout("nav", [NL, T, 128])
    nbk_o = dout("nbk", [NL, T, 256])
    nbv_o = dout("nbv", [NL, T, 256])
    nssm_o = dout("nssm", [NL, 8, 2, 384, 128])
    dbg_o = {}
    for k, shp in dbg.items():
        dbg_o[k] = dout("dbg_" + k, shp)

    xT = P.sbuf("xT", [128, KD, T], F32, chunk=2048)
    cst = P.sbuf("cst", [128, 8, 128], F32, chunk=512)
    cstb = P.sbuf("cstb", [128, 6, 128], BF16, chunk=256)
    small = P.sbuf("small", [128, 512], F32, chunk=64)
    wsl = [P.sbuf("wsl%d" % i, [128, KD, 256], BF16, chunk=512) for i in range(4)]
    ARENA_BYTES = 116 * 1024
    A = Arena(P, "arena", ARENA_BYTES)
    ps = P.psum("ps", [128, 4096], F32)
    psb = ps[:, :].bitcast(BF16)

    identf = cst[:, 0, :]
    trif = cst[:, 1, :]
    trib = cst[:, 2, :]
    permA = cst[:, 3, :]
    permB = cst[:, 4, :]
    onesf = cst[:, 5, :]
    identb = cstb[:, 0, :]
    mnegf = cstb[:, 1, :]
    mnegb = cstb[:, 2, :]
    bones = cstb[:, 3, :]
    onesb = cstb[:, 4, :]
    tri = [trif, trib]
    mneg = [mnegf, mnegb]

    _sp = [0]

    def salloc(n):
        o = _sp[0]
        _sp[0] += n
        assert _sp[0] <= 512
        return small[:, o:o + n]

    mod = [salloc(48) for _ in range(NL)]
    opsc = [salloc(16) for _ in range(NL)]
    cv = salloc(8)
    pflag = salloc(1)
    seqf = salloc(16)
    fnw = salloc(8)

    bank_rr = [0]

    reserved = set()

    def banks(n):
        for _ in range(16):
            b = bank_rr[0]
            if b + n > 8:
                b = 0
            bank_rr[0] = (b + n) % 8
            if not any((b + i) in reserved for i in range(n)):
                return b
        raise RuntimeError("no free psum banks")

    def pbank(b, n=1):
        return ps[:, b * 512:(b + n) * 512]

    wsl_rr = [0]

    def next_wsl():
        w = wsl[wsl_rr[0] % 4]
        wsl_rr[0] += 1
        return w

    for i, dsrc in enumerate((identf_d, trif_d, trib_d, permA_d, permB_d)):
        P.dma(cst[:, i, :], dsrc[:, :])
    P.memset(cst[:, 5, :], 1.0)
    P.dma(cstb[:, 0, :], identf_d[:, :], q="pool")
    P.dma(cstb[:, 1, :], mnegf_d[:, :], q="pool")
    P.dma(cstb[:, 2, :], mnegb_d[:, :], q="pool")
    P.dma(cstb[:, 3, :], bones_d[:, :], q="pool")
    P.memset(cstb[:, 4, :], 1.0)
    P.dma(cv, cvec_d[:, :])
    P.dma(pflag, pflag_d[:, :])
    P.dma(seqf, seqf_d[:, :])
    P.dma(fnw, fnw_d[:, :])

    m0 = A.mark()
    scv = A.alloc([128, 8], BF16)
    adab = A.alloc([128, 48], F32)
    P.act(scv, cv, AF.Silu)
    aslab = [A.alloc([128, KD, 512], BF16) for _ in range(2)]
    for l in range(NL):
        pb = banks(1)
        pm = pbank(pb)
        awv = ada_w_d[l].rearrange("(k p) n -> p k n", p=128)
        for s in range(12):
            sl = aslab[s % 2]
            P.dma(sl[:, :, :], awv[:, :, s * 512:(s + 1) * 512], q="pool")
            for jj in range(4):
                j = s * 4 + jj
                for k in range(KD):
                    P.mm(pm[:, j:j + 1], sl[:, k, jj * 128:(jj + 1) * 128], scv[:, k:k + 1],
                         start=(k == 0), stop=(k == KD - 1))
        P.dma(adab, ada_b_d[l])
        P.tt(mod[l], pm[:, 0:48], adab, ALU.add)
        P.ts(opsc[l][:, 0:8], mod[l][:, 8:16], 1.0, None, ALU.add)
        P.ts(opsc[l][:, 8:16], mod[l][:, 32:40], 1.0, None, ALU.add)
    A.reset(m0)

    if "mod" in dbg_o:
        P.dma(dbg_o["mod"][:, 0:48], mod[0])
        P.dma(dbg_o["mod"][:, 48:96], mod[1])
    if stop_after == "S0":
        P.finalize()
        return nc, P
    m0 = A.mark()
    xin = [A.alloc([128, D], F32) for _ in range(2)]
    for tt_ in range(16):
        xi = xin[tt_ % 2]
        P.dma(xi, x_d[tt_ * 128:(tt_ + 1) * 128, :])
        for hb in range(2):
            pb = banks(1)
            for k4 in range(4):
                k = hb * 4 + k4
                P.tr(pbank(pb)[:, k4 * 128:(k4 + 1) * 128], xi[:, k * 128:(k + 1) * 128], identf)
            P.copy(xT[:, hb * 4:(hb + 1) * 4, tt_ * 128:(tt_ + 1) * 128],
                   pbank(pb).rearrange("p (a b) -> p a b", b=128), eng=("act" if hb else "dve"))
    A.reset(m0)

    if stop_after == "S1":
        if "xT" in dbg_o:
            P.dma(dbg_o["xT"], xT[:, :, :])
        P.finalize()
        return nc, P
    def adanorm(l, which, t0, n, hT, col0=0):
        m = A.mark()
        sq = [A.alloc([128, n], BF16) for _ in range(2)]
        lnv = A.alloc([128, n], F32)
        tmp = [A.alloc([128, n], F32) for _ in range(2)]
        nbk = (n + 511) // 512
        pb = banks(nbk)
        pss = pbank(pb, nbk)
        for k in range(KD):
            if k % 2 == 0:
                P.act(sq[k % 2], xT[:, k, t0:t0 + n], AF.Square)
            else:
                P.tt(sq[k % 2], xT[:, k, t0:t0 + n], xT[:, k, t0:t0 + n], ALU.mult)
            for (j0, w) in colgroups(n):
                P.mm(pss[:, j0:j0 + w], onesb, sq[k % 2][:, j0:j0 + w], start=(k == 0), stop=(k == KD - 1))
        P.act(lnv, pss[:, 0:n], AF.Ln, bias=EPS, scale=1.0 / D)
        P.act(lnv, lnv, AF.Exp, scale=-0.5)
        shb = 0 if which == 0 else 24
        for k in range(KD):
            P.tt(tmp[k % 2], xT[:, k, t0:t0 + n], lnv, ALU.mult)
            P.act(hT[:, k, col0:col0 + n], tmp[k % 2], AF.Identity,
                  bias=mod[l][:, shb + k:shb + k + 1], scale=opsc[l][:, which * 8 + k:which * 8 + k + 1])
        A.reset(m)

    def load_wslab(l, c0, ncols):
        wb = next_wsl()
        wv = w_in_d[l].rearrange("(k p) n -> p k n", p=128)
        P.dma(wb[:, :, 0:ncols], wv[:, :, c0:c0 + ncols], q="pool")
        return wb

    def proj_feat(l, c0, ntiles, hT, ntok, consume, hcol0=0):
        i = 0
        while i < ntiles:
            nt = min(2, ntiles - i)
            wb = load_wslab(l, c0 + i * 128, nt * 128)
            for j in range(nt):
                nbk = (ntok + 511) // 512
                pb = banks(nbk)
                pt = pbank(pb, nbk)
                for (j0, w) in colgroups(ntok):
                    for k in range(KD):
                        P.mm(pt[:, j0:j0 + w], wb[:, k, j * 128:(j + 1) * 128], hT[:, k, hcol0 + j0:hcol0 + j0 + w],
                             start=(k == 0), stop=(k == KD - 1))
                consume(i + j, pt)
            i += nt

    def dma_dbg(name, src):
        if name in dbg_o:
            P.dma(dbg_o[name], src)

    def layer_params(l):
        lp = {}
        lp["aqn"] = A.alloc([128, 1], F32)
        lp["akn"] = A.alloc([128, 1], F32)
        lp["bsub"] = A.alloc([128, 1], F32)
        lp["scw"] = A.alloc([128, 7, 3], F32)
        lp["scb"] = A.alloc([128, 7], F32)
        lp["nscw0"] = A.alloc([128, 7], F32)
        lp["nscw2"] = A.alloc([128, 7], F32)
        lp["fcw"] = A.alloc([128, 44, 3], F32)
        lp["fcb"] = A.alloc([128, 44], F32)
        lp["nfcw0"] = A.alloc([128, 44], F32)
        lp["nfcw2"] = A.alloc([128, 44], F32)
        lp["arow"] = A.alloc([128, 12], F32)
        lp["dtb"] = A.alloc([128, 12], F32)
        lp["drow"] = A.alloc([128, 6], F32)
        lp["snw"] = A.alloc([128, 384], F32)
        lp["nlam"] = A.alloc([128, 1], F32)
        blam = A.alloc([128, 128], F32)
        lt = A.alloc([128, 4], F32)
        P.dma(lp["aqn"], aqn_d[l])
        P.dma(lp["akn"], akn_d[l])
        P.dma(lp["bsub"], bsub_d[l])
        P.dma(lp["scw"], scw_d[l])
        P.dma(lp["scb"], scb_d[l])
        P.dma(lp["fcw"], fcw_d[l])
        P.dma(lp["fcb"], fcb_d[l])
        P.dma(lp["arow"], alog_d[l:l + 1, :].partition_broadcast(128).rearrange("p a b -> p (a b)"))
        P.dma(lp["dtb"], dtb_d[l:l + 1, :].partition_broadcast(128).rearrange("p a b -> p (a b)"))
        P.dma(lp["drow"], ssmD_d[l:l + 1, :].partition_broadcast(128).rearrange("p a b -> p (a b)"))
        P.dma(lp["snw"], snw_d[l:l + 1, :].partition_broadcast(128).rearrange("p a b -> p (a b)"))
        P.dma(blam, blam_d[l:l + 1, :].partition_broadcast(128).rearrange("p a b -> p (a b)"))
        lam_init = 0.8 - 0.6 * math.exp(-0.3 * l)
        P.act(lp["arow"], lp["arow"], AF.Exp)
        P.ts(lp["arow"], lp["arow"], -1.0, None, ALU.mult)
        bl = blam.rearrange("p (a b) -> p a b", b=32)
        pr = A.alloc([128, 2, 32], F32)
        P.tt(pr[:, 0, :], bl[:, 0, :], bl[:, 1, :], ALU.mult)
        P.tt(pr[:, 1, :], bl[:, 2, :], bl[:, 3, :], ALU.mult)
        P.op("dve", lambda e: e.reduce_sum(lt[:, 0:2], pr, axis=AX.X), reads=[pr], writes=[lt[:, 0:2]])
        P.act(lt[:, 0:2], lt[:, 0:2], AF.Exp)
        P.tt(lt[:, 2:3], lt[:, 1:2], lt[:, 0:1], ALU.subtract)
        P.ts(lp["nlam"], lt[:, 2:3], -lam_init, None, ALU.add)
        P.ts(lp["bsub"], lp["bsub"], 1.0 - lam_init, None, ALU.mult)
        for (dst, src, tap, n) in ((lp["nscw0"], lp["scw"], 0, 7), (lp["nscw2"], lp["scw"], 2, 7),
                                   (lp["nfcw0"], lp["fcw"], 0, 44), (lp["nfcw2"], lp["fcw"], 2, 44)):
            P.ts(dst, src[:, :, tap], pflag[:, 0:1], -1.0, ALU.mult, ALU.mult)
        return lp

    def conv_evac(U, a, n, w3, bcol, pt):
        P.copy(U, pt[:, 0:n + 2], eng="act")
        P.act(a[:, 0:n], pt[:, 1:n + 1], AF.Identity, bias=bcol, scale=w3[:, 1:2])

    def conv_tile(U, a, n, w3, bcol, nw0, nw2, tstart, pt, evac=True):
        if evac:
            conv_evac(U, a, n, w3, bcol, pt)
        if tstart == 0:
            P.memset(U[:, 0:1], 0.0, eng="pool")
        if tstart + n == T:
            P.memset(U[:, n + 1:n + 2], 0.0, eng="pool")
        P.stt(a[:, 0:n], U[:, 0:n], w3[:, 0:1], a[:, 0:n], ALU.mult, ALU.add)
        P.stt(a[:, 0:n], U[:, 2:n + 2], w3[:, 2:3], a[:, 0:n], ALU.mult, ALU.add)
        starts = [b - tstart for b in range(256, T, 256) if tstart <= b < tstart + n]
        ends = [b - 1 - tstart for b in range(256, T, 256) if tstart <= b - 1 < tstart + n]
        if starts:
            s0, cnt = starts[0], len(starts)
            av = a[:, s0:s0 + (cnt - 1) * 256 + 1:256] if cnt > 1 else a[:, s0:s0 + 1]
            uv = U[:, s0:s0 + (cnt - 1) * 256 + 1:256] if cnt > 1 else U[:, s0:s0 + 1]
            P.stt(av, uv, nw0, av, ALU.mult, ALU.add)
        if ends:
            s0, cnt = ends[0], len(ends)
            av = a[:, s0:s0 + (cnt - 1) * 256 + 1:256] if cnt > 1 else a[:, s0:s0 + 1]
            uv = U[:, s0 + 2:s0 + 2 + (cnt - 1) * 256 + 1:256] if cnt > 1 else U[:, s0 + 2:s0 + 3]
            P.stt(av, uv, nw2, av, ALU.mult, ALU.add)

    for l in range(NL):
        A.reset(0)
        lp = layer_params(l)
        mLP = A.mark()
        yT = A.alloc([128, 16, 3, 128], BF16)
        mL = A.mark()
        if stop_after == "LP":
            if "lp" in dbg_o:
                P.dma(dbg_o["lp"][:, 0:12], lp["arow"])
                P.dma(dbg_o["nlam"], lp["nlam"])
                P.dma(dbg_o["lp"][:, 13:20], lp["nscw0"])
                P.dma(dbg_o["lp"][:, 20:26], lp["drow"])
            break

        xcT = A.alloc([128, 3, T], BF16)
        BT = A.alloc([128, 2, T], BF16)
        CT = A.alloc([128, 2, T], BF16)
        zg = A.alloc([128, 16, 384], BF16)
        dtt = A.alloc([128, 16, 12], F32)
        mS = A.mark()
        hT = A.alloc([128, KD, 514], BF16)
        P.memset(hT, 0.0, eng="pool")
        Ub = [A.alloc([128, 514], F32) for _ in range(2)]
        ab = [A.alloc([128, 512], F32) for _ in range(2)]
        dts = A.alloc([128, 4, 12], F32)
        cnt = [0]
        for c in range(4):
            c0 = c * 512
            tlo = max(c0 - 1, 0)
            thi = min(c0 + 513, T)
            adanorm(l, 0, tlo, thi - tlo, hT, col0=tlo - (c0 - 1))

            def cons_xbc(f, pt, c0=c0):
                U = Ub[cnt[0] % 2]
                a = ab[cnt[0] % 2]
                cnt[0] += 1
                conv_evac(U, a, 512, lp["scw"][:, f, :], lp["scb"][:, f:f + 1], pt)
                for fn_ in pend_silu:
                    fn_()
                del pend_silu[:]
                conv_tile(U, a, 512, lp["scw"][:, f, :], lp["scb"][:, f:f + 1],
                          lp["nscw0"][:, f:f + 1], lp["nscw2"][:, f:f + 1], c0, pt, evac=False)
                if f < 3:
                    dst = xcT[:, f, c0:c0 + 512]
                elif f < 5:
                    dst = BT[:, f - 3, c0:c0 + 512]
                else:
                    dst = CT[:, f - 5, c0:c0 + 512]
                pend_silu.append(lambda dst=dst, a=a: P.act(dst, a, AF.Silu))

            pend_silu = []
            proj_feat(l, C_CX, 7, hT, 514, cons_xbc)
            for fn_ in pend_silu:
                fn_()
            del pend_silu[:]
            wz = [load_wslab(l, C_CZ, 256), load_wslab(l, C_CZ + 256, 140)]
            pdt = pbank(banks(1))
            for t4 in range(4):
                pb = banks(1)
                pt = pbank(pb)
                lh = lambda k, t4=t4: hT[:, k, 1 + t4 * 128:1 + (t4 + 1) * 128]
                for k in range(KD):
                    P.mm(pt[:, 0:256], lh(k), wz[0][:, k, 0:256], start=(k == 0), stop=(k == KD - 1))
                for k in range(KD):
                    P.mm(pt[:, 256:384], lh(k), wz[1][:, k, 0:128], start=(k == 0), stop=(k == KD - 1),
                         skip_group_check=True)
                for k in range(KD):
                    P.mm(pdt[:, t4 * 16:t4 * 16 + 12], lh(k), wz[1][:, k, 128:140], start=(k == 0), stop=(k == KD - 1),
                         skip_group_check=True)
                P.act(zg[:, c * 4 + t4, :], pt[:, 0:384], AF.Silu)
            P.tt(dts, pdt[:, 0:64].rearrange("p (t j) -> p t j", j=16)[:, :, 0:12],
                 lp["dtb"].unsqueeze(1).to_broadcast([128, 4, 12]), ALU.add)
            P.act(dts, dts, AF.Exp)
            P.act(dtt[:, c * 4:(c + 1) * 4, :], dts, AF.Ln, bias=1.0)
        A.reset(mS)
        if "xcT" in dbg_o:
            tmpf = A.alloc([128, 3, T], F32)
            P.copy(tmpf, xcT)
            P.dma(dbg_o["xcT"], tmpf)
            A.reset(mS)
        if stop_after == "M1" and l == 0:
            break

        Sst = [A.alloc([128, 384], F32) for _ in range(2)]
        Sent = A.alloc([128, 384], BF16)
        Sbe = A.alloc([128, 16, 384], BF16)
        st_in = A.alloc([128, 3, 128], F32)
        for d_ in range(2):
            P.dma(st_in, ssm0_d[l, d_].rearrange("(j p) n -> p j n", p=128))
            pb = banks(1)
            for j in range(3):
                P.tr(pbank(pb)[:, j * 128:(j + 1) * 128], st_in[:, j, :], identf)
            P.copy(Sst[d_], pbank(pb)[:, 0:384])
        so = [A.alloc([128, 3, 128], F32) for _ in range(2)]
        xtok_b = [A.alloc([128, 384], BF16) for _ in range(2)]
        btok_b = [A.alloc([128, 256], BF16) for _ in range(2)]
        sm = [[A.alloc([128, 64], F32) for _ in range(2)] for _ in range(2)]
        xdt_b = [[A.alloc([128, 384], BF16) for _ in range(2)] for _ in range(2)]
        xdte_b = [[A.alloc([128, 384], BF16) for _ in range(2)] for _ in range(2)]
        dec_b = [A.alloc([128, 12, 128], F32) for _ in range(2)]
        sc_b = [A.alloc([128, 12, 128], BF16) for _ in range(2)]
        yc = [A.alloc([128, 384], F32) for _ in range(3)]
        ynb = A.alloc([128, 384], BF16)
        rs = A.alloc([128, 4], F32)
        scnt = [0]

        def prep_common(ci, par):
            t0 = ci * 128
            pb = banks(1)
            pv = psb[:, pb * 1024:(pb + 1) * 1024]
            for j in range(3):
                P.tr(pv[:, j * 128:(j + 1) * 128], xcT[:, j, t0:t0 + 128], identb)
            pb2 = banks(1)
            pv2 = psb[:, pb2 * 1024:(pb2 + 1) * 1024]
            for g in range(2):
                P.tr(pv2[:, g * 128:(g + 1) * 128], BT[:, g, t0:t0 + 128], identb)
            P.copy(xtok_b[par], pv[:, 0:384])
            P.copy(btok_b[par], pv2[:, 0:256], eng="act")

        def prep_dir(ci, par, d_, full):
            s_ = sm[par][d_]
            a = s_[:, 0:6]
            cs = s_[:, 6:12]
            ncs = s_[:, 12:18]
            dte = s_[:, 18:24]
            dtot = s_[:, 24:30]
            ecs = s_[:, 30:36]
            w_ = s_[:, 36:42]
            dtv = dtt[:, ci, d_ * 6:(d_ + 1) * 6]
            P.tt(a, dtv, lp["arow"][:, d_ * 6:(d_ + 1) * 6], ALU.mult)
            pb = banks(1)
            pt = pbank(pb)
            P.mm(pt[:, 0:6], tri[d_], a)
            P.mm(pt[:, 8:14], onesf, a)
            P.copy(cs, pt[:, 0:6])
            P.ts(ncs, cs, -1.0, None, ALU.mult)
            P.tt(dte, pt[:, 8:14], ncs, ALU.add)
            P.act(dte, dte, AF.Exp)
            P.act(dtot, pt[:, 8:14], AF.Exp)
            P.tt(w_, dtv, dte, ALU.mult)
            x3 = xtok_b[par].rearrange("p (h d) -> p h d", d=64)
            P.tt(xdte_b[par][d_].rearrange("p (h d) -> p h d", d=64), x3,
                 w_.unsqueeze(2).to_broadcast([128, 6, 64]), ALU.mult)
            if full:
                P.act(ecs, cs, AF.Exp)
                P.tt(xdt_b[par][d_].rearrange("p (h d) -> p h d", d=64), x3,
                     dtv.unsqueeze(2).to_broadcast([128, 6, 64]), ALU.mult)
            return dict(a=a, cs=cs, ncs=ncs, dte=dte, dtot=dtot, ecs=ecs)

        def chunk_state_update(ci, par, d_, q):
            pb = banks(1)
            pt = pbank(pb)
            for g in range(2):
                P.mm(pt[:, g * 192:(g + 1) * 192], btok_b[par][:, g * 128:(g + 1) * 128],
                     xdte_b[par][d_][:, g * 192:(g + 1) * 192])
            S3 = Sst[d_].rearrange("p (h d) -> p h d", d=64)
            P.tt(S3, S3, q["dtot"].unsqueeze(2).to_broadcast([128, 6, 64]), ALU.mult)
            P.tt(Sst[d_], Sst[d_], pt[:, 0:384], ALU.add)
            is_end = (ci % 2 == 1) if d_ == 0 else (ci % 2 == 0)
            if is_end:
                seq = ci // 2
                pb2 = banks(1)
                for j in range(3):
                    P.tr(pbank(pb2)[:, j * 128:(j + 1) * 128], Sst[d_][:, j * 128:(j + 1) * 128], identf)
                sob = so[scnt[0] % 2]
                scnt[0] += 1
                P.copy(sob, pbank(pb2)[:, 0:384].rearrange("p (j n) -> p j n", n=128), eng="act")
                P.dma(nssm_o[l, seq, d_].rearrange("(j p) n -> p j n", p=128), sob)
                bnd = ci if d_ == 0 else ci - 1
                if 0 <= bnd <= 14:
                    P.ts(Sst[d_], Sst[d_], seqf[:, bnd:bnd + 1], None, ALU.mult)

        qb_ = {}
        order = list(reversed(range(16)))
        prep_common(order[0], order[0] % 2)
        qb_[order[0]] = prep_dir(order[0], order[0] % 2, 1, False)
        for oi, ci in enumerate(order):
            par = ci % 2
            if oi + 1 < 16:
                cn = order[oi + 1]
                prep_common(cn, cn % 2)
                qb_[cn] = prep_dir(cn, cn % 2, 1, False)
            P.copy(Sbe[:, ci, :], Sst[1], eng="pool")
            chunk_state_update(ci, par, 1, qb_.pop(ci))

        stA = {}

        def stage_A(ci):
            par = ci % 2
            t0 = ci * 128
            prep_common(ci, par)
            qs = [prep_dir(ci, par, 0, True), prep_dir(ci, par, 1, True)]
            pcb = banks(1)
            for g in range(2):
                P.mm(pbank(pcb)[:, g * 128:(g + 1) * 128], BT[:, g, t0:t0 + 128], CT[:, g, t0:t0 + 128])
            pcs = banks(3)
            pcsv = pbank(pcs, 3).rearrange("p (j l) -> p j l", l=128)
            for d_ in range(2):
                for h in range(6):
                    j = d_ * 6 + h
                    P.mm(pcsv[:, j, :], qs[d_]["a"][:, h:h + 1].to_broadcast([128, 128]), tri[d_],
                         start=True, stop=False, skip_group_check=True)
                    P.mm(pcsv[:, j, :], identb, mneg[d_], start=False, stop=True, skip_group_check=True)
            dec = dec_b[par]
            scb_ = sc_b[par]
            for d_ in range(2):
                for h in range(6):
                    j = d_ * 6 + h
                    P.act(dec[:, j, :], pcsv[:, j, :], AF.Exp, bias=qs[d_]["ncs"][:, h:h + 1])
            for d_ in range(2):
                for g in range(2):
                    j0 = d_ * 6 + g * 3
                    P.tt(scb_[:, j0:j0 + 3, :],
                         pbank(pcb)[:, g * 128:(g + 1) * 128].unsqueeze(1).to_broadcast([128, 3, 128]),
                         dec[:, j0:j0 + 3, :], ALU.mult)
            stA[ci] = qs

        def stage_B(ci):
            par = ci % 2
            t0 = ci * 128
            qs = stA.pop(ci)
            scb_ = sc_b[par]
            P.copy(Sent, Sst[0], eng="pool")
            pyd = banks(1)
            for h in range(6):
                for d_ in range(2):
                    j = d_ * 6 + h
                    P.mm(pbank(pyd)[:, h * 64:(h + 1) * 64], scb_[:, j, :], xdt_b[par][d_][:, h * 64:(h + 1) * 64],
                         start=(d_ == 0), stop=(d_ == 1), skip_group_check=True)
            pyo = [banks(1), banks(1)]
            for d_ in range(2):
                src = Sent if d_ == 0 else Sbe[:, ci, :]
                for g in range(2):
                    P.mm(pbank(pyo[d_])[:, g * 192:(g + 1) * 192], CT[:, g, t0:t0 + 128], src[:, g * 192:(g + 1) * 192])
            y0, y1, y2 = yc
            for d_, yy in ((0, y0), (1, y1)):
                P.tt(yy.rearrange("p (h d) -> p h d", d=64),
                     pbank(pyo[d_])[:, 0:384].rearrange("p (h d) -> p h d", d=64),
                     qs[d_]["ecs"].unsqueeze(2).to_broadcast([128, 6, 64]), ALU.mult)
            P.tt(y0, y0, y1, ALU.add)
            P.tt(y0, pbank(pyd)[:, 0:384], y0, ALU.add)
            P.tt(y1.rearrange("p (h d) -> p h d", d=64), xtok_b[par].rearrange("p (h d) -> p h d", d=64),
                 lp["drow"].unsqueeze(2).to_broadcast([128, 6, 64]), ALU.mult)
            P.tt(y0, y0, y1, ALU.add)
            dma_dbg("yraw%d" % ci, y0)
            P.tt(y0, y0, zg[:, ci, :], ALU.mult)
            P.tt(y2, y0, y0, ALU.mult)
            P.op("dve", lambda e, y2=y2: e.reduce_sum(rs[:, 0:1], y2, axis=AX.X), reads=[y2], writes=[rs[:, 0:1]])
            P.act(rs[:, 1:2], rs[:, 0:1], AF.Ln, bias=EPS, scale=1.0 / 384)
            P.act(rs[:, 1:2], rs[:, 1:2], AF.Exp, scale=-0.5)
            P.stt(ynb, y0, rs[:, 1:2], lp["snw"], ALU.mult, ALU.mult)
            pb = banks(1)
            pv = psb[:, pb * 1024:(pb + 1) * 1024]
            for j in range(3):
                P.tr(pv[:, j * 128:(j + 1) * 128], ynb[:, j * 128:(j + 1) * 128], identb)
            P.copy(yT[:, ci, :, :], pv[:, 0:384].rearrange("p (j t) -> p j t", t=128), eng="act")
            chunk_state_update(ci, par, 0, qs[0])

        stage_A(0)
        for ci in range(16):
            if ci + 1 < 16:
                stage_A(ci + 1)
            stage_B(ci)
        A.reset(mL)
        if stop_after == "M2" and l == 0:
            break

        KA = A.alloc([128, 2, NKEY], BF16)
        KB = A.alloc([128, 4, NKEY], BF16)
        VX = A.alloc([128, NKT, 576], BF16)
        mK = A.mark()
        P.memset(KA, 0.0, eng="pool")
        P.memset(KB, 0.0, eng="pool")
        P.dma(KA[64:73, :, :], mk_d[:, :].unsqueeze(1).to_broadcast([9, 2, NKEY]), q="pool")
        P.dma(KB[32:41, :, :], mk_d[:, :].unsqueeze(1).to_broadcast([9, 4, NKEY]), q="pool")
        P.dma(KB[96:105, :, :], mk_d[:, :].unsqueeze(1).to_broadcast([9, 4, NKEY]), q="pool")
        VX4 = VX.rearrange("p k (a s d) -> p k a s d", s=3, d=64)
        P.memset(VX4[:, :, :, 1, :], 1.0, eng="pool")
        hT = A.alloc([128, KD, 512], BF16)
        rope = [A.alloc([128, 512], F32) for _ in range(4)]
        sqb = A.alloc([128, 512], BF16)
        lnv = A.alloc([128, 512], F32)
        kn = A.alloc([128, 512], F32)
        t1 = A.alloc([128, 512], F32)
        kr = A.alloc([128, 512], F32)
        ktr = [A.alloc([128, 512], F32) for _ in range(2)]
        vst = [A.alloc([128, 384], F32) for _ in range(2)]
        cin = A.alloc([128, 2, 256], F32)
        ktc = [0]

        def rope_apply(src, dst, perm, cosT, sinT):
            pb = banks(1)
            P.mm(pbank(pb), perm, src)
            P.tt(t1, src, cosT, ALU.mult)
            P.tt(dst, pbank(pb), sinT, ALU.mult)
            P.tt(dst, dst, t1, ALU.add)

        def head_norm(pt, wcol, dst):
            P.act(sqb, pt, AF.Square)
            pb = banks(1)
            P.mm(pbank(pb), bones, sqb)
            P.act(lnv, pbank(pb), AF.Ln, bias=EPS, scale=1.0 / 64)
            P.act(lnv, lnv, AF.Exp, scale=-0.5)
            P.stt(dst, pt, wcol, lnv, ALU.mult, ALU.mult)

        def out_tok(src, dst_d, c0, ncol):
            pb = banks(1)
            for t4 in range(4):
                P.tr(pbank(pb)[:, t4 * 128:(t4 + 1) * 128], src[:, t4 * 128:(t4 + 1) * 128], identf)
            kt_ = ktr[ktc[0] % 2]
            ktc[0] += 1
            P.copy(kt_, pbank(pb), eng="act")
            P.dma(dst_d[c0:c0 + 512, ncol:ncol + 128].rearrange("(t p) f -> p t f", p=128),
                  kt_.rearrange("p (t f) -> p t f", f=128))

        for kt2 in range(2):
            P.dma(cin[:, 0, 0:128], cak_d[l, kt2 * 128:(kt2 + 1) * 128, :])
            pb = banks(1)
            P.tr(pbank(pb)[:, 0:128], cin[:, 0, 0:128], identf)
            kc = T + kt2 * 128
            P.copy(KA[0:64, 0, kc:kc + 128], pbank(pb)[0:64, 0:128])
            P.copy(KA[0:64, 1, kc:kc + 128], pbank(pb)[64:128, 0:128])
            P.dma(cin[:, 1, :], cbk_d[l, kt2 * 128:(kt2 + 1) * 128, :])
            pb = banks(1)
            for j in range(2):
                P.tr(pbank(pb)[:, j * 128:(j + 1) * 128], cin[:, 1, j * 128:(j + 1) * 128], identf)
            for h in range(4):
                j = h // 2
                r0 = (h % 2) * 64
                P.copy(KB[0:32, h, kc:kc + 128], pbank(pb)[r0:r0 + 32, j * 128:(j + 1) * 128])
                P.copy(KB[64:96, h, kc:kc + 128], pbank(pb)[r0 + 32:r0 + 64, j * 128:(j + 1) * 128])
            P.dma(VX4[:, 16 + kt2, 0, 0:3:2, :], cav_d[l, kt2 * 128:(kt2 + 1) * 128, :].rearrange("p (s d) -> p s d", d=64), q="pool")
            for a_ in range(2):
                P.dma(VX4[:, 16 + kt2, 1 + a_, 0:3:2, :],
                      cbv_d[l, kt2 * 128:(kt2 + 1) * 128, a_ * 128:(a_ + 1) * 128].rearrange("p (s d) -> p s d", d=64), q="pool")

        for c in range(4):
            c0 = c * 512
            adanorm(l, 0, c0, 512, hT)
            for i in (0, 2):
                P.dma(rope[i], rope_d[i][:, c0:c0 + 512])
                P.dma(rope[i + 1], rope_d[i + 1][:, c0:c0 + 512])

            def cons_k(f, pt, c0=c0):
                if f == 0:
                    head_norm(pt, lp["akn"][:, 0:1], kn)
                    rope_apply(kn, kr, permA, rope[0], rope[1])
                    P.copy(KA[0:64, 0, c0:c0 + 512], kr[0:64, :], eng="pool")
                    P.copy(KA[0:64, 1, c0:c0 + 512], kr[64:128, :], eng="pool")
                    out_tok(kr, nak_o[l], c0, 0)
                else:
                    P.copy(kn, pt, eng="act")
                    rope_apply(kn, kr, permB, rope[2], rope[3])
                    for hh in range(2):
                        h = (f - 1) * 2 + hh
                        r0 = hh * 64
                        P.copy(KB[0:32, h, c0:c0 + 512], kr[r0:r0 + 32, :], eng="dve")
                        P.copy(KB[64:96, h, c0:c0 + 512], kr[r0 + 32:r0 + 64, :], eng="dve")
                    out_tok(kr, nbk_o[l], c0, (f - 1) * 128)

            proj_feat(l, C_AK, 1, hT, 512, cons_k)
            proj_feat(l, C_BK, 2, hT, 512, lambda f, pt: cons_k(f + 1, pt))
            wv_ = [load_wslab(l, C_AV, 256), load_wslab(l, C_AV + 256, 128)]
            for t4 in range(4):
                pb = banks(1)
                pt = pbank(pb)
                for k in range(KD):
                    P.mm(pt[:, 0:256], hT[:, k, t4 * 128:(t4 + 1) * 128], wv_[0][:, k, 0:256],
                         start=(k == 0), stop=(k == KD - 1))
                for k in range(KD):
                    P.mm(pt[:, 256:384], hT[:, k, t4 * 128:(t4 + 1) * 128], wv_[1][:, k, 0:128],
                         start=(k == 0), stop=(k == KD - 1), skip_group_check=True)
                vs = vst[t4 % 2]
                P.copy(vs, pt[:, 0:384], eng="act")
                tk = c * 4 + t4
                P.dma(nav_o[l, tk * 128:(tk + 1) * 128, :], vs[:, 0:128])
                P.dma(nbv_o[l, tk * 128:(tk + 1) * 128, :], vs[:, 128:384])
                P.copy(VX4[:, tk, :, 0:3:2, :], vs.rearrange("p (a s d) -> p a s d", s=2, d=64), eng="pool")
        A.reset(mK)
        if stop_after == "M3" and l == 0:
            break

        hT = A.alloc([128, KD, 512], BF16)
        sqb = A.alloc([128, 512], BF16)
        lnv = A.alloc([128, 512], F32)
        QA = A.alloc([128, 6, 512], BF16)
        QB = A.alloc([128, 4, 512], BF16)
        mixT = A.alloc([128, 5, 512], BF16)
        bo = A.alloc([128, 512], F32)
        mOv = A.mark()
        rope = [A.alloc([128, 512], F32) for _ in range(4)]
        kn = A.alloc([128, 512], F32)
        t1 = A.alloc([128, 512], F32)
        kr = A.alloc([128, 512], F32)
        A.reset(mOv)
        PT = [A.alloc([128, 1024], BF16) for _ in range(3)]
        rz = A.alloc([128, 2, 512], F32)
        ob = A.alloc([128, 2, 512], F32)
        A.reset(mOv)
        P.memset(QA, 0.0, eng="pool")
        P.memset(QB, 0.0, eng="pool")
        sA = 1.0 / 8.0
        sB = 32.0 ** -0.5
        ptc = [0]
        for c in range(4):
            c0 = c * 512
            adanorm(l, 0, c0, 512, hT)
            for i in range(4):
                P.dma(rope[i], rope_d[i][:, c0:c0 + 512])
            P.dma(QA[64:73, :, :], mq_d[:, c0:c0 + 512].unsqueeze(1).to_broadcast([9, 6, 512]), q="pool")
            P.dma(QB[32:41, :, :], mq_d[:, c0:c0 + 512].unsqueeze(1).to_broadcast([9, 4, 512]), q="pool")
            P.dma(QB[96:105, :, :], mq_d[:, c0:c0 + 512].unsqueeze(1).to_broadcast([9, 4, 512]), q="pool")

            def cons_qa(f, pt):
                head_norm(pt, lp["aqn"][:, 0:1], kn)
                rope_apply(kn, kr, permA, rope[0], rope[1])
                P.copy(QA[0:64, f, :], kr[0:64, :], eng="pool")
                P.copy(QA[0:64, f + 3, :], kr[64:128, :], eng="pool")

            def cons_qb(f, pt):
                P.copy(kn, pt, eng="act")
                rope_apply(kn, kr, permB, rope[2], rope[3])
                for hh in range(2):
                    h = f * 2 + hh
                    r0 = hh * 64
                    P.copy(QB[0:32, h, :], kr[r0:r0 + 32, :], eng="dve")
                    P.copy(QB[64:96, h, :], kr[r0 + 32:r0 + 64, :], eng="dve")

            proj_feat(l, C_AQ, 3, hT, 512, cons_qa)
            proj_feat(l, C_BQ, 2, hT, 512, cons_qb)

            groups = [("A", f) for f in range(3)] + [("B", h) for h in range(4)]
            def group_units(kind, idx):
                if kind == "A":
                    return ([(KA[0:73, 0, :], QA[0:73, idx, :], 0, 128),
                             (KA[0:73, 1, :], QA[0:73, idx + 3, :], 64, 192)], sA)
                h = idx
                vc = (1 + h // 2) * 192 + (0 if h % 2 == 0 else 64)
                return ([(KB[0:41, h, :], QB[0:41, h, :], vc, vc + 128),
                         (KB[64:105, h, :], QB[64:105, h, :], vc, vc + 128)], sB)

            gunits = [group_units(k_, i_) for (k_, i_) in groups]
            steps = [(gi, kt) for gi in range(len(groups)) for kt in range(NKT)]
            pts = {}
            pending = []

            def emit_S(si):
                gi, kt = steps[si]
                units, scl = gunits[gi]
                sb_ = 2 * (si % 2)
                for u, (Kt, Qt, v0, v1) in enumerate(units):
                    P.mm(pbank(sb_ + u), Kt[:, kt * 128:(kt + 1) * 128], Qt)
                pt_ = PT[si % 3]
                pts[si] = pt_
                P.act(pt_, pbank(sb_, 2), AF.Exp, scale=scl)

            def emit_AV(si):
                gi, kt = steps[si]
                units, scl = gunits[gi]
                ob_ = 4 + 2 * (gi % 2)
                pt_ = pts.pop(si)
                for u, (Kt, Qt, v0, v1) in enumerate(units):
                    P.mm(pbank(ob_ + u), VX[:, kt, v0:v1], pt_[:, u * 512:(u + 1) * 512],
                         start=(kt == 0), stop=(kt == NKT - 1))
                if kt == NKT - 1:
                    evac(gi, si)

            def evac(gi, si):
                kind, idx = groups[gi]
                ob_ = 4 + 2 * (gi % 2)
                if kind == "A":
                    O0 = pbank(ob_)
                    O1 = pbank(ob_ + 1)
                    P.op("dve", lambda e, O0=O0: e.reciprocal(rz[0:64, 0, :], O0[64:128, :]),
                         reads=[O0[64:128, :]], writes=[rz[0:64, 0, :]])
                    P.tt(mixT[0:64, idx, :], O0[0:64, :], rz[0:64, 0, :], ALU.mult)
                    P.op("dve", lambda e, O1=O1: e.reciprocal(rz[64:128, 1, :], O1[0:64, :]),
                         reads=[O1[0:64, :]], writes=[rz[64:128, 1, :]])
                    P.tt(mixT[64:128, idx, :], O1[64:128, :], rz[64:128, 1, :], ALU.mult)
                    return
                h = idx
                r0 = 0 if h % 2 == 0 else 64
                z0 = 64 - r0
                for u in range(2):
                    Ou = pbank(ob_ + u)
                    P.op("dve", lambda e, Ou=Ou, u=u, r0=r0, z0=z0: e.reciprocal(rz[r0:r0 + 64, u, :], Ou[z0:z0 + 64, :]),
                         reads=[Ou[z0:z0 + 64, :]], writes=[rz[r0:r0 + 64, u, :]])
                    P.tt(ob[r0:r0 + 64, u, :], Ou[r0:r0 + 64, :], rz[r0:r0 + 64, u, :], ALU.mult)
                P.stt(bo[r0:r0 + 64, :], ob[r0:r0 + 64, 1, :], lp["nlam"][r0:r0 + 64, 0:1], ob[r0:r0 + 64, 0, :],
                      ALU.mult, ALU.add)
                if h % 2 == 1:
                    P.act(sqb, bo, AF.Square)

                    def subln(h=h):
                        pb = 2 * (len(pts) % 2)
                        P.mm(pbank(pb), bones, sqb)
                        P.act(lnv, pbank(pb), AF.Ln, bias=EPS, scale=1.0 / 64)
                        P.act(lnv, lnv, AF.Exp, scale=-0.5)
                        P.stt(mixT[:, 3 + h // 2, :], bo, lp["bsub"][:, 0:1], lnv, ALU.mult, ALU.mult)
                    pending.append((si + 3, subln))

            NS = len(steps)
            emit_S(0)
            for si in range(1, NS):
                emit_S(si)
                emit_AV(si - 1)
                for item in list(pending):
                    if item[0] <= si:
                        pending.remove(item)
                        item[1]()
            emit_AV(NS - 1)
            for item in list(pending):
                pending.remove(item)
                item[1]()
            if ("mixT%d" % c) in dbg_o:
                tmpf = A.alloc([128, 5, 512], F32)
                P.copy(tmpf, mixT)
                P.dma(dbg_o["mixT%d" % c], tmpf)
            wov = w_out_d[l].rearrange("(k p) n -> p k n", p=128)
            for dp in range(4):
                wb = next_wsl()
                P.dma(wb[:, :, :], wov[:, :, dp * 256:(dp + 1) * 256], q="pool")
                for dd in range(2):
                    dt_ = dp * 2 + dd
                    pb = banks(1)
                    pt = pbank(pb)
                    for k in range(KD):
                        if k < 5:
                            rhs = mixT[:, k, :]
                        else:
                            rhs = yT[:, c * 4:(c + 1) * 4, k - 5, :]
                        P.mm(pt, wb[:, k, dd * 128:(dd + 1) * 128], rhs, start=(k == 0), stop=(k == KD - 1))
                    P.stt(xT[:, dt_, c0:c0 + 512], pt, mod[l][:, 16 + dt_:17 + dt_], xT[:, dt_, c0:c0 + 512],
                          ALU.mult, ALU.add)
        A.reset(mL)
        if stop_after == "M4" and l == 0:
            break

        A.reset(mLP)
        hT = A.alloc([128, KD, 1026], BF16)
        P.memset(hT, 0.0, eng="pool")
        hkeep = A.alloc([128, KD, 1], BF16)
        hmid = A.alloc([128, NFF, 1024], BF16)
        wd = [A.alloc([128, NFF, 128], BF16) for _ in range(2)]
        mOv = A.mark()
        Ub = [[A.alloc([128, 1026], F32) for _ in range(2)] for _ in range(2)]
        ab = [[A.alloc([128, 1024], F32) for _ in range(2)] for _ in range(2)]
        A.reset(mOv)
        upv = ffn_up_d[l].rearrange("(k p) n -> p k n", p=128)
        dnv = ffn_down_d[l].rearrange("(k p) n -> p k n", p=128)
        fc = [0]
        for hf in range(2):
            h0 = hf * 1024
            if hf == 0:
                adanorm(l, 1, 0, 1025, hT, col0=1)
                P.copy(hkeep, hT[:, :, 1024:1025], eng="pool")
            else:
                adanorm(l, 1, 1024, 1024, hT, col0=1)
                P.copy(hT[:, :, 0:1], hkeep, eng="pool")
            pend_f = []
            for s in range(11):
                wg = next_wsl()
                P.dma(wg[:, :, :], upv[:, :, s * 256:(s + 1) * 256], q="pool")
                wvv = next_wsl()
                P.dma(wvv[:, :, :], upv[:, :, D_FF + s * 256:D_FF + (s + 1) * 256], q="pool")
                for j in range(2):
                    i = s * 2 + j
                    par = i % 2
                    pts_ = []
                    for which, wb in ((0, wg), (1, wvv)):
                        pb = banks(3)
                        pt = pbank(pb, 3)
                        for (j0, w) in colgroups(1026):
                            for k in range(KD):
                                P.mm(pt[:, j0:j0 + w], wb[:, k, j * 128:(j + 1) * 128], hT[:, k, j0:j0 + w],
                                     start=(k == 0), stop=(k == KD - 1))
                        pts_.append(pt)
                    for which in range(2):
                        ft = i + which * NFF
                        conv_evac(Ub[par][which], ab[par][which], 1024, lp["fcw"][:, ft, :], lp["fcb"][:, ft:ft + 1], pts_[which])
                    for fn_ in pend_f:
                        fn_()
                    del pend_f[:]
                    for which in range(2):
                        ft = i + which * NFF
                        conv_tile(Ub[par][which], ab[par][which], 1024, lp["fcw"][:, ft, :], lp["fcb"][:, ft:ft + 1],
                                  lp["nfcw0"][:, ft:ft + 1], lp["nfcw2"][:, ft:ft + 1], h0, pts_[which], evac=False)

                    def fin(i=i, par=par):
                        P.act(ab[par][0], ab[par][0], AF.Silu)
                        P.tt(hmid[:, i, :], ab[par][1], ab[par][0], ALU.mult)
                    pend_f.append(fin)
            for fn_ in pend_f:
                fn_()
            del pend_f[:]
            for dt_ in range(8):
                wdb = wd[dt_ % 2]
                P.dma(wdb[:, :, :], dnv[:, :, dt_ * 128:(dt_ + 1) * 128], q="pool")
                for nch in range(2):
                    pb = banks(1)
                    pt = pbank(pb)
                    for i in range(NFF):
                        P.mm(pt, wdb[:, i, :], hmid[:, i, nch * 512:(nch + 1) * 512],
                             start=(i == 0), stop=(i == NFF - 1))
                    tsl = slice(h0 + nch * 512, h0 + (nch + 1) * 512)
                    P.stt(xT[:, dt_, tsl], pt, mod[l][:, 40 + dt_:41 + dt_], xT[:, dt_, tsl], ALU.mult, ALU.add)
        A.reset(mL)
        if stop_after == "F0" and l == 0:
            break

    A.reset(0)
    if stop_after is None or stop_after == "FIN":
        sq = [A.alloc([128, 512], BF16) for _ in range(2)]
        lnv = A.alloc([128, 512], F32)
        yn = A.alloc([128, KD, 512], F32)
        yo = [A.alloc([128, D], F32) for _ in range(2)]
        for c in range(4):
            c0 = c * 512
            pb = banks(1)
            for k in range(KD):
                P.act(sq[k % 2], xT[:, k, c0:c0 + 512], AF.Square)
                P.mm(pbank(pb), onesb, sq[k % 2], start=(k == 0), stop=(k == KD - 1))
            P.act(lnv, pbank(pb), AF.Ln, bias=EPS, scale=1.0 / D)
            P.act(lnv, lnv, AF.Exp, scale=-0.5)
            for k in range(KD):
                P.stt(yn[:, k, :], xT[:, k, c0:c0 + 512], fnw[:, k:k + 1], lnv, ALU.mult, ALU.mult)
            for t4 in range(4):
                yb = yo[t4 % 2]
                for hb in range(2):
                    pb2 = banks(1)
                    for k4 in range(4):
                        k = hb * 4 + k4
                        P.tr(pbank(pb2)[:, k4 * 128:(k4 + 1) * 128], yn[:, k, t4 * 128:(t4 + 1) * 128], identf)
                    P.copy(yb[:, hb * 512:(hb + 1) * 512], pbank(pb2), eng=("act" if hb else "dve"))
                tk = c * 4 + t4
                P.dma(y_o[tk * 128:(tk + 1) * 128, :], yb)
    else:
        pass
    if "xT" in dbg_o:
        P.dma(dbg_o["xT"], xT[:, :, :])
    P.finalize()
    P.arena_peak = A.peak
    return nc, P


def _rope_tables(L, d, grid_w=64, theta=10000.0):
    rows = L // grid_w
    row = np.repeat(np.arange(rows), grid_w).astype(np.float32)
    col = np.tile(np.arange(grid_w), rows).astype(np.float32)
    quarter = d // 4
    inv = (np.float32(theta) ** (-np.arange(quarter, dtype=np.float32) / np.float32(quarter))).astype(np.float32)
    ang_r = row[:, None] * inv[None, :]
    ang_c = col[:, None] * inv[None, :]
    ang = np.concatenate([ang_r, ang_r, ang_c, ang_c], axis=-1)
    return np.cos(ang).astype(np.float32), np.sin(ang).astype(np.float32)


def _perm_mat(d):
    q = d // 4
    Pm = np.zeros((128, 128), np.float32)
    for b0 in range(0, 128, d):
        for i in range(q):
            Pm[b0 + q + i, b0 + i] = -1.0
            Pm[b0 + i, b0 + q + i] = 1.0
            Pm[b0 + 3 * q + i, b0 + 2 * q + i] = -1.0
            Pm[b0 + 2 * q + i, b0 + 3 * q + i] = 1.0
    return Pm


def _consts():
    r = np.arange(128)
    c = {}
    c["identf"] = np.eye(128, dtype=np.float32)
    c["trif"] = (r[:, None] <= r[None, :]).astype(np.float32)
    c["trib"] = (r[:, None] >= r[None, :]).astype(np.float32)
    c["mnegf"] = np.where(r[None, :] < r[:, None], NEG, 0.0).astype(np.float32)
    c["mnegb"] = np.where(r[None, :] > r[:, None], NEG, 0.0).astype(np.float32)
    c["permA"] = _perm_mat(64)
    c["permB"] = _perm_mat(32)
    bo = np.zeros((128, 128), np.float32)
    bo[:64, :64] = 1.0
    bo[64:, 64:] = 1.0
    c["bones"] = bo
    return c


def _shared_weights(inp):
    f = lambda a: np.ascontiguousarray(np.asarray(a, dtype=np.float32))
    w = {}
    w["ada_w"] = f(inp["ada_w"])
    w["ada_b"] = f(np.asarray(inp["ada_b"]).reshape(NL, 48, 128).transpose(0, 2, 1))
    win = np.asarray(inp["w_in"], dtype=np.float32)
    o = np.cumsum([0, 384, 128, 128, 256, 256, 256, 384, 384, 256, 256, 12])
    aq, ak, av, bq, bk, bv, cx, cz, cB, cC, cdt = [win[:, :, o[i]:o[i + 1]] for i in range(11)]
    aqh = aq.reshape(NL, D, 6, 64)
    aqp = aqh[:, :, [0, 3, 1, 4, 2, 5], :].reshape(NL, D, 384)
    w["w_in"] = f(np.concatenate([aqp, ak, bq, bk, cx, cB, cC, av, bv, cz, cdt], axis=-1))
    tile2 = lambda v: f(np.concatenate([v, v], axis=-1)[:, :, None])
    w["aqn"] = tile2(np.asarray(inp["a_q_norm"]))
    w["akn"] = tile2(np.asarray(inp["a_k_norm"]))
    w["blam"] = f(np.asarray(inp["b_lambda"]).reshape(NL, 128))
    w["bsub"] = tile2(np.asarray(inp["b_subln"]))
    w["scw"] = f(np.asarray(inp["ssm_conv_w"]).reshape(NL, 3, 7, 128).transpose(0, 3, 2, 1))
    w["scb"] = f(np.asarray(inp["ssm_conv_b"]).reshape(NL, 7, 128).transpose(0, 2, 1))
    w["alog"] = f(np.asarray(inp["ssm_A_log"]).reshape(NL, 12))
    w["dtb"] = f(np.asarray(inp["ssm_dt_bias"]).reshape(NL, 12))
    w["ssmD"] = f(inp["ssm_D"])
    w["snw"] = f(inp["ssm_norm_w"])
    wo = np.asarray(inp["w_out"], dtype=np.float32)
    woa = wo[:, :384, :].reshape(NL, 6, 64, D)[:, [0, 3, 1, 4, 2, 5]].reshape(NL, 384, D)
    w["w_out"] = f(np.concatenate([woa, wo[:, 384:, :]], axis=1))
    w["ffn_up"] = f(inp["ffn_up"])
    w["fcw"] = f(np.asarray(inp["ffn_conv_w"]).reshape(NL, 3, 44, 128).transpose(0, 3, 2, 1))
    w["fcb"] = f(np.asarray(inp["ffn_conv_b"]).reshape(NL, 44, 128).transpose(0, 2, 1))
    w["ffn_down"] = f(inp["ffn_down"])
    w["fnw"] = f(np.asarray(inp["final_norm_w"]).reshape(8, 128).T)
    return w


def _core_inputs(inp, core, shared, consts, ropeS):
    f = lambda a: np.ascontiguousarray(np.asarray(a, dtype=np.float32))
    m = dict(shared)
    m.update(consts)
    is_prompt = core >= 4
    if not is_prompt:
        b = core
        m["x"] = f(inp["x_sample"][b])
        cvec = np.asarray(inp["c"])[b]
        m["cak"] = f(np.asarray(inp["cache_a_k"])[b].reshape(NL, 256, 128))
        m["cav"] = f(np.asarray(inp["cache_a_v"])[b].reshape(NL, 256, 128))
        m["cbk"] = f(np.asarray(inp["cache_b_k"])[b].reshape(NL, 256, 256))
        m["cbv"] = f(np.asarray(inp["cache_b_v"])[b].reshape(NL, 256, 256))
        m["ssm0"] = f(np.asarray(inp["state_ssm"])[b].reshape(NL, 2, 384, 128))
        mq = np.zeros((9, T), np.float32)
        mq[8] = 1.0
        mk = np.zeros((9, NKEY), np.float32)
        m["cosA"], m["sinA"], m["cosB"], m["sinB"] = ropeS
        m["pflag"] = np.zeros((128, 1), np.float32)
        m["seqf"] = np.ones((128, 16), np.float32)
    else:
        j = core - 4
        m["x"] = f(np.asarray(inp["x_prompt"])[8 * j:8 * j + 8].reshape(T, D))
        cvec = np.asarray(inp["c_ctx"])
        m["cak"] = np.zeros((NL, 256, 128), np.float32)
        m["cav"] = np.zeros((NL, 256, 128), np.float32)
        m["cbk"] = np.zeros((NL, 256, 256), np.float32)
        m["cbv"] = np.zeros((NL, 256, 256), np.float32)
        m["ssm0"] = np.zeros((NL, 2, 384, 128), np.float32)
        seq_q = np.arange(T) // 256
        mq = np.zeros((9, T), np.float32)
        mq[seq_q, np.arange(T)] = 1.0
        mq[8] = 1.0
        mk = np.zeros((9, NKEY), np.float32)
        mk[seq_q, np.arange(T)] = MASKV
        mk[8] = -MASKV
        one = np.ones((128, T), np.float32)
        zero = np.zeros((128, T), np.float32)
        m["cosA"], m["sinA"], m["cosB"], m["sinB"] = one, zero, one, zero
        m["pflag"] = np.ones((128, 1), np.float32)
        sf = np.ones((128, 16), np.float32)
        sf[:, 1::2] = 0.0
        m["seqf"] = sf
    m["cvec"] = f(np.asarray(cvec).reshape(8, 128).T)
    m["mq"] = mq
    m["mk"] = mk
    return m


def make_in_maps(inp):
    shared = _shared_weights(inp)
    consts = _consts()
    cA, sA_ = _rope_tables(T, 64)
    cB, sB_ = _rope_tables(T, 32)
    tA = lambda a: np.ascontiguousarray(np.tile(a.T, (2, 1)))
    tB = lambda a: np.ascontiguousarray(np.tile(a.T, (4, 1)))
    ropeS = (tA(cA), tA(sA_), tB(cB), tB(sB_))
    return [_core_inputs(inp, core, shared, consts, ropeS) for core in range(8)]


_NC_CACHE = {}


def kernel(**inputs):
    in_maps = make_in_maps(inputs)
    if "nc" not in _NC_CACHE:
        _NC_CACHE["nc"] = build()[0]
    nc = _NC_CACHE["nc"]
    res = run_bass_kernel_spmd(nc, in_maps, core_ids=list(range(8)))
    r = res.results
    B = 32
    y_sample = np.stack([r[b]["y"] for b in range(4)], axis=0).astype(np.float32)
    y_prompt = np.concatenate([r[4 + j]["y"].reshape(8, 256, D) for j in range(4)], axis=0).astype(np.float32)

    def gather(name, tail):
        parts = []
        for j in range(4):
            a = r[4 + j][name]
            a = a.reshape(NL, 8, 256, -1).transpose(1, 0, 2, 3)
            parts.append(a)
        a = np.concatenate(parts, axis=0)
        return np.ascontiguousarray(a.reshape((B, NL, 256) + tail)).astype(np.float32)

    new_a_k = gather("nak", (2, 64))
    new_a_v = gather("nav", (2, 64))
    new_b_k = gather("nbk", (4, 2, 32))
    new_b_v = gather("nbv", (4, 64))
    parts = []
    for j in range(4):
        a = r[4 + j]["nssm"]
        parts.append(a.transpose(1, 0, 2, 3, 4))
    new_ssm = np.ascontiguousarray(np.concatenate(parts, axis=0).reshape(B, NL, 2, 6, 64, 128)).astype(np.float32)
    return (y_prompt, y_sample, new_a_k, new_a_v, new_b_k, new_b_v, new_ssm)
```

```python
import math
import numpy as np
import concourse.bass as bass
import concourse.mybir as mybir
from concourse.bass_utils import run_bass_kernel_spmd

F32 = mybir.dt.float32
BF16 = mybir.dt.bfloat16
AF = mybir.ActivationFunctionType
ALU = mybir.AluOpType
AX = mybir.AxisListType
_DSZ = {F32: 4, BF16: 2}


def dsz(dt):
    return _DSZ[dt]


class _Op:
    __slots__ = ("eng", "fn", "deps", "gidx", "idx", "is_dma", "defer", "sig", "tok", "dsem", "dval")

    def __init__(self, eng, fn, deps, gidx, idx, is_dma, defer):
        self.eng = eng
        self.fn = fn
        self.deps = deps
        self.gidx = gidx
        self.idx = idx
        self.is_dma = is_dma
        self.defer = defer
        self.sig = False
        self.tok = None
        self.dsem = None
        self.dval = None


class Prog:
    ENGS = ("pe", "act", "dve", "pool", "sp")
    NDS = 8

    def __init__(self, nc):
        self.nc = nc
        self.ops = {e: [] for e in self.ENGS}
        self.all_ops = []
        self.last_w = {}
        self.readers = {}
        self.chunk = {}
        self.rowbytes = {}
        self._cm = []
        self.psum_names = set()

    def sbuf(self, name, shape, dtype, chunk=None):
        g = self.nc.sbuf_tensor(name, list(shape), dtype)
        t = g.__enter__()
        self._cm.append(g)
        rb = int(np.prod(shape[1:])) * dsz(dtype)
        self.rowbytes[name] = rb
        self.chunk[name] = chunk if chunk else rb
        return t

    def psum(self, name, shape, dtype=F32):
        g = self.nc.psum_tensor(name, list(shape), dtype)
        t = g.__enter__()
        self._cm.append(g)
        rb = int(np.prod(shape[1:])) * dsz(dtype)
        self.rowbytes[name] = rb
        self.chunk[name] = 2048
        self.psum_names.add(name)
        return t

    def res(self, ap):
        sp = str(ap.space)
        if "DRAM" in sp.upper():
            return []
        name = ap.tensor.name
        rb = self.rowbytes[name]
        es = dsz(ap.dtype)
        ch = self.chunk[name]
        base = (int(ap.offset) * es) % rb
        dims = [(abs(st), cnt) for (st, cnt) in ap.ap[1:] if cnt > 1]
        if not dims:
            return [(name, base // ch)]
        dims.sort()
        inner_st, inner_cnt = dims[0]
        outer = dims[1:]
        nout = 1
        for (_, c) in outer:
            nout *= c
        keys = set()
        if nout > 256:
            span = sum((c - 1) * s for (s, c) in dims)
            lo, hi = base, base + (span + 1) * es
            return [(name, c) for c in range(lo // ch, (hi - 1) // ch + 1)]
        offs = [0]
        for (s, c) in outer:
            offs = [o + i * s for o in offs for i in range(c)]
        ispan = ((inner_cnt - 1) * inner_st + 1) * es
        for o in offs:
            lo = base + o * es
            hi = lo + ispan
            for c in range(lo // ch, (hi - 1) // ch + 1):
                keys.add((name, c))
        return list(keys)

    limit = None

    def op(self, eng, fn, reads=(), writes=(), is_dma=False, defer=False):
        if self.limit is not None and len(self.all_ops) >= self.limit:
            return None
        rk = set()
        for a in reads:
            rk.update(self.res(a))
        wk = set()
        for a in writes:
            wk.update(self.res(a))
        deps = set()
        for r in rk:
            w = self.last_w.get(r)
            if w is not None:
                deps.add(w)
            if r[0] in self.psum_names:
                for t in self.readers.get(r, ()):
                    if t.eng != eng:
                        deps.add(t)
        for r in wk:
            w = self.last_w.get(r)
            if w is not None:
                deps.add(w)
            for t in self.readers.get(r, ()):
                deps.add(t)
        o = _Op(eng, fn, deps, len(self.all_ops), len(self.ops[eng]), is_dma, defer)
        self.ops[eng].append(o)
        self.all_ops.append(o)
        for r in rk:
            if r not in wk:
                self.readers.setdefault(r, []).append(o)
        for r in wk:
            self.last_w[r] = o
            self.readers[r] = []
        return o

    def mm(self, out, lhsT, rhs, start=True, stop=True, defer=None, **kw):
        if defer is None:
            defer = not stop
        return self.op("pe", lambda e: e.matmul(out, lhsT, rhs, start=start, stop=stop, **kw),
                       reads=[lhsT, rhs], writes=[out], defer=defer)

    def tr(self, out, in_, ident, defer=False):
        return self.op("pe", lambda e: e.transpose(out, in_, ident), reads=[in_, ident], writes=[out], defer=defer)

    def act(self, out, in_, func, bias=None, scale=None):
        kw = {}
        reads = [in_]
        if bias is not None:
            kw["bias"] = bias
            if not isinstance(bias, (int, float)):
                reads.append(bias)
        if scale is not None:
            kw["scale"] = scale
            if not isinstance(scale, (int, float)):
                reads.append(scale)
        return self.op("act", lambda e: e.activation(out, in_, func, **kw), reads=reads, writes=[out])

    def tt(self, out, in0, in1, op, eng="dve"):
        return self.op(eng, lambda e: e.tensor_tensor(out, in0, in1, op), reads=[in0, in1], writes=[out])

    def ts(self, out, in0, s1, s2, op0, op1=None, eng="dve"):
        reads = [in0]
        for s in (s1, s2):
            if s is not None and not isinstance(s, (int, float)):
                reads.append(s)
        if op1 is None:
            return self.op(eng, lambda e: e.tensor_scalar(out, in0, s1, s2, op0), reads=reads, writes=[out])
        return self.op(eng, lambda e: e.tensor_scalar(out, in0, s1, s2, op0, op1), reads=reads, writes=[out])

    def stt(self, out, in0, scalar, in1, op0, op1, eng="dve"):
        reads = [in0, in1]
        if not isinstance(scalar, (int, float)):
            reads.append(scalar)
        return self.op(eng, lambda e: e.scalar_tensor_tensor(out, in0, scalar, in1, op0, op1),
                       reads=reads, writes=[out])

    def copy(self, out, in_, eng="dve"):
        if eng == "act":
            return self.op("act", lambda e: e.copy(out, in_), reads=[in_], writes=[out])
        return self.op(eng, lambda e: e.tensor_copy(out, in_), reads=[in_], writes=[out])

    def memset(self, ap, val, eng="pool"):
        return self.op(eng, lambda e: e.memset(ap, val), writes=[ap])

    def dma(self, out, in_, q="sp"):
        return self.op(q, lambda e: e.dma_start(out=out, in_=in_), reads=[in_], writes=[out], is_dma=True)

    def finalize(self):
        nc = self.nc
        nxt = {}
        for e in self.ENGS:
            lst = self.ops[e]
            cur = None
            for k in range(len(lst) - 1, -1, -1):
                o = lst[k]
                if not o.is_dma and not o.defer:
                    cur = o
                nxt[o] = cur
        resolve = {}
        for o in self.all_ops:
            for d in o.deps:
                if d.is_dma or (o.eng == "pe" and d.eng == "pe"):
                    continue
                t = d
                if d.defer:
                    t2 = nxt[d]
                    if t2 is not None and t2.gidx < o.gidx:
                        t = t2
                resolve[(d, o)] = t
                t.sig = True
        sems = {e: nc.alloc_semaphore(name="c_" + e) for e in self.ENGS}
        dsems = {}
        for e in self.ENGS:
            if any(o.is_dma for o in self.ops[e]):
                dsems[e] = [nc.alloc_semaphore(name="d_%s_%d" % (e, i)) for i in range(self.NDS)]
        for e in self.ENGS:
            c = 0
            j = 0
            for o in self.ops[e]:
                if o.is_dma:
                    o.dsem = dsems[e][j % self.NDS]
                    o.dval = 16 * (j // self.NDS + 1)
                    j += 1
                elif o.sig:
                    c += 1
                    o.tok = c
        self.sig_counts = {e: sum(1 for o in self.ops[e] if o.sig) for e in self.ENGS}
        progs = self

        def emit(e, eng):
            waited = {}
            lst = progs.ops[e]
            for o in lst:
                need = {}
                for d in o.deps:
                    if e == "pe" and d.eng == "pe":
                        continue
                    if d.is_dma:
                        key, val = d.dsem, d.dval
                    else:
                        t = resolve[(d, o)]
                        key, val = sems[t.eng], t.tok
                    if need.get(key, 0) < val:
                        need[key] = val
                if o.is_dma and o.dval > 16:
                    key, val = o.dsem, o.dval - 16
                    if need.get(key, 0) < val:
                        need[key] = val
                for key, val in need.items():
                    if waited.get(key, 0) < val:
                        eng.wait_ge(key, val)
                        waited[key] = val
                ins = o.fn(eng)
                if o.is_dma:
                    ins.then_inc(o.dsem, 16)
                elif o.sig:
                    ins.then_inc(sems[e], 1)
            if e in dsems:
                last = {}
                for o in lst:
                    if o.is_dma:
                        last[o.dsem] = o.dval
                for key, val in last.items():
                    if waited.get(key, 0) < val:
                        eng.wait_ge(key, val)

        with nc.Block() as block:
            if self.ops["pe"]:
                @block.tensor
                def _(eng):
                    emit("pe", eng)
            if self.ops["act"]:
                @block.scalar
                def _(eng):
                    emit("act", eng)
            if self.ops["dve"]:
                @block.vector
                def _(eng):
                    emit("dve", eng)
            if self.ops["pool"]:
                @block.gpsimd
                def _(eng):
                    emit("pool", eng)
            if self.ops["sp"]:
                @block.sync
                def _(eng):
                    emit("sp", eng)
        return nc


T = 2048
D = 1024
KD = 8
NL = 2
NKEY = 2304
NKT = 18
EPS = 1e-6
D_FF = 2816
NFF = 22
C_AQ, C_AK, C_BQ, C_BK, C_CX, C_CB, C_CC = 0, 384, 512, 768, 1024, 1408, 1664
C_AV, C_BV, C_CZ, C_DT = 1920, 2048, 2304, 2688
MASKV = 2048.0
NEG = -30000.0


class Arena:
    def __init__(self, P, name, nbytes):
        self.P = P
        self.t = P.sbuf(name, [128, nbytes // 4], F32, chunk=256)
        self.n = nbytes
        self.ptr = 0
        self.peak = 0

    def mark(self):
        return self.ptr

    def reset(self, m=0):
        self.ptr = m

    def alloc(self, shape, dtype):
        nb = int(np.prod(shape[1:])) * dsz(dtype)
        nb = (nb + 255) // 256 * 256
        off = self.ptr
        self.ptr += nb
        self.peak = max(self.peak, self.ptr)
        assert self.ptr <= self.n, "arena overflow %d > %d" % (self.ptr, self.n)
        n_el = int(np.prod(shape[1:]))
        v = self.t[:, off // 4:(off + nb) // 4]
        if dtype != F32:
            v = v.bitcast(dtype)
        v = v[:, 0:n_el]
        if len(shape) == 3:
            v = v.rearrange("p (a b) -> p a b", b=shape[2])
        elif len(shape) == 4:
            v = v.rearrange("p (a b c) -> p a b c", b=shape[2], c=shape[3])
        return v


def colgroups(n):
    out = []
    j = 0
    while j < n:
        w = min(512, n - j)
        out.append((j, w))
        j += w
    return out


def build(stop_after=None, dbg=None, limit=None):
    dbg = dbg or {}
    nc = bass.Bass("TRN2", target_bir_lowering=False)
    P = Prog(nc)
    P.limit = limit

    def din(name, shape):
        return nc.dram_tensor(name, list(shape), F32, kind="ExternalInput").ap()

    def dout(name, shape):
        return nc.dram_tensor(name, list(shape), F32, kind="ExternalOutput").ap()

    x_d = din("x", [T, D])
    cvec_d = din("cvec", [128, 8])
    cak_d = din("cak", [NL, 256, 128])
    cav_d = din("cav", [NL, 256, 128])
    cbk_d = din("cbk", [NL, 256, 256])
    cbv_d = din("cbv", [NL, 256, 256])
    ssm0_d = din("ssm0", [NL, 2, 384, 128])
    mq_d = din("mq", [9, T])
    mk_d = din("mk", [9, NKEY])
    rope_d = [din(n, [128, T]) for n in ("cosA", "sinA", "cosB", "sinB")]
    pflag_d = din("pflag", [128, 1])
    seqf_d = din("seqf", [128, 16])
    identf_d = din("identf", [128, 128])
    trif_d = din("trif", [128, 128])
    trib_d = din("trib", [128, 128])
    mnegf_d = din("mnegf", [128, 128])
    mnegb_d = din("mnegb", [128, 128])
    permA_d = din("permA", [128, 128])
    permB_d = din("permB", [128, 128])
    bones_d = din("bones", [128, 128])
    ada_w_d = din("ada_w", [NL, D, 6 * D])
    ada_b_d = din("ada_b", [NL, 128, 48])
    w_in_d = din("w_in", [NL, D, 2700])
    aqn_d = din("aqn", [NL, 128, 1])
    akn_d = din("akn", [NL, 128, 1])
    blam_d = din("blam", [NL, 128])
    bsub_d = din("bsub", [NL, 128, 1])
    scw_d = din("scw", [NL, 128, 7, 3])
    scb_d = din("scb", [NL, 128, 7])
    alog_d = din("alog", [NL, 12])
    dtb_d = din("dtb", [NL, 12])
    ssmD_d = din("ssmD", [NL, 6])
    snw_d = din("snw", [NL, 384])
    w_out_d = din("w_out", [NL, D, D])
    ffn_up_d = din("ffn_up", [NL, D, 2 * D_FF])
    fcw_d = din("fcw", [NL, 128, 44, 3])
    fcb_d = din("fcb", [NL, 128, 44])
    ffn_down_d = din("ffn_down", [NL, D_FF, D])
    fnw_d = din("fnw", [128, 8])

    y_o = dout("y", [T, D])
    nak_o = dout("nak", [NL, T, 128])
    nav_o = dout("nav", [NL, T, 128])
    nbk_o = dout("nbk", [NL, T, 256])
    nbv_o = dout("nbv", [NL, T, 256])
    nssm_o = dout("nssm", [NL, 8, 2, 384, 128])
    dbg_o = {}
    for k, shp in dbg.items():
        dbg_o[k] = dout("dbg_" + k, shp)

    xT = P.sbuf("xT", [128, KD, T], F32, chunk=2048)
    cst = P.sbuf("cst", [128, 8, 128], F32, chunk=512)
    cstb = P.sbuf("cstb", [128, 6, 128], BF16, chunk=256)
    small = P.sbuf("small", [128, 512], F32, chunk=64)
    wsl = [P.sbuf("wsl%d" % i, [128, KD, 256], BF16, chunk=512) for i in range(4)]
    ARENA_BYTES = 116 * 1024
    A = Arena(P, "arena", ARENA_BYTES)
    ps = P.psum("ps", [128, 4096], F32)
    psb = ps[:, :].bitcast(BF16)

    identf = cst[:, 0, :]
    trif = cst[:, 1, :]
    trib = cst[:, 2, :]
    permA = cst[:, 3, :]
    permB = cst[:, 4, :]
    onesf = cst[:, 5, :]
    identb = cstb[:, 0, :]
    mnegf = cstb[:, 1, :]
    mnegb = cstb[:, 2, :]
    bones = cstb[:, 3, :]
    onesb = cstb[:, 4, :]
    tri = [trif, trib]
    mneg = [mnegf, mnegb]

    _sp = [0]

    def salloc(n):
        o = _sp[0]
        _sp[0] += n
        assert _sp[0] <= 512
        return small[:, o:o + n]

    mod = [salloc(48) for _ in range(NL)]
    opsc = [salloc(16) for _ in range(NL)]
    cv = salloc(8)
    pflag = salloc(1)
    seqf = salloc(16)
    fnw = salloc(8)

    bank_rr = [0]

    reserved = set()

    def banks(n):
        for _ in range(16):
            b = bank_rr[0]
            if b + n > 8:
                b = 0
            bank_rr[0] = (b + n) % 8
            if not any((b + i) in reserved for i in range(n)):
                return b
        raise RuntimeError("no free psum banks")

    def pbank(b, n=1):
        return ps[:, b * 512:(b + n) * 512]

    wsl_rr = [0]

    def next_wsl():
        w = wsl[wsl_rr[0] % 4]
        wsl_rr[0] += 1
        return w

    for i, dsrc in enumerate((identf_d, trif_d, trib_d, permA_d, permB_d)):
        P.dma(cst[:, i, :], dsrc[:, :])
    P.memset(cst[:, 5, :], 1.0)
    P.dma(cstb[:, 0, :], identf_d[:, :], q="pool")
    P.dma(cstb[:, 1, :], mnegf_d[:, :], q="pool")
    P.dma(cstb[:, 2, :], mnegb_d[:, :], q="pool")
    P.dma(cstb[:, 3, :], bones_d[:, :], q="pool")
    P.memset(cstb[:, 4, :], 1.0)
    P.dma(cv, cvec_d[:, :])
    P.dma(pflag, pflag_d[:, :])
    P.dma(seqf, seqf_d[:, :])
    P.dma(fnw, fnw_d[:, :])

    m0 = A.mark()
    scv = A.alloc([128, 8], BF16)
    adab = A.alloc([128, 48], F32)
    P.act(scv, cv, AF.Silu)
    aslab = [A.alloc([128, KD, 512], BF16) for _ in range(2)]
    for l in range(NL):
        pb = banks(1)
        pm = pbank(pb)
        awv = ada_w_d[l].rearrange("(k p) n -> p k n", p=128)
        for s in range(12):
            sl = aslab[s % 2]
            P.dma(sl[:, :, :], awv[:, :, s * 512:(s + 1) * 512], q="pool")
            for jj in range(4):
                j = s * 4 + jj
                for k in range(KD):
                    P.mm(pm[:, j:j + 1], sl[:, k, jj * 128:(jj + 1) * 128], scv[:, k:k + 1],
                         start=(k == 0), stop=(k == KD - 1))
        P.dma(adab, ada_b_d[l])
        P.tt(mod[l], pm[:, 0:48], adab, ALU.add)
        P.ts(opsc[l][:, 0:8], mod[l][:, 8:16], 1.0, None, ALU.add)
        P.ts(opsc[l][:, 8:16], mod[l][:, 32:40], 1.0, None, ALU.add)
    A.reset(m0)

    if "mod" in dbg_o:
        P.dma(dbg_o["mod"][:, 0:48], mod[0])
        P.dma(dbg_o["mod"][:, 48:96], mod[1])
    if stop_after == "S0":
        P.finalize()
        return nc, P
    m0 = A.mark()
    xin = [A.alloc([128, D], F32) for _ in range(2)]
    for tt_ in range(16):
        xi = xin[tt_ % 2]
        P.dma(xi, x_d[tt_ * 128:(tt_ + 1) * 128, :])
        for hb in range(2):
            pb = banks(1)
            for k4 in range(4):
                k = hb * 4 + k4
                P.tr(pbank(pb)[:, k4 * 128:(k4 + 1) * 128], xi[:, k * 128:(k + 1) * 128], identf)
            P.copy(xT[:, hb * 4:(hb + 1) * 4, tt_ * 128:(tt_ + 1) * 128],
                   pbank(pb).rearrange("p (a b) -> p a b", b=128), eng=("act" if hb else "dve"))
    A.reset(m0)

    if stop_after == "S1":
        if "xT" in dbg_o:
            P.dma(dbg_o["xT"], xT[:, :, :])
        P.finalize()
        return nc, P
    def adanorm(l, which, t0, n, hT, col0=0):
        m = A.mark()
        sq = [A.alloc([128, n], BF16) for _ in range(2)]
        lnv = A.alloc([128, n], F32)
        tmp = [A.alloc([128, n], F32) for _ in range(2)]
        nbk = (n + 511) // 512
        pb = banks(nbk)
        pss = pbank(pb, nbk)
        for k in range(KD):
            if k % 2 == 0:
                P.act(sq[k % 2], xT[:, k, t0:t0 + n], AF.Square)
            else:
                P.tt(sq[k % 2], xT[:, k, t0:t0 + n], xT[:, k, t0:t0 + n], ALU.mult)
            for (j0, w) in colgroups(n):
                P.mm(pss[:, j0:j0 + w], onesb, sq[k % 2][:, j0:j0 + w], start=(k == 0), stop=(k == KD - 1))
        P.act(lnv, pss[:, 0:n], AF.Ln, bias=EPS, scale=1.0 / D)
        P.act(lnv, lnv, AF.Exp, scale=-0.5)
        shb = 0 if which == 0 else 24
        for k in range(KD):
            P.tt(tmp[k % 2], xT[:, k, t0:t0 + n], lnv, ALU.mult)
            P.act(hT[:, k, col0:col0 + n], tmp[k % 2], AF.Identity,
                  bias=mod[l][:, shb + k:shb + k + 1], scale=opsc[l][:, which * 8 + k:which * 8 + k + 1])
        A.reset(m)

    def load_wslab(l, c0, ncols):
        wb = next_wsl()
        wv = w_in_d[l].rearrange("(k p) n -> p k n", p=128)
        P.dma(wb[:, :, 0:ncols], wv[:, :, c0:c0 + ncols], q="pool")
        return wb

    def proj_feat(l, c0, ntiles, hT, ntok, consume, hcol0=0):
        i = 0
        pend = None
        while i < ntiles:
            nt = min(2, ntiles - i)
            wb = load_wslab(l, c0 + i * 128, nt * 128)
            for j in range(nt):
                nbk = (ntok + 511) // 512
                pb = banks(nbk)
                pt = pbank(pb, nbk)
                for (j0, w) in colgroups(ntok):
                    for k in range(KD):
                        P.mm(pt[:, j0:j0 + w], wb[:, k, j * 128:(j + 1) * 128], hT[:, k, hcol0 + j0:hcol0 + j0 + w],
                             start=(k == 0), stop=(k == KD - 1))
                if pend is not None:
                    reserved.update(range(pb, pb + nbk))
                    consume(*pend)
                    reserved.difference_update(range(pb, pb + nbk))
                pend = (i + j, pt)
            i += nt
        if pend is not None:
            consume(*pend)

    def dma_dbg(name, src):
        if name in dbg_o:
            P.dma(dbg_o[name], src)

    def layer_params(l):
        lp = {}
        lp["aqn"] = A.alloc([128, 1], F32)
        lp["akn"] = A.alloc([128, 1], F32)
        lp["bsub"] = A.alloc([128, 1], F32)
        lp["scw"] = A.alloc([128, 7, 3], F32)
        lp["scb"] = A.alloc([128, 7], F32)
        lp["nscw0"] = A.alloc([128, 7], F32)
        lp["nscw2"] = A.alloc([128, 7], F32)
        lp["fcw"] = A.alloc([128, 44, 3], F32)
        lp["fcb"] = A.alloc([128, 44], F32)
        lp["nfcw0"] = A.alloc([128, 44], F32)
        lp["nfcw2"] = A.alloc([128, 44], F32)
        lp["arow"] = A.alloc([128, 12], F32)
        lp["dtb"] = A.alloc([128, 12], F32)
        lp["drow"] = A.alloc([128, 6], F32)
        lp["snw"] = A.alloc([128, 384], F32)
        lp["nlam"] = A.alloc([128, 1], F32)
        blam = A.alloc([128, 128], F32)
        lt = A.alloc([128, 4], F32)
        P.dma(lp["aqn"], aqn_d[l])
        P.dma(lp["akn"], akn_d[l])
        P.dma(lp["bsub"], bsub_d[l])
        P.dma(lp["scw"], scw_d[l])
        P.dma(lp["scb"], scb_d[l])
        P.dma(lp["fcw"], fcw_d[l])
        P.dma(lp["fcb"], fcb_d[l])
        P.dma(lp["arow"], alog_d[l:l + 1, :].partition_broadcast(128).rearrange("p a b -> p (a b)"))
        P.dma(lp["dtb"], dtb_d[l:l + 1, :].partition_broadcast(128).rearrange("p a b -> p (a b)"))
        P.dma(lp["drow"], ssmD_d[l:l + 1, :].partition_broadcast(128).rearrange("p a b -> p (a b)"))
        P.dma(lp["snw"], snw_d[l:l + 1, :].partition_broadcast(128).rearrange("p a b -> p (a b)"))
        P.dma(blam, blam_d[l:l + 1, :].partition_broadcast(128).rearrange("p a b -> p (a b)"))
        lam_init = 0.8 - 0.6 * math.exp(-0.3 * l)
        P.act(lp["arow"], lp["arow"], AF.Exp)
        P.ts(lp["arow"], lp["arow"], -1.0, None, ALU.mult)
        bl = blam.rearrange("p (a b) -> p a b", b=32)
        pr = A.alloc([128, 2, 32], F32)
        P.tt(pr[:, 0, :], bl[:, 0, :], bl[:, 1, :], ALU.mult)
        P.tt(pr[:, 1, :], bl[:, 2, :], bl[:, 3, :], ALU.mult)
        P.op("dve", lambda e: e.reduce_sum(lt[:, 0:2], pr, axis=AX.X), reads=[pr], writes=[lt[:, 0:2]])
        P.act(lt[:, 0:2], lt[:, 0:2], AF.Exp)
        P.tt(lt[:, 2:3], lt[:, 1:2], lt[:, 0:1], ALU.subtract)
        P.ts(lp["nlam"], lt[:, 2:3], -lam_init, None, ALU.add)
        P.ts(lp["bsub"], lp["bsub"], 1.0 - lam_init, None, ALU.mult)
        for (dst, src, tap, n) in ((lp["nscw0"], lp["scw"], 0, 7), (lp["nscw2"], lp["scw"], 2, 7),
                                   (lp["nfcw0"], lp["fcw"], 0, 44), (lp["nfcw2"], lp["fcw"], 2, 44)):
            P.ts(dst, src[:, :, tap], pflag[:, 0:1], -1.0, ALU.mult, ALU.mult)
        return lp

    def conv_evac(U, a, n, w3, bcol, pt):
        P.copy(U, pt[:, 0:n + 2], eng="act")
        P.act(a[:, 0:n], pt[:, 1:n + 1], AF.Identity, bias=bcol, scale=w3[:, 1:2])

    def conv_tile(U, a, n, w3, bcol, nw0, nw2, tstart, pt, evac=True):
        if evac:
            conv_evac(U, a, n, w3, bcol, pt)
        if tstart == 0:
            P.memset(U[:, 0:1], 0.0, eng="pool")
        if tstart + n == T:
            P.memset(U[:, n + 1:n + 2], 0.0, eng="pool")
        P.stt(a[:, 0:n], U[:, 0:n], w3[:, 0:1], a[:, 0:n], ALU.mult, ALU.add)
        P.stt(a[:, 0:n], U[:, 2:n + 2], w3[:, 2:3], a[:, 0:n], ALU.mult, ALU.add)
        starts = [b - tstart for b in range(256, T, 256) if tstart <= b < tstart + n]
        ends = [b - 1 - tstart for b in range(256, T, 256) if tstart <= b - 1 < tstart + n]
        if starts:
            s0, cnt = starts[0], len(starts)
            av = a[:, s0:s0 + (cnt - 1) * 256 + 1:256] if cnt > 1 else a[:, s0:s0 + 1]
            uv = U[:, s0:s0 + (cnt - 1) * 256 + 1:256] if cnt > 1 else U[:, s0:s0 + 1]
            P.stt(av, uv, nw0, av, ALU.mult, ALU.add)
        if ends:
            s0, cnt = ends[0], len(ends)
            av = a[:, s0:s0 + (cnt - 1) * 256 + 1:256] if cnt > 1 else a[:, s0:s0 + 1]
            uv = U[:, s0 + 2:s0 + 2 + (cnt - 1) * 256 + 1:256] if cnt > 1 else U[:, s0 + 2:s0 + 3]
            P.stt(av, uv, nw2, av, ALU.mult, ALU.add)

    for l in range(NL):
        A.reset(0)
        lp = layer_params(l)
        mLP = A.mark()
        yT = A.alloc([128, 16, 3, 128], BF16)
        mL = A.mark()
        if stop_after == "LP":
            if "lp" in dbg_o:
                P.dma(dbg_o["lp"][:, 0:12], lp["arow"])
                P.dma(dbg_o["nlam"], lp["nlam"])
                P.dma(dbg_o["lp"][:, 13:20], lp["nscw0"])
                P.dma(dbg_o["lp"][:, 20:26], lp["drow"])
            break

        xcT = A.alloc([128, 3, T], BF16)
        BT = A.alloc([128, 2, T], BF16)
        CT = A.alloc([128, 2, T], BF16)
        zg = A.alloc([128, 16, 384], BF16)
        dtt = A.alloc([128, 16, 12], F32)
        mS = A.mark()
        hT = A.alloc([128, KD, 514], BF16)
        P.memset(hT, 0.0, eng="pool")
        Ub = [A.alloc([128, 514], F32) for _ in range(2)]
        ab = [A.alloc([128, 512], F32) for _ in range(2)]
        dts = A.alloc([128, 4, 12], F32)
        cnt = [0]
        for c in range(4):
            c0 = c * 512
            tlo = max(c0 - 1, 0)
            thi = min(c0 + 513, T)
            adanorm(l, 0, tlo, thi - tlo, hT, col0=tlo - (c0 - 1))

            def cons_xbc(f, pt, c0=c0):
                U = Ub[cnt[0] % 2]
                a = ab[cnt[0] % 2]
                cnt[0] += 1
                conv_evac(U, a, 512, lp["scw"][:, f, :], lp["scb"][:, f:f + 1], pt)
                for fn_ in pend_silu:
                    fn_()
                del pend_silu[:]
                conv_tile(U, a, 512, lp["scw"][:, f, :], lp["scb"][:, f:f + 1],
                          lp["nscw0"][:, f:f + 1], lp["nscw2"][:, f:f + 1], c0, pt, evac=False)
                if f < 3:
                    dst = xcT[:, f, c0:c0 + 512]
                elif f < 5:
                    dst = BT[:, f - 3, c0:c0 + 512]
                else:
                    dst = CT[:, f - 5, c0:c0 + 512]
                pend_silu.append(lambda dst=dst, a=a: P.act(dst, a, AF.Silu))

            pend_silu = []
            proj_feat(l, C_CX, 7, hT, 514, cons_xbc)
            for fn_ in pend_silu:
                fn_()
            del pend_silu[:]
            wz = [load_wslab(l, C_CZ, 256), load_wslab(l, C_CZ + 256, 140)]
            pdt = pbank(banks(1))
            for t4 in range(4):
                pb = banks(1)
                pt = pbank(pb)
                lh = lambda k, t4=t4: hT[:, k, 1 + t4 * 128:1 + (t4 + 1) * 128]
                for k in range(KD):
                    P.mm(pt[:, 0:256], lh(k), wz[0][:, k, 0:256], start=(k == 0), stop=(k == KD - 1))
                for k in range(KD):
                    P.mm(pt[:, 256:384], lh(k), wz[1][:, k, 0:128], start=(k == 0), stop=(k == KD - 1),
                         skip_group_check=True)
                for k in range(KD):
                    P.mm(pdt[:, t4 * 16:t4 * 16 + 12], lh(k), wz[1][:, k, 128:140], start=(k == 0), stop=(k == KD - 1),
                         skip_group_check=True)
                P.act(zg[:, c * 4 + t4, :], pt[:, 0:384], AF.Silu)
            P.tt(dts, pdt[:, 0:64].rearrange("p (t j) -> p t j", j=16)[:, :, 0:12],
                 lp["dtb"].unsqueeze(1).to_broadcast([128, 4, 12]), ALU.add)
            P.act(dts, dts, AF.Exp)
            P.act(dtt[:, c * 4:(c + 1) * 4, :], dts, AF.Ln, bias=1.0)
        A.reset(mS)
        if "xcT" in dbg_o:
            tmpf = A.alloc([128, 3, T], F32)
            P.copy(tmpf, xcT)
            P.dma(dbg_o["xcT"], tmpf)
            A.reset(mS)
        if stop_after == "M1" and l == 0:
            break

        Sst = [A.alloc([128, 384], F32) for _ in range(2)]
        Sent = A.alloc([128, 384], BF16)
        Sbe = A.alloc([128, 16, 384], BF16)
        st_in = A.alloc([128, 3, 128], F32)
        for d_ in range(2):
            P.dma(st_in, ssm0_d[l, d_].rearrange("(j p) n -> p j n", p=128))
            pb = banks(1)
            for j in range(3):
                P.tr(pbank(pb)[:, j * 128:(j + 1) * 128], st_in[:, j, :], identf)
            P.copy(Sst[d_], pbank(pb)[:, 0:384])
        so = [A.alloc([128, 3, 128], F32) for _ in range(2)]
        xtok_b = [A.alloc([128, 384], BF16) for _ in range(2)]
        btok_b = [A.alloc([128, 256], BF16) for _ in range(2)]
        sm = [[A.alloc([128, 64], F32) for _ in range(2)] for _ in range(2)]
        xdt_b = [[A.alloc([128, 384], BF16) for _ in range(2)] for _ in range(2)]
        xdte_b = [[A.alloc([128, 384], BF16) for _ in range(2)] for _ in range(2)]
        dec_b = [A.alloc([128, 12, 128], F32) for _ in range(2)]
        sc_b = [A.alloc([128, 12, 128], BF16) for _ in range(2)]
        yc = [A.alloc([128, 384], F32) for _ in range(3)]
        ynb = A.alloc([128, 384], BF16)
        rs = A.alloc([128, 4], F32)
        scnt = [0]

        def prep_common(ci, par):
            t0 = ci * 128
            pb = banks(1)
            pv = psb[:, pb * 1024:(pb + 1) * 1024]
            for j in range(3):
                P.tr(pv[:, j * 128:(j + 1) * 128], xcT[:, j, t0:t0 + 128], identb)
            pb2 = banks(1)
            pv2 = psb[:, pb2 * 1024:(pb2 + 1) * 1024]
            for g in range(2):
                P.tr(pv2[:, g * 128:(g + 1) * 128], BT[:, g, t0:t0 + 128], identb)
            P.copy(xtok_b[par], pv[:, 0:384])
            P.copy(btok_b[par], pv2[:, 0:256], eng="act")

        def prep_dir(ci, par, d_, full):
            s_ = sm[par][d_]
            a = s_[:, 0:6]
            cs = s_[:, 6:12]
            ncs = s_[:, 12:18]
            dte = s_[:, 18:24]
            dtot = s_[:, 24:30]
            ecs = s_[:, 30:36]
            w_ = s_[:, 36:42]
            dtv = dtt[:, ci, d_ * 6:(d_ + 1) * 6]
            P.tt(a, dtv, lp["arow"][:, d_ * 6:(d_ + 1) * 6], ALU.mult)
            pb = banks(1)
            pt = pbank(pb)
            P.mm(pt[:, 0:6], tri[d_], a)
            P.mm(pt[:, 8:14], onesf, a)
            P.copy(cs, pt[:, 0:6])
            P.ts(ncs, cs, -1.0, None, ALU.mult)
            P.tt(dte, pt[:, 8:14], ncs, ALU.add)
            P.act(dte, dte, AF.Exp)
            P.act(dtot, pt[:, 8:14], AF.Exp)
            P.tt(w_, dtv, dte, ALU.mult)
            x3 = xtok_b[par].rearrange("p (h d) -> p h d", d=64)
            P.tt(xdte_b[par][d_].rearrange("p (h d) -> p h d", d=64), x3,
                 w_.unsqueeze(2).to_broadcast([128, 6, 64]), ALU.mult)
            if full:
                P.act(ecs, cs, AF.Exp)
                P.tt(xdt_b[par][d_].rearrange("p (h d) -> p h d", d=64), x3,
                     dtv.unsqueeze(2).to_broadcast([128, 6, 64]), ALU.mult)
            return dict(a=a, cs=cs, ncs=ncs, dte=dte, dtot=dtot, ecs=ecs)

        def chunk_state_update(ci, par, d_, q):
            pb = banks(1)
            pt = pbank(pb)
            for g in range(2):
                P.mm(pt[:, g * 192:(g + 1) * 192], btok_b[par][:, g * 128:(g + 1) * 128],
                     xdte_b[par][d_][:, g * 192:(g + 1) * 192])
            S3 = Sst[d_].rearrange("p (h d) -> p h d", d=64)
            P.tt(S3, S3, q["dtot"].unsqueeze(2).to_broadcast([128, 6, 64]), ALU.mult)
            P.tt(Sst[d_], Sst[d_], pt[:, 0:384], ALU.add)
            is_end = (ci % 2 == 1) if d_ == 0 else (ci % 2 == 0)
            if is_end:
                seq = ci // 2
                pb2 = banks(1)
                for j in range(3):
                    P.tr(pbank(pb2)[:, j * 128:(j + 1) * 128], Sst[d_][:, j * 128:(j + 1) * 128], identf)
                sob = so[scnt[0] % 2]
                scnt[0] += 1
                P.copy(sob, pbank(pb2)[:, 0:384].rearrange("p (j n) -> p j n", n=128), eng="act")
                P.dma(nssm_o[l, seq, d_].rearrange("(j p) n -> p j n", p=128), sob)
                bnd = ci if d_ == 0 else ci - 1
                if 0 <= bnd <= 14:
                    P.ts(Sst[d_], Sst[d_], seqf[:, bnd:bnd + 1], None, ALU.mult)

        qb_ = {}
        order = list(reversed(range(16)))
        prep_common(order[0], order[0] % 2)
        qb_[order[0]] = prep_dir(order[0], order[0] % 2, 1, False)
        for oi, ci in enumerate(order):
            par = ci % 2
            if oi + 1 < 16:
                cn = order[oi + 1]
                prep_common(cn, cn % 2)
                qb_[cn] = prep_dir(cn, cn % 2, 1, False)
            P.copy(Sbe[:, ci, :], Sst[1], eng="pool")
            chunk_state_update(ci, par, 1, qb_.pop(ci))

        stA = {}

        def stage_A(ci):
            par = ci % 2
            t0 = ci * 128
            prep_common(ci, par)
            qs = [prep_dir(ci, par, 0, True), prep_dir(ci, par, 1, True)]
            pcb = banks(1)
            for g in range(2):
                P.mm(pbank(pcb)[:, g * 128:(g + 1) * 128], BT[:, g, t0:t0 + 128], CT[:, g, t0:t0 + 128])
            pcs = banks(3)
            pcsv = pbank(pcs, 3).rearrange("p (j l) -> p j l", l=128)
            for d_ in range(2):
                for h in range(6):
                    j = d_ * 6 + h
                    P.mm(pcsv[:, j, :], qs[d_]["a"][:, h:h + 1].to_broadcast([128, 128]), tri[d_],
                         start=True, stop=False, skip_group_check=True)
                    P.mm(pcsv[:, j, :], identb, mneg[d_], start=False, stop=True, skip_group_check=True)
            dec = dec_b[par]
            scb_ = sc_b[par]
            for d_ in range(2):
                for h in range(6):
                    j = d_ * 6 + h
                    P.act(dec[:, j, :], pcsv[:, j, :], AF.Exp, bias=qs[d_]["ncs"][:, h:h + 1])
            for d_ in range(2):
                for g in range(2):
                    j0 = d_ * 6 + g * 3
                    P.tt(scb_[:, j0:j0 + 3, :],
                         pbank(pcb)[:, g * 128:(g + 1) * 128].unsqueeze(1).to_broadcast([128, 3, 128]),
                         dec[:, j0:j0 + 3, :], ALU.mult)
            stA[ci] = qs

        def stage_B(ci):
            par = ci % 2
            t0 = ci * 128
            qs = stA.pop(ci)
            scb_ = sc_b[par]
            P.copy(Sent, Sst[0], eng="pool")
            pyd = banks(1)
            for h in range(6):
                for d_ in range(2):
                    j = d_ * 6 + h
                    P.mm(pbank(pyd)[:, h * 64:(h + 1) * 64], scb_[:, j, :], xdt_b[par][d_][:, h * 64:(h + 1) * 64],
                         start=(d_ == 0), stop=(d_ == 1), skip_group_check=True)
            pyo = [banks(1), banks(1)]
            for d_ in range(2):
                src = Sent if d_ == 0 else Sbe[:, ci, :]
                for g in range(2):
                    P.mm(pbank(pyo[d_])[:, g * 192:(g + 1) * 192], CT[:, g, t0:t0 + 128], src[:, g * 192:(g + 1) * 192])
            y0, y1, y2 = yc
            for d_, yy in ((0, y0), (1, y1)):
                P.tt(yy.rearrange("p (h d) -> p h d", d=64),
                     pbank(pyo[d_])[:, 0:384].rearrange("p (h d) -> p h d", d=64),
                     qs[d_]["ecs"].unsqueeze(2).to_broadcast([128, 6, 64]), ALU.mult)
            P.tt(y0, y0, y1, ALU.add)
            P.tt(y0, pbank(pyd)[:, 0:384], y0, ALU.add)
            P.tt(y1.rearrange("p (h d) -> p h d", d=64), xtok_b[par].rearrange("p (h d) -> p h d", d=64),
                 lp["drow"].unsqueeze(2).to_broadcast([128, 6, 64]), ALU.mult)
            P.tt(y0, y0, y1, ALU.add)
            dma_dbg("yraw%d" % ci, y0)
            P.tt(y0, y0, zg[:, ci, :], ALU.mult)
            P.tt(y2, y0, y0, ALU.mult)
            P.op("dve", lambda e, y2=y2: e.reduce_sum(rs[:, 0:1], y2, axis=AX.X), reads=[y2], writes=[rs[:, 0:1]])
            P.act(rs[:, 1:2], rs[:, 0:1], AF.Ln, bias=EPS, scale=1.0 / 384)
            P.act(rs[:, 1:2], rs[:, 1:2], AF.Exp, scale=-0.5)
            P.stt(ynb, y0, rs[:, 1:2], lp["snw"], ALU.mult, ALU.mult)
            pb = banks(1)
            pv = psb[:, pb * 1024:(pb + 1) * 1024]
            for j in range(3):
                P.tr(pv[:, j * 128:(j + 1) * 128], ynb[:, j * 128:(j + 1) * 128], identb)
            P.copy(yT[:, ci, :, :], pv[:, 0:384].rearrange("p (j t) -> p j t", t=128), eng="act")
            chunk_state_update(ci, par, 0, qs[0])

        stage_A(0)
        for ci in range(16):
            if ci + 1 < 16:
                stage_A(ci + 1)
            stage_B(ci)
        A.reset(mL)
        if stop_after == "M2" and l == 0:
            break

        KA = A.alloc([128, 2, NKEY], BF16)
        KB = A.alloc([128, 4, NKEY], BF16)
        VX = A.alloc([128, NKT, 576], BF16)
        mK = A.mark()
        P.memset(KA, 0.0, eng="pool")
        P.memset(KB, 0.0, eng="pool")
        P.dma(KA[64:73, :, :], mk_d[:, :].unsqueeze(1).to_broadcast([9, 2, NKEY]), q="pool")
        P.dma(KB[32:41, :, :], mk_d[:, :].unsqueeze(1).to_broadcast([9, 4, NKEY]), q="pool")
        P.dma(KB[96:105, :, :], mk_d[:, :].unsqueeze(1).to_broadcast([9, 4, NKEY]), q="pool")
        VX4 = VX.rearrange("p k (a s d) -> p k a s d", s=3, d=64)
        P.memset(VX4[:, :, :, 1, :], 1.0, eng="pool")
        hT = A.alloc([128, KD, 512], BF16)
        rope = [A.alloc([128, 512], F32) for _ in range(4)]
        sqb = A.alloc([128, 512], BF16)
        lnv = A.alloc([128, 512], F32)
        kn = A.alloc([128, 512], F32)
        t1 = A.alloc([128, 512], F32)
        kr = A.alloc([128, 512], F32)
        ktr = [A.alloc([128, 512], F32) for _ in range(2)]
        vst = [A.alloc([128, 384], F32) for _ in range(2)]
        cin = A.alloc([128, 2, 256], F32)
        ktc = [0]

        def rope_apply(src, dst, perm, cosT, sinT):
            pb = banks(1)
            P.mm(pbank(pb), perm, src)
            P.tt(t1, src, cosT, ALU.mult)
            P.tt(dst, pbank(pb), sinT, ALU.mult)
            P.tt(dst, dst, t1, ALU.add)

        def head_norm(pt, wcol, dst):
            P.act(sqb, pt, AF.Square)
            pb = banks(1)
            P.mm(pbank(pb), bones, sqb)
            P.act(lnv, pbank(pb), AF.Ln, bias=EPS, scale=1.0 / 64)
            P.act(lnv, lnv, AF.Exp, scale=-0.5)
            P.stt(dst, pt, wcol, lnv, ALU.mult, ALU.mult)

        def out_tok(src, dst_d, c0, ncol):
            pb = banks(1)
            for t4 in range(4):
                P.tr(pbank(pb)[:, t4 * 128:(t4 + 1) * 128], src[:, t4 * 128:(t4 + 1) * 128], identf)
            kt_ = ktr[ktc[0] % 2]
            ktc[0] += 1
            P.copy(kt_, pbank(pb), eng="act")
            P.dma(dst_d[c0:c0 + 512, ncol:ncol + 128].rearrange("(t p) f -> p t f", p=128),
                  kt_.rearrange("p (t f) -> p t f", f=128))

        for kt2 in range(2):
            P.dma(cin[:, 0, 0:128], cak_d[l, kt2 * 128:(kt2 + 1) * 128, :])
            pb = banks(1)
            P.tr(pbank(pb)[:, 0:128], cin[:, 0, 0:128], identf)
            kc = T + kt2 * 128
            P.copy(KA[0:64, 0, kc:kc + 128], pbank(pb)[0:64, 0:128])
            P.copy(KA[0:64, 1, kc:kc + 128], pbank(pb)[64:128, 0:128])
            P.dma(cin[:, 1, :], cbk_d[l, kt2 * 128:(kt2 + 1) * 128, :])
            pb = banks(1)
            for j in range(2):
                P.tr(pbank(pb)[:, j * 128:(j + 1) * 128], cin[:, 1, j * 128:(j + 1) * 128], identf)
            for h in range(4):
                j = h // 2
                r0 = (h % 2) * 64
                P.copy(KB[0:32, h, kc:kc + 128], pbank(pb)[r0:r0 + 32, j * 128:(j + 1) * 128])
                P.copy(KB[64:96, h, kc:kc + 128], pbank(pb)[r0 + 32:r0 + 64, j * 128:(j + 1) * 128])
            P.dma(VX4[:, 16 + kt2, 0, 0:3:2, :], cav_d[l, kt2 * 128:(kt2 + 1) * 128, :].rearrange("p (s d) -> p s d", d=64), q="pool")
            for a_ in range(2):
                P.dma(VX4[:, 16 + kt2, 1 + a_, 0:3:2, :],
                      cbv_d[l, kt2 * 128:(kt2 + 1) * 128, a_ * 128:(a_ + 1) * 128].rearrange("p (s d) -> p s d", d=64), q="pool")

        for c in range(4):
            c0 = c * 512
            adanorm(l, 0, c0, 512, hT)
            for i in (0, 2):
                P.dma(rope[i], rope_d[i][:, c0:c0 + 512])
                P.dma(rope[i + 1], rope_d[i + 1][:, c0:c0 + 512])

            def cons_k(f, pt, c0=c0):
                if f == 0:
                    head_norm(pt, lp["akn"][:, 0:1], kn)
                    rope_apply(kn, kr, permA, rope[0], rope[1])
                    P.copy(KA[0:64, 0, c0:c0 + 512], kr[0:64, :], eng="pool")
                    P.copy(KA[0:64, 1, c0:c0 + 512], kr[64:128, :], eng="pool")
                    out_tok(kr, nak_o[l], c0, 0)
                else:
                    P.copy(kn, pt, eng="act")
                    rope_apply(kn, kr, permB, rope[2], rope[3])
                    for hh in range(2):
                        h = (f - 1) * 2 + hh
                        r0 = hh * 64
                        P.copy(KB[0:32, h, c0:c0 + 512], kr[r0:r0 + 32, :], eng="dve")
                        P.copy(KB[64:96, h, c0:c0 + 512], kr[r0 + 32:r0 + 64, :], eng="dve")
                    out_tok(kr, nbk_o[l], c0, (f - 1) * 128)

            proj_feat(l, C_AK, 1, hT, 512, cons_k)
            proj_feat(l, C_BK, 2, hT, 512, lambda f, pt: cons_k(f + 1, pt))
            wv_ = [load_wslab(l, C_AV, 256), load_wslab(l, C_AV + 256, 128)]
            for t4 in range(4):
                pb = banks(1)
                pt = pbank(pb)
                for k in range(KD):
                    P.mm(pt[:, 0:256], hT[:, k, t4 * 128:(t4 + 1) * 128], wv_[0][:, k, 0:256],
                         start=(k == 0), stop=(k == KD - 1))
                for k in range(KD):
                    P.mm(pt[:, 256:384], hT[:, k, t4 * 128:(t4 + 1) * 128], wv_[1][:, k, 0:128],
                         start=(k == 0), stop=(k == KD - 1), skip_group_check=True)
                vs = vst[t4 % 2]
                P.copy(vs, pt[:, 0:384], eng="act")
                tk = c * 4 + t4
                P.dma(nav_o[l, tk * 128:(tk + 1) * 128, :], vs[:, 0:128])
                P.dma(nbv_o[l, tk * 128:(tk + 1) * 128, :], vs[:, 128:384])
                P.copy(VX4[:, tk, :, 0:3:2, :], vs.rearrange("p (a s d) -> p a s d", s=2, d=64), eng="pool")
        A.reset(mK)
        if stop_after == "M3" and l == 0:
            break

        hT = A.alloc([128, KD, 512], BF16)
        sqb = A.alloc([128, 512], BF16)
        lnv = A.alloc([128, 512], F32)
        QA = A.alloc([128, 6, 512], BF16)
        QB = A.alloc([128, 4, 512], BF16)
        mixT = A.alloc([128, 5, 512], BF16)
        bo = A.alloc([128, 512], F32)
        mOv = A.mark()
        rope = [A.alloc([128, 512], F32) for _ in range(4)]
        kn = A.alloc([128, 512], F32)
        t1 = A.alloc([128, 512], F32)
        kr = A.alloc([128, 512], F32)
        A.reset(mOv)
        PT = [A.alloc([128, 1024], BF16) for _ in range(3)]
        rz = A.alloc([128, 2, 512], F32)
        ob = A.alloc([128, 2, 512], F32)
        A.reset(mOv)
        P.memset(QA, 0.0, eng="pool")
        P.memset(QB, 0.0, eng="pool")
        sA = 1.0 / 8.0
        sB = 32.0 ** -0.5
        ptc = [0]
        for c in range(4):
            c0 = c * 512
            adanorm(l, 0, c0, 512, hT)
            for i in range(4):
                P.dma(rope[i], rope_d[i][:, c0:c0 + 512])
            P.dma(QA[64:73, :, :], mq_d[:, c0:c0 + 512].unsqueeze(1).to_broadcast([9, 6, 512]), q="pool")
            P.dma(QB[32:41, :, :], mq_d[:, c0:c0 + 512].unsqueeze(1).to_broadcast([9, 4, 512]), q="pool")
            P.dma(QB[96:105, :, :], mq_d[:, c0:c0 + 512].unsqueeze(1).to_broadcast([9, 4, 512]), q="pool")

            def cons_qa(f, pt):
                head_norm(pt, lp["aqn"][:, 0:1], kn)
                rope_apply(kn, kr, permA, rope[0], rope[1])
                P.copy(QA[0:64, f, :], kr[0:64, :], eng="pool")
                P.copy(QA[0:64, f + 3, :], kr[64:128, :], eng="pool")

            def cons_qb(f, pt):
                P.copy(kn, pt, eng="act")
                rope_apply(kn, kr, permB, rope[2], rope[3])
                for hh in range(2):
                    h = f * 2 + hh
                    r0 = hh * 64
                    P.copy(QB[0:32, h, :], kr[r0:r0 + 32, :], eng="dve")
                    P.copy(QB[64:96, h, :], kr[r0 + 32:r0 + 64, :], eng="dve")

            proj_feat(l, C_AQ, 3, hT, 512, cons_qa)
            proj_feat(l, C_BQ, 2, hT, 512, cons_qb)

            groups = [("A", f) for f in range(3)] + [("B", h) for h in range(4)]
            def group_units(kind, idx):
                if kind == "A":
                    return ([(KA[0:73, 0, :], QA[0:73, idx, :], 0, 128),
                             (KA[0:73, 1, :], QA[0:73, idx + 3, :], 64, 192)], sA)
                h = idx
                vc = (1 + h // 2) * 192 + (0 if h % 2 == 0 else 64)
                return ([(KB[0:41, h, :], QB[0:41, h, :], vc, vc + 128),
                         (KB[64:105, h, :], QB[64:105, h, :], vc, vc + 128)], sB)

            gunits = [group_units(k_, i_) for (k_, i_) in groups]
            steps = [(gi, kt) for gi in range(len(groups)) for kt in range(NKT)]
            pts = {}
            pending = []

            def emit_S(si):
                gi, kt = steps[si]
                units, scl = gunits[gi]
                sb_ = 2 * (si % 2)
                for u, (Kt, Qt, v0, v1) in enumerate(units):
                    P.mm(pbank(sb_ + u), Kt[:, kt * 128:(kt + 1) * 128], Qt)
                pt_ = PT[si % 3]
                pts[si] = pt_
                P.act(pt_, pbank(sb_, 2), AF.Exp, scale=scl)

            def emit_AV(si):
                gi, kt = steps[si]
                units, scl = gunits[gi]
                ob_ = 4 + 2 * (gi % 2)
                pt_ = pts.pop(si)
                for u, (Kt, Qt, v0, v1) in enumerate(units):
                    P.mm(pbank(ob_ + u), VX[:, kt, v0:v1], pt_[:, u * 512:(u + 1) * 512],
                         start=(kt == 0), stop=(kt == NKT - 1))
                if kt == NKT - 1:
                    evac(gi, si)

            def evac(gi, si):
                kind, idx = groups[gi]
                ob_ = 4 + 2 * (gi % 2)
                if kind == "A":
                    O0 = pbank(ob_)
                    O1 = pbank(ob_ + 1)
                    P.op("dve", lambda e, O0=O0: e.reciprocal(rz[0:64, 0, :], O0[64:128, :]),
                         reads=[O0[64:128, :]], writes=[rz[0:64, 0, :]])
                    P.tt(mixT[0:64, idx, :], O0[0:64, :], rz[0:64, 0, :], ALU.mult)
                    P.op("dve", lambda e, O1=O1: e.reciprocal(rz[64:128, 1, :], O1[0:64, :]),
                         reads=[O1[0:64, :]], writes=[rz[64:128, 1, :]])
                    P.tt(mixT[64:128, idx, :], O1[64:128, :], rz[64:128, 1, :], ALU.mult)
                    return
                h = idx
                r0 = 0 if h % 2 == 0 else 64
                z0 = 64 - r0
                for u in range(2):
                    Ou = pbank(ob_ + u)
                    P.op("dve", lambda e, Ou=Ou, u=u, r0=r0, z0=z0: e.reciprocal(rz[r0:r0 + 64, u, :], Ou[z0:z0 + 64, :]),
                         reads=[Ou[z0:z0 + 64, :]], writes=[rz[r0:r0 + 64, u, :]])
                    P.tt(ob[r0:r0 + 64, u, :], Ou[r0:r0 + 64, :], rz[r0:r0 + 64, u, :], ALU.mult)
                P.stt(bo[r0:r0 + 64, :], ob[r0:r0 + 64, 1, :], lp["nlam"][r0:r0 + 64, 0:1], ob[r0:r0 + 64, 0, :],
                      ALU.mult, ALU.add)
                if h % 2 == 1:
                    P.act(sqb, bo, AF.Square)

                    def subln(h=h):
                        pb = 2 * (len(pts) % 2)
                        P.mm(pbank(pb), bones, sqb)
                        P.act(lnv, pbank(pb), AF.Ln, bias=EPS, scale=1.0 / 64)
                        P.act(lnv, lnv, AF.Exp, scale=-0.5)
                        P.stt(mixT[:, 3 + h // 2, :], bo, lp["bsub"][:, 0:1], lnv, ALU.mult, ALU.mult)
                    pending.append((si + 3, subln))

            NS = len(steps)
            emit_S(0)
            for si in range(1, NS):
                emit_S(si)
                emit_AV(si - 1)
                for item in list(pending):
                    if item[0] <= si:
                        pending.remove(item)
                        item[1]()
            emit_AV(NS - 1)
            for item in list(pending):
                pending.remove(item)
                item[1]()
            if ("mixT%d" % c) in dbg_o:
                tmpf = A.alloc([128, 5, 512], F32)
                P.copy(tmpf, mixT)
                P.dma(dbg_o["mixT%d" % c], tmpf)
            wov = w_out_d[l].rearrange("(k p) n -> p k n", p=128)
            for dp in range(4):
                wb = next_wsl()
                P.dma(wb[:, :, :], wov[:, :, dp * 256:(dp + 1) * 256], q="pool")
                for dd in range(2):
                    dt_ = dp * 2 + dd
                    pb = banks(1)
                    pt = pbank(pb)
                    for k in range(KD):
                        if k < 5:
                            rhs = mixT[:, k, :]
                        else:
                            rhs = yT[:, c * 4:(c + 1) * 4, k - 5, :]
                        P.mm(pt, wb[:, k, dd * 128:(dd + 1) * 128], rhs, start=(k == 0), stop=(k == KD - 1))
                    P.stt(xT[:, dt_, c0:c0 + 512], pt, mod[l][:, 16 + dt_:17 + dt_], xT[:, dt_, c0:c0 + 512],
                          ALU.mult, ALU.add)
        A.reset(mL)
        if stop_after == "M4" and l == 0:
            break

        A.reset(mLP)
        hT = A.alloc([128, KD, 1026], BF16)
        P.memset(hT, 0.0, eng="pool")
        hkeep = A.alloc([128, KD, 1], BF16)
        hmid = A.alloc([128, NFF, 1024], BF16)
        wd = [A.alloc([128, NFF, 128], BF16) for _ in range(2)]
        mOv = A.mark()
        Ub = [[A.alloc([128, 1026], F32) for _ in range(2)] for _ in range(2)]
        ab = [[A.alloc([128, 1024], F32) for _ in range(2)] for _ in range(2)]
        A.reset(mOv)
        upv = ffn_up_d[l].rearrange("(k p) n -> p k n", p=128)
        dnv = ffn_down_d[l].rearrange("(k p) n -> p k n", p=128)
        fc = [0]
        for hf in range(2):
            h0 = hf * 1024
            if hf == 0:
                adanorm(l, 1, 0, 1025, hT, col0=1)
                P.copy(hkeep, hT[:, :, 1024:1025], eng="pool")
            else:
                adanorm(l, 1, 1024, 1024, hT, col0=1)
                P.copy(hT[:, :, 0:1], hkeep, eng="pool")
            pend_f = []
            for s in range(11):
                wg = next_wsl()
                P.dma(wg[:, :, :], upv[:, :, s * 256:(s + 1) * 256], q="pool")
                wvv = next_wsl()
                P.dma(wvv[:, :, :], upv[:, :, D_FF + s * 256:D_FF + (s + 1) * 256], q="pool")
                for j in range(2):
                    i = s * 2 + j
                    par = i % 2
                    pts_ = []
                    for which, wb in ((0, wg), (1, wvv)):
                        pb = banks(3)
                        pt = pbank(pb, 3)
                        for (j0, w) in colgroups(1026):
                            for k in range(KD):
                                P.mm(pt[:, j0:j0 + w], wb[:, k, j * 128:(j + 1) * 128], hT[:, k, j0:j0 + w],
                                     start=(k == 0), stop=(k == KD - 1))
                        pts_.append(pt)
                    for which in range(2):
                        ft = i + which * NFF
                        conv_evac(Ub[par][which], ab[par][which], 1024, lp["fcw"][:, ft, :], lp["fcb"][:, ft:ft + 1], pts_[which])
                    for fn_ in pend_f:
                        fn_()
                    del pend_f[:]
                    for which in range(2):
                        ft = i + which * NFF
                        conv_tile(Ub[par][which], ab[par][which], 1024, lp["fcw"][:, ft, :], lp["fcb"][:, ft:ft + 1],
                                  lp["nfcw0"][:, ft:ft + 1], lp["nfcw2"][:, ft:ft + 1], h0, pts_[which], evac=False)

                    def fin(i=i, par=par):
                        P.act(ab[par][0], ab[par][0], AF.Silu)
                        P.tt(hmid[:, i, :], ab[par][1], ab[par][0], ALU.mult)
                    pend_f.append(fin)
            for fn_ in pend_f:
                fn_()
            del pend_f[:]
            for dt_ in range(8):
                wdb = wd[dt_ % 2]
                P.dma(wdb[:, :, :], dnv[:, :, dt_ * 128:(dt_ + 1) * 128], q="pool")
                for nch in range(2):
                    pb = banks(1)
                    pt = pbank(pb)
                    for i in range(NFF):
                        P.mm(pt, wdb[:, i, :], hmid[:, i, nch * 512:(nch + 1) * 512],
                             start=(i == 0), stop=(i == NFF - 1))
                    tsl = slice(h0 + nch * 512, h0 + (nch + 1) * 512)
                    P.stt(xT[:, dt_, tsl], pt, mod[l][:, 40 + dt_:41 + dt_], xT[:, dt_, tsl], ALU.mult, ALU.add)
        A.reset(mL)
        if stop_after == "F0" and l == 0:
            break

    A.reset(0)
    if stop_after is None or stop_after == "FIN":
        sq = [A.alloc([128, 512], BF16) for _ in range(2)]
        lnv = A.alloc([128, 512], F32)
        yn = A.alloc([128, KD, 512], F32)
        yo = [A.alloc([128, D], F32) for _ in range(2)]
        for c in range(4):
            c0 = c * 512
            pb = banks(1)
            for k in range(KD):
                P.act(sq[k % 2], xT[:, k, c0:c0 + 512], AF.Square)
                P.mm(pbank(pb), onesb, sq[k % 2], start=(k == 0), stop=(k == KD - 1))
            P.act(lnv, pbank(pb), AF.Ln, bias=EPS, scale=1.0 / D)
            P.act(lnv, lnv, AF.Exp, scale=-0.5)
            for k in range(KD):
                P.stt(yn[:, k, :], xT[:, k, c0:c0 + 512], fnw[:, k:k + 1], lnv, ALU.mult, ALU.mult)
            for t4 in range(4):
                yb = yo[t4 % 2]
                for hb in range(2):
                    pb2 = banks(1)
                    for k4 in range(4):
                        k = hb * 4 + k4
                        P.tr(pbank(pb2)[:, k4 * 128:(k4 + 1) * 128], yn[:, k, t4 * 128:(t4 + 1) * 128], identf)
                    P.copy(yb[:, hb * 512:(hb + 1) * 512], pbank(pb2), eng=("act" if hb else "dve"))
                tk = c * 4 + t4
                P.dma(y_o[tk * 128:(tk + 1) * 128, :], yb)
    else:
        pass
    if "xT" in dbg_o:
        P.dma(dbg_o["xT"], xT[:, :, :])
    P.finalize()
    P.arena_peak = A.peak
    return nc, P


def _rope_tables(L, d, grid_w=64, theta=10000.0):
    rows = L // grid_w
    row = np.repeat(np.arange(rows), grid_w).astype(np.float32)
    col = np.tile(np.arange(grid_w), rows).astype(np.float32)
    quarter = d // 4
    inv = (np.float32(theta) ** (-np.arange(quarter, dtype=np.float32) / np.float32(quarter))).astype(np.float32)
    ang_r = row[:, None] * inv[None, :]
    ang_c = col[:, None] * inv[None, :]
    ang = np.concatenate([ang_r, ang_r, ang_c, ang_c], axis=-1)
    return np.cos(ang).astype(np.float32), np.sin(ang).astype(np.float32)


def _perm_mat(d):
    q = d // 4
    Pm = np.zeros((128, 128), np.float32)
    for b0 in range(0, 128, d):
        for i in range(q):
            Pm[b0 + q + i, b0 + i] = -1.0
            Pm[b0 + i, b0 + q + i] = 1.0
            Pm[b0 + 3 * q + i, b0 + 2 * q + i] = -1.0
            Pm[b0 + 2 * q + i, b0 + 3 * q + i] = 1.0
    return Pm


def _consts():
    r = np.arange(128)
    c = {}
    c["identf"] = np.eye(128, dtype=np.float32)
    c["trif"] = (r[:, None] <= r[None, :]).astype(np.float32)
    c["trib"] = (r[:, None] >= r[None, :]).astype(np.float32)
    c["mnegf"] = np.where(r[None, :] < r[:, None], NEG, 0.0).astype(np.float32)
    c["mnegb"] = np.where(r[None, :] > r[:, None], NEG, 0.0).astype(np.float32)
    c["permA"] = _perm_mat(64)
    c["permB"] = _perm_mat(32)
    bo = np.zeros((128, 128), np.float32)
    bo[:64, :64] = 1.0
    bo[64:, 64:] = 1.0
    c["bones"] = bo
    return c


def _shared_weights(inp):
    f = lambda a: np.ascontiguousarray(np.asarray(a, dtype=np.float32))
    w = {}
    w["ada_w"] = f(inp["ada_w"])
    w["ada_b"] = f(np.asarray(inp["ada_b"]).reshape(NL, 48, 128).transpose(0, 2, 1))
    win = np.asarray(inp["w_in"], dtype=np.float32)
    o = np.cumsum([0, 384, 128, 128, 256, 256, 256, 384, 384, 256, 256, 12])
    aq, ak, av, bq, bk, bv, cx, cz, cB, cC, cdt = [win[:, :, o[i]:o[i + 1]] for i in range(11)]
    aqh = aq.reshape(NL, D, 6, 64)
    aqp = aqh[:, :, [0, 3, 1, 4, 2, 5], :].reshape(NL, D, 384)
    w["w_in"] = f(np.concatenate([aqp, ak, bq, bk, cx, cB, cC, av, bv, cz, cdt], axis=-1))
    tile2 = lambda v: f(np.concatenate([v, v], axis=-1)[:, :, None])
    w["aqn"] = tile2(np.asarray(inp["a_q_norm"]))
    w["akn"] = tile2(np.asarray(inp["a_k_norm"]))
    w["blam"] = f(np.asarray(inp["b_lambda"]).reshape(NL, 128))
    w["bsub"] = tile2(np.asarray(inp["b_subln"]))
    w["scw"] = f(np.asarray(inp["ssm_conv_w"]).reshape(NL, 3, 7, 128).transpose(0, 3, 2, 1))
    w["scb"] = f(np.asarray(inp["ssm_conv_b"]).reshape(NL, 7, 128).transpose(0, 2, 1))
    w["alog"] = f(np.asarray(inp["ssm_A_log"]).reshape(NL, 12))
    w["dtb"] = f(np.asarray(inp["ssm_dt_bias"]).reshape(NL, 12))
    w["ssmD"] = f(inp["ssm_D"])
    w["snw"] = f(inp["ssm_norm_w"])
    wo = np.asarray(inp["w_out"], dtype=np.float32)
    woa = wo[:, :384, :].reshape(NL, 6, 64, D)[:, [0, 3, 1, 4, 2, 5]].reshape(NL, 384, D)
    w["w_out"] = f(np.concatenate([woa, wo[:, 384:, :]], axis=1))
    w["ffn_up"] = f(inp["ffn_up"])
    w["fcw"] = f(np.asarray(inp["ffn_conv_w"]).reshape(NL, 3, 44, 128).transpose(0, 3, 2, 1))
    w["fcb"] = f(np.asarray(inp["ffn_conv_b"]).reshape(NL, 44, 128).transpose(0, 2, 1))
    w["ffn_down"] = f(inp["ffn_down"])
    w["fnw"] = f(np.asarray(inp["final_norm_w"]).reshape(8, 128).T)
    return w


def _core_inputs(inp, core, shared, consts, ropeS):
    f = lambda a: np.ascontiguousarray(np.asarray(a, dtype=np.float32))
    m = dict(shared)
    m.update(consts)
    is_prompt = core >= 4
    if not is_prompt:
        b = core
        m["x"] = f(inp["x_sample"][b])
        cvec = np.asarray(inp["c"])[b]
        m["cak"] = f(np.asarray(inp["cache_a_k"])[b].reshape(NL, 256, 128))
        m["cav"] = f(np.asarray(inp["cache_a_v"])[b].reshape(NL, 256, 128))
        m["cbk"] = f(np.asarray(inp["cache_b_k"])[b].reshape(NL, 256, 256))
        m["cbv"] = f(np.asarray(inp["cache_b_v"])[b].reshape(NL, 256, 256))
        m["ssm0"] = f(np.asarray(inp["state_ssm"])[b].reshape(NL, 2, 384, 128))
        mq = np.zeros((9, T), np.float32)
        mq[8] = 1.0
        mk = np.zeros((9, NKEY), np.float32)
        m["cosA"], m["sinA"], m["cosB"], m["sinB"] = ropeS
        m["pflag"] = np.zeros((128, 1), np.float32)
        m["seqf"] = np.ones((128, 16), np.float32)
    else:
        j = core - 4
        m["x"] = f(np.asarray(inp["x_prompt"])[8 * j:8 * j + 8].reshape(T, D))
        cvec = np.asarray(inp["c_ctx"])
        m["cak"] = np.zeros((NL, 256, 128), np.float32)
        m["cav"] = np.zeros((NL, 256, 128), np.float32)
        m["cbk"] = np.zeros((NL, 256, 256), np.float32)
        m["cbv"] = np.zeros((NL, 256, 256), np.float32)
        m["ssm0"] = np.zeros((NL, 2, 384, 128), np.float32)
        seq_q = np.arange(T) // 256
        mq = np.zeros((9, T), np.float32)
        mq[seq_q, np.arange(T)] = 1.0
        mq[8] = 1.0
        mk = np.zeros((9, NKEY), np.float32)
        mk[seq_q, np.arange(T)] = MASKV
        mk[8] = -MASKV
        one = np.ones((128, T), np.float32)
        zero = np.zeros((128, T), np.float32)
        m["cosA"], m["sinA"], m["cosB"], m["sinB"] = one, zero, one, zero
        m["pflag"] = np.ones((128, 1), np.float32)
        sf = np.ones((128, 16), np.float32)
        sf[:, 1::2] = 0.0
        m["seqf"] = sf
    m["cvec"] = f(np.asarray(cvec).reshape(8, 128).T)
    m["mq"] = mq
    m["mk"] = mk
    return m


def make_in_maps(inp):
    shared = _shared_weights(inp)
    consts = _consts()
    cA, sA_ = _rope_tables(T, 64)
    cB, sB_ = _rope_tables(T, 32)
    tA = lambda a: np.ascontiguousarray(np.tile(a.T, (2, 1)))
    tB = lambda a: np.ascontiguousarray(np.tile(a.T, (4, 1)))
    ropeS = (tA(cA), tA(sA_), tB(cB), tB(sB_))
    return [_core_inputs(inp, core, shared, consts, ropeS) for core in range(8)]


_NC_CACHE = {}


def kernel(**inputs):
    in_maps = make_in_maps(inputs)
    if "nc" not in _NC_CACHE:
        _NC_CACHE["nc"] = build()[0]
    nc = _NC_CACHE["nc"]
    res = run_bass_kernel_spmd(nc, in_maps, core_ids=list(range(8)))
    r = res.results
    B = 32
    y_sample = np.stack([r[b]["y"] for b in range(4)], axis=0).astype(np.float32)
    y_prompt = np.concatenate([r[4 + j]["y"].reshape(8, 256, D) for j in range(4)], axis=0).astype(np.float32)

    def gather(name, tail):
        parts = []
        for j in range(4):
            a = r[4 + j][name]
            a = a.reshape(NL, 8, 256, -1).transpose(1, 0, 2, 3)
            parts.append(a)
        a = np.concatenate(parts, axis=0)
        return np.ascontiguousarray(a.reshape((B, NL, 256) + tail)).astype(np.float32)

    new_a_k = gather("nak", (2, 64))
    new_a_v = gather("nav", (2, 64))
    new_b_k = gather("nbk", (4, 2, 32))
    new_b_v = gather("nbv", (4, 64))
    parts = []
    for j in range(4):
        a = r[4 + j]["nssm"]
        parts.append(a.transpose(1, 0, 2, 3, 4))
    new_ssm = np.ascontiguousarray(np.concatenate(parts, axis=0).reshape(B, NL, 2, 6, 64, 128)).astype(np.float32)
    return (y_prompt, y_sample, new_a_k, new_a_v, new_b_k, new_b_v, new_ssm)
```

```python
import math
import numpy as np
import concourse.bass as bass
import concourse.mybir as mybir
from concourse.bass_utils import run_bass_kernel_spmd

F32 = mybir.dt.float32
BF16 = mybir.dt.bfloat16
AF = mybir.ActivationFunctionType
ALU = mybir.AluOpType
AX = mybir.AxisListType
_DSZ = {F32: 4, BF16: 2}


def dsz(dt):
    return _DSZ[dt]


class _Op:
    __slots__ = ("eng", "fn", "deps", "gidx", "idx", "is_dma", "defer", "sig", "tok", "dsem", "dval")

    def __init__(self, eng, fn, deps, gidx, idx, is_dma, defer):
        self.eng = eng
        self.fn = fn
        self.deps = deps
        self.gidx = gidx
        self.idx = idx
        self.is_dma = is_dma
        self.defer = defer
        self.sig = False
        self.tok = None
        self.dsem = None
        self.dval = None


class Prog:
    ENGS = ("pe", "act", "dve", "pool", "sp")
    NDS = 8

    def __init__(self, nc):
        self.nc = nc
        self.ops = {e: [] for e in self.ENGS}
        self.all_ops = []
        self.last_w = {}
        self.readers = {}
        self.chunk = {}
        self.rowbytes = {}
        self._cm = []
        self.psum_names = set()

    def sbuf(self, name, shape, dtype, chunk=None):
        g = self.nc.sbuf_tensor(name, list(shape), dtype)
        t = g.__enter__()
        self._cm.append(g)
        rb = int(np.prod(shape[1:])) * dsz(dtype)
        self.rowbytes[name] = rb
        self.chunk[name] = chunk if chunk else rb
        return t

    def psum(self, name, shape, dtype=F32):
        g = self.nc.psum_tensor(name, list(shape), dtype)
        t = g.__enter__()
        self._cm.append(g)
        rb = int(np.prod(shape[1:])) * dsz(dtype)
        self.rowbytes[name] = rb
        self.chunk[name] = 2048
        self.psum_names.add(name)
        return t

    def res(self, ap):
        sp = str(ap.space)
        if "DRAM" in sp.upper():
            return []
        name = ap.tensor.name
        rb = self.rowbytes[name]
        es = dsz(ap.dtype)
        ch = self.chunk[name]
        base = (int(ap.offset) * es) % rb
        dims = [(abs(st), cnt) for (st, cnt) in ap.ap[1:] if cnt > 1]
        if not dims:
            return [(name, base // ch)]
        dims.sort()
        inner_st, inner_cnt = dims[0]
        outer = dims[1:]
        nout = 1
        for (_, c) in outer:
            nout *= c
        keys = set()
        if nout > 256:
            span = sum((c - 1) * s for (s, c) in dims)
            lo, hi = base, base + (span + 1) * es
            return [(name, c) for c in range(lo // ch, (hi - 1) // ch + 1)]
        offs = [0]
        for (s, c) in outer:
            offs = [o + i * s for o in offs for i in range(c)]
        ispan = ((inner_cnt - 1) * inner_st + 1) * es
        for o in offs:
            lo = base + o * es
            hi = lo + ispan
            for c in range(lo // ch, (hi - 1) // ch + 1):
                keys.add((name, c))
        return list(keys)

    limit = None

    def op(self, eng, fn, reads=(), writes=(), is_dma=False, defer=False):
        if self.limit is not None and len(self.all_ops) >= self.limit:
            return None
        rk = set()
        for a in reads:
            rk.update(self.res(a))
        wk = set()
        for a in writes:
            wk.update(self.res(a))
        deps = set()
        for r in rk:
            w = self.last_w.get(r)
            if w is not None:
                deps.add(w)
            if r[0] in self.psum_names:
                for t in self.readers.get(r, ()):
                    if t.eng != eng:
                        deps.add(t)
        for r in wk:
            w = self.last_w.get(r)
            if w is not None:
                deps.add(w)
            for t in self.readers.get(r, ()):
                deps.add(t)
        o = _Op(eng, fn, deps, len(self.all_ops), len(self.ops[eng]), is_dma, defer)
        self.ops[eng].append(o)
        self.all_ops.append(o)
        for r in rk:
            if r not in wk:
                self.readers.setdefault(r, []).append(o)
        for r in wk:
            self.last_w[r] = o
            self.readers[r] = []
        return o

    def mm(self, out, lhsT, rhs, start=True, stop=True, defer=None, **kw):
        if defer is None:
            defer = not stop
        return self.op("pe", lambda e: e.matmul(out, lhsT, rhs, start=start, stop=stop, **kw),
                       reads=[lhsT, rhs], writes=[out], defer=defer)

    def tr(self, out, in_, ident, defer=False):
        return self.op("pe", lambda e: e.transpose(out, in_, ident), reads=[in_, ident], writes=[out], defer=defer)

    def act(self, out, in_, func, bias=None, scale=None):
        kw = {}
        reads = [in_]
        if bias is not None:
            kw["bias"] = bias
            if not isinstance(bias, (int, float)):
                reads.append(bias)
        if scale is not None:
            kw["scale"] = scale
            if not isinstance(scale, (int, float)):
                reads.append(scale)
        return self.op("act", lambda e: e.activation(out, in_, func, **kw), reads=reads, writes=[out])

    def tt(self, out, in0, in1, op, eng="dve"):
        return self.op(eng, lambda e: e.tensor_tensor(out, in0, in1, op), reads=[in0, in1], writes=[out])

    def ts(self, out, in0, s1, s2, op0, op1=None, eng="dve"):
        reads = [in0]
        for s in (s1, s2):
            if s is not None and not isinstance(s, (int, float)):
                reads.append(s)
        if op1 is None:
            return self.op(eng, lambda e: e.tensor_scalar(out, in0, s1, s2, op0), reads=reads, writes=[out])
        return self.op(eng, lambda e: e.tensor_scalar(out, in0, s1, s2, op0, op1), reads=reads, writes=[out])

    def stt(self, out, in0, scalar, in1, op0, op1, eng="dve"):
        reads = [in0, in1]
        if not isinstance(scalar, (int, float)):
            reads.append(scalar)
        return self.op(eng, lambda e: e.scalar_tensor_tensor(out, in0, scalar, in1, op0, op1),
                       reads=reads, writes=[out])

    def copy(self, out, in_, eng="dve"):
        if eng == "act":
            return self.op("act", lambda e: e.copy(out, in_), reads=[in_], writes=[out])
        return self.op(eng, lambda e: e.tensor_copy(out, in_), reads=[in_], writes=[out])

    def memset(self, ap, val, eng="pool"):
        return self.op(eng, lambda e: e.memset(ap, val), writes=[ap])

    def dma(self, out, in_, q="sp"):
        return self.op(q, lambda e: e.dma_start(out=out, in_=in_), reads=[in_], writes=[out], is_dma=True)

    def finalize(self):
        nc = self.nc
        nxt = {}
        for e in self.ENGS:
            lst = self.ops[e]
            cur = None
            for k in range(len(lst) - 1, -1, -1):
                o = lst[k]
                if not o.is_dma and not o.defer:
                    cur = o
                nxt[o] = cur
        resolve = {}
        for o in self.all_ops:
            for d in o.deps:
                if d.is_dma or (o.eng == "pe" and d.eng == "pe"):
                    continue
                t = d
                if d.defer:
                    t2 = nxt[d]
                    if t2 is not None and t2.gidx < o.gidx:
                        t = t2
                resolve[(d, o)] = t
                t.sig = True
        sems = {e: nc.alloc_semaphore(name="c_" + e) for e in self.ENGS}
        dsems = {}
        for e in self.ENGS:
            if any(o.is_dma for o in self.ops[e]):
                dsems[e] = [nc.alloc_semaphore(name="d_%s_%d" % (e, i)) for i in range(self.NDS)]
        for e in self.ENGS:
            c = 0
            j = 0
            for o in self.ops[e]:
                if o.is_dma:
                    o.dsem = dsems[e][j % self.NDS]
                    o.dval = 16 * (j // self.NDS + 1)
                    j += 1
                elif o.sig:
                    c += 1
                    o.tok = c
        self.sig_counts = {e: sum(1 for o in self.ops[e] if o.sig) for e in self.ENGS}
        progs = self

        def emit(e, eng):
            waited = {}
            lst = progs.ops[e]
            for o in lst:
                need = {}
                for d in o.deps:
                    if e == "pe" and d.eng == "pe":
                        continue
                    if d.is_dma:
                        key, val = d.dsem, d.dval
                    else:
                        t = resolve[(d, o)]
                        key, val = sems[t.eng], t.tok
                    if need.get(key, 0) < val:
                        need[key] = val
                if o.is_dma and o.dval > 16:
                    key, val = o.dsem, o.dval - 16
                    if need.get(key, 0) < val:
                        need[key] = val
                for key, val in need.items():
                    if waited.get(key, 0) < val:
                        eng.wait_ge(key, val)
                        waited[key] = val
                ins = o.fn(eng)
                if o.is_dma:
                    ins.then_inc(o.dsem, 16)
                elif o.sig:
                    ins.then_inc(sems[e], 1)
            if e in dsems:
                last = {}
                for o in lst:
                    if o.is_dma:
                        last[o.dsem] = o.dval
                for key, val in last.items():
                    if waited.get(key, 0) < val:
                        eng.wait_ge(key, val)

        with nc.Block() as block:
            if self.ops["pe"]:
                @block.tensor
                def _(eng):
                    emit("pe", eng)
            if self.ops["act"]:
                @block.scalar
                def _(eng):
                    emit("act", eng)
            if self.ops["dve"]:
                @block.vector
                def _(eng):
                    emit("dve", eng)
            if self.ops["pool"]:
                @block.gpsimd
                def _(eng):
                    emit("pool", eng)
            if self.ops["sp"]:
                @block.sync
                def _(eng):
                    emit("sp", eng)
        return nc


T = 2048
D = 1024
KD = 8
NL = 2
NKEY = 2304
NKT = 18
EPS = 1e-6
D_FF = 2816
NFF = 22
C_AQ, C_AK, C_BQ, C_BK, C_CX, C_CB, C_CC = 0, 384, 512, 768, 1024, 1408, 1664
C_AV, C_BV, C_CZ, C_DT = 1920, 2048, 2304, 2688
MASKV = 2048.0
NEG = -30000.0


class Arena:
    def __init__(self, P, name, nbytes):
        self.P = P
        self.t = P.sbuf(name, [128, nbytes // 4], F32, chunk=256)
        self.n = nbytes
        self.ptr = 0
        self.peak = 0

    def mark(self):
        return self.ptr

    def reset(self, m=0):
        self.ptr = m

    def alloc(self, shape, dtype):
        nb = int(np.prod(shape[1:])) * dsz(dtype)
        nb = (nb + 255) // 256 * 256
        off = self.ptr
        self.ptr += nb
        self.peak = max(self.peak, self.ptr)
        assert self.ptr <= self.n, "arena overflow %d > %d" % (self.ptr, self.n)
        n_el = int(np.prod(shape[1:]))
        v = self.t[:, off // 4:(off + nb) // 4]
        if dtype != F32:
            v = v.bitcast(dtype)
        v = v[:, 0:n_el]
        if len(shape) == 3:
            v = v.rearrange("p (a b) -> p a b", b=shape[2])
        elif len(shape) == 4:
            v = v.rearrange("p (a b c) -> p a b c", b=shape[2], c=shape[3])
        return v


def colgroups(n):
    out = []
    j = 0
    while j < n:
        w = min(512, n - j)
        out.append((j, w))
        j += w
    return out


def build(stop_after=None, dbg=None, limit=None):
    dbg = dbg or {}
    nc = bass.Bass("TRN2", target_bir_lowering=False)
    P = Prog(nc)
    P.limit = limit

    def din(name, shape):
        return nc.dram_tensor(name, list(shape), F32, kind="ExternalInput").ap()

    def dout(name, shape):
        return nc.dram_tensor(name, list(shape), F32, kind="ExternalOutput").ap()

    x_d = din("x", [T, D])
    cvec_d = din("cvec", [128, 8])
    cak_d = din("cak", [NL, 256, 128])
    cav_d = din("cav", [NL, 256, 128])
    cbk_d = din("cbk", [NL, 256, 256])
    cbv_d = din("cbv", [NL, 256, 256])
    ssm0_d = din("ssm0", [NL, 2, 384, 128])
    mq_d = din("mq", [9, T])
    mk_d = din("mk", [9, NKEY])
    rope_d = [din(n, [128, T]) for n in ("cosA", "sinA", "cosB", "sinB")]
    pflag_d = din("pflag", [128, 1])
    seqf_d = din("seqf", [128, 16])
    identf_d = din("identf", [128, 128])
    trif_d = din("trif", [128, 128])
    trib_d = din("trib", [128, 128])
    mnegf_d = din("mnegf", [128, 128])
    mnegb_d = din("mnegb", [128, 128])
    permA_d = din("permA", [128, 128])
    permB_d = din("permB", [128, 128])
    bones_d = din("bones", [128, 128])
    ada_w_d = din("ada_w", [NL, D, 6 * D])
    ada_b_d = din("ada_b", [NL, 128, 48])
    w_in_d = din("w_in", [NL, D, 2700])
    aqn_d = din("aqn", [NL, 128, 1])
    akn_d = din("akn", [NL, 128, 1])
    blam_d = din("blam", [NL, 128])
    bsub_d = din("bsub", [NL, 128, 1])
    scw_d = din("scw", [NL, 128, 7, 3])
    scb_d = din("scb", [NL, 128, 7])
    alog_d = din("alog", [NL, 12])
    dtb_d = din("dtb", [NL, 12])
    ssmD_d = din("ssmD", [NL, 6])
    snw_d = din("snw", [NL, 384])
    w_out_d = din("w_out", [NL, D, D])
    ffn_up_d = din("ffn_up", [NL, D, 2 * D_FF])
    fcw_d = din("fcw", [NL, 128, 44, 3])
    fcb_d = din("fcb", [NL, 128, 44])
    ffn_down_d = din("ffn_down", [NL, D_FF, D])
    fnw_d = din("fnw", [128, 8])

    y_o = dout("y", [T, D])
    nak_o = dout("nak", [NL, T, 128])
    nav_o = dout("nav", [NL, T, 128])
    nbk_o = dout("nbk", [NL, T, 256])
    nbv_o = dout("nbv", [NL, T, 256])
    nssm_o = dout("nssm", [NL, 8, 2, 384, 128])
    dbg_o = {}
    for k, shp in dbg.items():
        dbg_o[k] = dout("dbg_" + k, shp)

    xT = P.sbuf("xT", [128, KD, T], F32, chunk=2048)
    cst = P.sbuf("cst", [128, 8, 128], F32, chunk=512)
    cstb = P.sbuf("cstb", [128, 6, 128], BF16, chunk=256)
    small = P.sbuf("small", [128, 512], F32, chunk=64)
    wsl = [P.sbuf("wsl%d" % i, [128, KD, 256], BF16, chunk=512) for i in range(4)]
    ARENA_BYTES = 116 * 1024
    A = Arena(P, "arena", ARENA_BYTES)
    ps = P.psum("ps", [128, 4096], F32)
    psb = ps[:, :].bitcast(BF16)

    identf = cst[:, 0, :]
    trif = cst[:, 1, :]
    trib = cst[:, 2, :]
    permA = cst[:, 3, :]
    permB = cst[:, 4, :]
    onesf = cst[:, 5, :]
    identb = cstb[:, 0, :]
    mnegf = cstb[:, 1, :]
    mnegb = cstb[:, 2, :]
    bones = cstb[:, 3, :]
    onesb = cstb[:, 4, :]
    tri = [trif, trib]
    mneg = [mnegf, mnegb]

    _sp = [0]

    def salloc(n):
        o = _sp[0]
        _sp[0] += n
        assert _sp[0] <= 512
        return small[:, o:o + n]

    mod = [salloc(48) for _ in range(NL)]
    opsc = [salloc(16) for _ in range(NL)]
    cv = salloc(8)
    pflag = salloc(1)
    seqf = salloc(16)
    fnw = salloc(8)

    bank_rr = [0]

    reserved = set()

    def banks(n):
        for _ in range(16):
            b = bank_rr[0]
            if b + n > 8:
                b = 0
            bank_rr[0] = (b + n) % 8
            if not any((b + i) in reserved for i in range(n)):
                return b
        raise RuntimeError("no free psum banks")

    def pbank(b, n=1):
        return ps[:, b * 512:(b + n) * 512]

    wsl_rr = [0]

    def next_wsl():
        w = wsl[wsl_rr[0] % 4]
        wsl_rr[0] += 1
        return w

    for i, dsrc in enumerate((identf_d, trif_d, trib_d, permA_d, permB_d)):
        P.dma(cst[:, i, :], dsrc[:, :])
    P.memset(cst[:, 5, :], 1.0)
    P.dma(cstb[:, 0, :], identf_d[:, :], q="pool")
    P.dma(cstb[:, 1, :], mnegf_d[:, :], q="pool")
    P.dma(cstb[:, 2, :], mnegb_d[:, :], q="pool")
    P.dma(cstb[:, 3, :], bones_d[:, :], q="pool")
    P.memset(cstb[:, 4, :], 1.0)
    P.dma(cv, cvec_d[:, :])
    P.dma(pflag, pflag_d[:, :])
    P.dma(seqf, seqf_d[:, :])
    P.dma(fnw, fnw_d[:, :])

    m0 = A.mark()
    scv = A.alloc([128, 8], BF16)
    adab = A.alloc([128, 48], F32)
    P.act(scv, cv, AF.Silu)
    aslab = [A.alloc([128, KD, 512], BF16) for _ in range(2)]
    for l in range(NL):
        pb = banks(1)
        pm = pbank(pb)
        awv = ada_w_d[l].rearrange("(k p) n -> p k n", p=128)
        for s in range(12):
            sl = aslab[s % 2]
            P.dma(sl[:, :, :], awv[:, :, s * 512:(s + 1) * 512], q="pool")
            for jj in range(4):
                j = s * 4 + jj
                for k in range(KD):
                    P.mm(pm[:, j:j + 1], sl[:, k, jj * 128:(jj + 1) * 128], scv[:, k:k + 1],
                         start=(k == 0), stop=(k == KD - 1))
        P.dma(adab, ada_b_d[l])
        P.tt(mod[l], pm[:, 0:48], adab, ALU.add)
        P.ts(opsc[l][:, 0:8], mod[l][:, 8:16], 1.0, None, ALU.add)
        P.ts(opsc[l][:, 8:16], mod[l][:, 32:40], 1.0, None, ALU.add)
    A.reset(m0)

    if "mod" in dbg_o:
        P.dma(dbg_o["mod"][:, 0:48], mod[0])
        P.dma(dbg_o["mod"][:, 48:96], mod[1])
    if stop_after == "S0":
        P.finalize()
        return nc, P
    m0 = A.mark()
    xin = [A.alloc([128, D], F32) for _ in range(2)]
    for tt_ in range(16):
        xi = xin[tt_ % 2]
        P.dma(xi, x_d[tt_ * 128:(tt_ + 1) * 128, :])
        for hb in range(2):
            pb = banks(1)
            for k4 in range(4):
                k = hb * 4 + k4
                P.tr(pbank(pb)[:, k4 * 128:(k4 + 1) * 128], xi[:, k * 128:(k + 1) * 128], identf)
            P.copy(xT[:, hb * 4:(hb + 1) * 4, tt_ * 128:(tt_ + 1) * 128],
                   pbank(pb).rearrange("p (a b) -> p a b", b=128), eng=("act" if hb else "dve"))
    A.reset(m0)

    if stop_after == "S1":
        if "xT" in dbg_o:
            P.dma(dbg_o["xT"], xT[:, :, :])
        P.finalize()
        return nc, P
    def adanorm(l, which, t0, n, hT, col0=0):
        m = A.mark()
        sq = [A.alloc([128, n], BF16) for _ in range(2)]
        lnv = A.alloc([128, n], F32)
        tmp = [A.alloc([128, n], F32) for _ in range(2)]
        nbk = (n + 511) // 512
        pb = banks(nbk)
        pss = pbank(pb, nbk)
        for k in range(KD):
            if k % 2 == 0:
                P.act(sq[k % 2], xT[:, k, t0:t0 + n], AF.Square)
            else:
                P.tt(sq[k % 2], xT[:, k, t0:t0 + n], xT[:, k, t0:t0 + n], ALU.mult)
            for (j0, w) in colgroups(n):
                P.mm(pss[:, j0:j0 + w], onesb, sq[k % 2][:, j0:j0 + w], start=(k == 0), stop=(k == KD - 1))
        P.act(lnv, pss[:, 0:n], AF.Ln, bias=EPS, scale=1.0 / D)
        P.act(lnv, lnv, AF.Exp, scale=-0.5)
        shb = 0 if which == 0 else 24
        for k in range(KD):
            P.tt(tmp[k % 2], xT[:, k, t0:t0 + n], lnv, ALU.mult)
            P.act(hT[:, k, col0:col0 + n], tmp[k % 2], AF.Identity,
                  bias=mod[l][:, shb + k:shb + k + 1], scale=opsc[l][:, which * 8 + k:which * 8 + k + 1])
        A.reset(m)

    def load_wslab(l, c0, ncols):
        wb = next_wsl()
        wv = w_in_d[l].rearrange("(k p) n -> p k n", p=128)
        P.dma(wb[:, :, 0:ncols], wv[:, :, c0:c0 + ncols], q="pool")
        return wb

    def proj_feat(l, c0, ntiles, hT, ntok, consume, hcol0=0):
        i = 0
        pend = None
        while i < ntiles:
            nt = min(2, ntiles - i)
            wb = load_wslab(l, c0 + i * 128, nt * 128)
            for j in range(nt):
                nbk = (ntok + 511) // 512
                pb = banks(nbk)
                pt = pbank(pb, nbk)
                for (j0, w) in colgroups(ntok):
                    for k in range(KD):
                        P.mm(pt[:, j0:j0 + w], wb[:, k, j * 128:(j + 1) * 128], hT[:, k, hcol0 + j0:hcol0 + j0 + w],
                             start=(k == 0), stop=(k == KD - 1))
                if pend is not None:
                    reserved.update(range(pb, pb + nbk))
                    consume(*pend)
                    reserved.difference_update(range(pb, pb + nbk))
                pend = (i + j, pt)
            i += nt
        if pend is not None:
            consume(*pend)

    def dma_dbg(name, src):
        if name in dbg_o:
            P.dma(dbg_o[name], src)

    def layer_params(l):
        lp = {}
        lp["aqn"] = A.alloc([128, 1], F32)
        lp["akn"] = A.alloc([128, 1], F32)
        lp["bsub"] = A.alloc([128, 1], F32)
        lp["scw"] = A.alloc([128, 7, 3], F32)
        lp["scb"] = A.alloc([128, 7], F32)
        lp["nscw0"] = A.alloc([128, 7], F32)
        lp["nscw2"] = A.alloc([128, 7], F32)
        lp["fcw"] = A.alloc([128, 44, 3], F32)
        lp["fcb"] = A.alloc([128, 44], F32)
        lp["nfcw0"] = A.alloc([128, 44], F32)
        lp["nfcw2"] = A.alloc([128, 44], F32)
        lp["arow"] = A.alloc([128, 12], F32)
        lp["dtb"] = A.alloc([128, 12], F32)
        lp["drow"] = A.alloc([128, 6], F32)
        lp["snw"] = A.alloc([128, 384], F32)
        lp["nlam"] = A.alloc([128, 1], F32)
        blam = A.alloc([128, 128], F32)
        lt = A.alloc([128, 4], F32)
        P.dma(lp["aqn"], aqn_d[l])
        P.dma(lp["akn"], akn_d[l])
        P.dma(lp["bsub"], bsub_d[l])
        P.dma(lp["scw"], scw_d[l])
        P.dma(lp["scb"], scb_d[l])
        P.dma(lp["fcw"], fcw_d[l])
        P.dma(lp["fcb"], fcb_d[l])
        P.dma(lp["arow"], alog_d[l:l + 1, :].partition_broadcast(128).rearrange("p a b -> p (a b)"))
        P.dma(lp["dtb"], dtb_d[l:l + 1, :].partition_broadcast(128).rearrange("p a b -> p (a b)"))
        P.dma(lp["drow"], ssmD_d[l:l + 1, :].partition_broadcast(128).rearrange("p a b -> p (a b)"))
        P.dma(lp["snw"], snw_d[l:l + 1, :].partition_broadcast(128).rearrange("p a b -> p (a b)"))
        P.dma(blam, blam_d[l:l + 1, :].partition_broadcast(128).rearrange("p a b -> p (a b)"))
        lam_init = 0.8 - 0.6 * math.exp(-0.3 * l)
        P.act(lp["arow"], lp["arow"], AF.Exp)
        P.ts(lp["arow"], lp["arow"], -1.0, None, ALU.mult)
        bl = blam.rearrange("p (a b) -> p a b", b=32)
        pr = A.alloc([128, 2, 32], F32)
        P.tt(pr[:, 0, :], bl[:, 0, :], bl[:, 1, :], ALU.mult)
        P.tt(pr[:, 1, :], bl[:, 2, :], bl[:, 3, :], ALU.mult)
        P.op("dve", lambda e: e.reduce_sum(lt[:, 0:2], pr, axis=AX.X), reads=[pr], writes=[lt[:, 0:2]])
        P.act(lt[:, 0:2], lt[:, 0:2], AF.Exp)
        P.tt(lt[:, 2:3], lt[:, 1:2], lt[:, 0:1], ALU.subtract)
        P.ts(lp["nlam"], lt[:, 2:3], -lam_init, None, ALU.add)
        P.ts(lp["bsub"], lp["bsub"], 1.0 - lam_init, None, ALU.mult)
        for (dst, src, tap, n) in ((lp["nscw0"], lp["scw"], 0, 7), (lp["nscw2"], lp["scw"], 2, 7),
                                   (lp["nfcw0"], lp["fcw"], 0, 44), (lp["nfcw2"], lp["fcw"], 2, 44)):
            P.ts(dst, src[:, :, tap], pflag[:, 0:1], -1.0, ALU.mult, ALU.mult)
        return lp

    def conv_evac(U, a, n, w3, bcol, pt):
        P.copy(U, pt[:, 0:n + 2], eng="act")
        P.act(a[:, 0:n], pt[:, 1:n + 1], AF.Identity, bias=bcol, scale=w3[:, 1:2])

    def conv_tile(U, a, n, w3, bcol, nw0, nw2, tstart, pt, evac=True):
        if evac:
            conv_evac(U, a, n, w3, bcol, pt)
        if tstart == 0:
            P.memset(U[:, 0:1], 0.0, eng="pool")
        if tstart + n == T:
            P.memset(U[:, n + 1:n + 2], 0.0, eng="pool")
        P.stt(a[:, 0:n], U[:, 0:n], w3[:, 0:1], a[:, 0:n], ALU.mult, ALU.add)
        P.stt(a[:, 0:n], U[:, 2:n + 2], w3[:, 2:3], a[:, 0:n], ALU.mult, ALU.add)
        starts = [b - tstart for b in range(256, T, 256) if tstart <= b < tstart + n]
        ends = [b - 1 - tstart for b in range(256, T, 256) if tstart <= b - 1 < tstart + n]
        if starts:
            s0, cnt = starts[0], len(starts)
            av = a[:, s0:s0 + (cnt - 1) * 256 + 1:256] if cnt > 1 else a[:, s0:s0 + 1]
            uv = U[:, s0:s0 + (cnt - 1) * 256 + 1:256] if cnt > 1 else U[:, s0:s0 + 1]
            P.stt(av, uv, nw0, av, ALU.mult, ALU.add)
        if ends:
            s0, cnt = ends[0], len(ends)
            av = a[:, s0:s0 + (cnt - 1) * 256 + 1:256] if cnt > 1 else a[:, s0:s0 + 1]
            uv = U[:, s0 + 2:s0 + 2 + (cnt - 1) * 256 + 1:256] if cnt > 1 else U[:, s0 + 2:s0 + 3]
            P.stt(av, uv, nw2, av, ALU.mult, ALU.add)

    for l in range(NL):
        A.reset(0)
        lp = layer_params(l)
        mLP = A.mark()
        yT = A.alloc([128, 16, 3, 128], BF16)
        mL = A.mark()
        if stop_after == "LP":
            if "lp" in dbg_o:
                P.dma(dbg_o["lp"][:, 0:12], lp["arow"])
                P.dma(dbg_o["nlam"], lp["nlam"])
                P.dma(dbg_o["lp"][:, 13:20], lp["nscw0"])
                P.dma(dbg_o["lp"][:, 20:26], lp["drow"])
            break

        xcT = A.alloc([128, 3, T], BF16)
        BT = A.alloc([128, 2, T], BF16)
        CT = A.alloc([128, 2, T], BF16)
        zg = A.alloc([128, 16, 384], BF16)
        dtt = A.alloc([128, 16, 12], F32)
        mS = A.mark()
        hT = A.alloc([128, KD, 514], BF16)
        P.memset(hT, 0.0, eng="pool")
        Ub = [A.alloc([128, 514], F32) for _ in range(2)]
        ab = [A.alloc([128, 512], F32) for _ in range(2)]
        dts = A.alloc([128, 4, 12], F32)
        cnt = [0]
        for c in range(4):
            c0 = c * 512
            tlo = max(c0 - 1, 0)
            thi = min(c0 + 513, T)
            adanorm(l, 0, tlo, thi - tlo, hT, col0=tlo - (c0 - 1))

            def cons_xbc(f, pt, c0=c0):
                U = Ub[cnt[0] % 2]
                a = ab[cnt[0] % 2]
                cnt[0] += 1
                conv_evac(U, a, 512, lp["scw"][:, f, :], lp["scb"][:, f:f + 1], pt)
                for fn_ in pend_silu:
                    fn_()
                del pend_silu[:]
                conv_tile(U, a, 512, lp["scw"][:, f, :], lp["scb"][:, f:f + 1],
                          lp["nscw0"][:, f:f + 1], lp["nscw2"][:, f:f + 1], c0, pt, evac=False)
                if f < 3:
                    dst = xcT[:, f, c0:c0 + 512]
                elif f < 5:
                    dst = BT[:, f - 3, c0:c0 + 512]
                else:
                    dst = CT[:, f - 5, c0:c0 + 512]
                pend_silu.append(lambda dst=dst, a=a: P.act(dst, a, AF.Silu))

            pend_silu = []
            proj_feat(l, C_CX, 7, hT, 514, cons_xbc)
            for fn_ in pend_silu:
                fn_()
            del pend_silu[:]
            wz = [load_wslab(l, C_CZ, 256), load_wslab(l, C_CZ + 256, 140)]
            pdt = pbank(banks(1))
            for t4 in range(4):
                pb = banks(1)
                pt = pbank(pb)
                lh = lambda k, t4=t4: hT[:, k, 1 + t4 * 128:1 + (t4 + 1) * 128]
                for k in range(KD):
                    P.mm(pt[:, 0:256], lh(k), wz[0][:, k, 0:256], start=(k == 0), stop=(k == KD - 1))
                for k in range(KD):
                    P.mm(pt[:, 256:384], lh(k), wz[1][:, k, 0:128], start=(k == 0), stop=(k == KD - 1),
                         skip_group_check=True)
                for k in range(KD):
                    P.mm(pdt[:, t4 * 16:t4 * 16 + 12], lh(k), wz[1][:, k, 128:140], start=(k == 0), stop=(k == KD - 1),
                         skip_group_check=True)
                P.act(zg[:, c * 4 + t4, :], pt[:, 0:384], AF.Silu)
            P.tt(dts, pdt[:, 0:64].rearrange("p (t j) -> p t j", j=16)[:, :, 0:12],
                 lp["dtb"].unsqueeze(1).to_broadcast([128, 4, 12]), ALU.add)
            P.act(dts, dts, AF.Exp)
            P.act(dtt[:, c * 4:(c + 1) * 4, :], dts, AF.Ln, bias=1.0)
        A.reset(mS)
        if "xcT" in dbg_o:
            tmpf = A.alloc([128, 3, T], F32)
            P.copy(tmpf, xcT)
            P.dma(dbg_o["xcT"], tmpf)
            A.reset(mS)
        if stop_after == "M1" and l == 0:
            break

        Sst = [A.alloc([128, 384], F32) for _ in range(2)]
        Sent = A.alloc([128, 384], BF16)
        Sbe = A.alloc([128, 16, 384], BF16)
        st_in = A.alloc([128, 3, 128], F32)
        for d_ in range(2):
            P.dma(st_in, ssm0_d[l, d_].rearrange("(j p) n -> p j n", p=128))
            pb = banks(1)
            for j in range(3):
                P.tr(pbank(pb)[:, j * 128:(j + 1) * 128], st_in[:, j, :], identf)
            P.copy(Sst[d_], pbank(pb)[:, 0:384])
        so = [A.alloc([128, 3, 128], F32) for _ in range(2)]
        xtok_b = [A.alloc([128, 384], BF16) for _ in range(2)]
        btok_b = [A.alloc([128, 256], BF16) for _ in range(2)]
        sm = [[A.alloc([128, 64], F32) for _ in range(2)] for _ in range(2)]
        xdt_b = [[A.alloc([128, 384], BF16) for _ in range(2)] for _ in range(2)]
        xdte_b = [[A.alloc([128, 384], BF16) for _ in range(2)] for _ in range(2)]
        dec_b = [A.alloc([128, 12, 128], F32) for _ in range(2)]
        sc_b = [A.alloc([128, 12, 128], BF16) for _ in range(2)]
        yc = [A.alloc([128, 384], F32) for _ in range(3)]
        ynb = A.alloc([128, 384], BF16)
        rs = A.alloc([128, 4], F32)
        scnt = [0]

        def prep_common(ci, par):
            t0 = ci * 128
            pb = banks(1)
            pv = psb[:, pb * 1024:(pb + 1) * 1024]
            for j in range(3):
                P.tr(pv[:, j * 128:(j + 1) * 128], xcT[:, j, t0:t0 + 128], identb)
            pb2 = banks(1)
            pv2 = psb[:, pb2 * 1024:(pb2 + 1) * 1024]
            for g in range(2):
                P.tr(pv2[:, g * 128:(g + 1) * 128], BT[:, g, t0:t0 + 128], identb)
            P.copy(xtok_b[par], pv[:, 0:384])
            P.copy(btok_b[par], pv2[:, 0:256], eng="act")

        def prep_dir(ci, par, d_, full):
            s_ = sm[par][d_]
            a = s_[:, 0:6]
            cs = s_[:, 6:12]
            ncs = s_[:, 12:18]
            dte = s_[:, 18:24]
            dtot = s_[:, 24:30]
            ecs = s_[:, 30:36]
            w_ = s_[:, 36:42]
            dtv = dtt[:, ci, d_ * 6:(d_ + 1) * 6]
            P.tt(a, dtv, lp["arow"][:, d_ * 6:(d_ + 1) * 6], ALU.mult)
            pb = banks(1)
            pt = pbank(pb)
            P.mm(pt[:, 0:6], tri[d_], a)
            P.mm(pt[:, 8:14], onesf, a)
            P.copy(cs, pt[:, 0:6])
            P.ts(ncs, cs, -1.0, None, ALU.mult)
            P.tt(dte, pt[:, 8:14], ncs, ALU.add)
            P.act(dte, dte, AF.Exp)
            P.act(dtot, pt[:, 8:14], AF.Exp)
            P.tt(w_, dtv, dte, ALU.mult)
            x3 = xtok_b[par].rearrange("p (h d) -> p h d", d=64)
            P.tt(xdte_b[par][d_].rearrange("p (h d) -> p h d", d=64), x3,
                 w_.unsqueeze(2).to_broadcast([128, 6, 64]), ALU.mult)
            if full:
                P.act(ecs, cs, AF.Exp)
                P.tt(xdt_b[par][d_].rearrange("p (h d) -> p h d", d=64), x3,
                     dtv.unsqueeze(2).to_broadcast([128, 6, 64]), ALU.mult)
            return dict(a=a, cs=cs, ncs=ncs, dte=dte, dtot=dtot, ecs=ecs)

        def chunk_state_update(ci, par, d_, q):
            pb = banks(1)
            pt = pbank(pb)
            for g in range(2):
                P.mm(pt[:, g * 192:(g + 1) * 192], btok_b[par][:, g * 128:(g + 1) * 128],
                     xdte_b[par][d_][:, g * 192:(g + 1) * 192])
            S3 = Sst[d_].rearrange("p (h d) -> p h d", d=64)
            P.tt(S3, S3, q["dtot"].unsqueeze(2).to_broadcast([128, 6, 64]), ALU.mult)
            P.tt(Sst[d_], Sst[d_], pt[:, 0:384], ALU.add)
            is_end = (ci % 2 == 1) if d_ == 0 else (ci % 2 == 0)
            if is_end:
                seq = ci // 2
                pb2 = banks(1)
                for j in range(3):
                    P.tr(pbank(pb2)[:, j * 128:(j + 1) * 128], Sst[d_][:, j * 128:(j + 1) * 128], identf)
                sob = so[scnt[0] % 2]
                scnt[0] += 1
                P.copy(sob, pbank(pb2)[:, 0:384].rearrange("p (j n) -> p j n", n=128), eng="act")
                P.dma(nssm_o[l, seq, d_].rearrange("(j p) n -> p j n", p=128), sob)
                bnd = ci if d_ == 0 else ci - 1
                if 0 <= bnd <= 14:
                    P.ts(Sst[d_], Sst[d_], seqf[:, bnd:bnd + 1], None, ALU.mult)

        qb_ = {}
        order = list(reversed(range(16)))
        prep_common(order[0], order[0] % 2)
        qb_[order[0]] = prep_dir(order[0], order[0] % 2, 1, False)
        for oi, ci in enumerate(order):
            par = ci % 2
            if oi + 1 < 16:
                cn = order[oi + 1]
                prep_common(cn, cn % 2)
                qb_[cn] = prep_dir(cn, cn % 2, 1, False)
            P.copy(Sbe[:, ci, :], Sst[1], eng="pool")
            chunk_state_update(ci, par, 1, qb_.pop(ci))

        stA = {}

        def stage_A(ci):
            par = ci % 2
            t0 = ci * 128
            prep_common(ci, par)
            qs = [prep_dir(ci, par, 0, True), prep_dir(ci, par, 1, True)]
            pcb = banks(1)
            for g in range(2):
                P.mm(pbank(pcb)[:, g * 128:(g + 1) * 128], BT[:, g, t0:t0 + 128], CT[:, g, t0:t0 + 128])
            pcs = banks(3)
            pcsv = pbank(pcs, 3).rearrange("p (j l) -> p j l", l=128)
            for d_ in range(2):
                for h in range(6):
                    j = d_ * 6 + h
                    P.mm(pcsv[:, j, :], qs[d_]["a"][:, h:h + 1].to_broadcast([128, 128]), tri[d_],
                         start=True, stop=False, skip_group_check=True)
                    P.mm(pcsv[:, j, :], identb, mneg[d_], start=False, stop=True, skip_group_check=True)
            dec = dec_b[par]
            scb_ = sc_b[par]
            for d_ in range(2):
                for h in range(6):
                    j = d_ * 6 + h
                    P.act(dec[:, j, :], pcsv[:, j, :], AF.Exp, bias=qs[d_]["ncs"][:, h:h + 1])
            for d_ in range(2):
                for g in range(2):
                    j0 = d_ * 6 + g * 3
                    P.tt(scb_[:, j0:j0 + 3, :],
                         pbank(pcb)[:, g * 128:(g + 1) * 128].unsqueeze(1).to_broadcast([128, 3, 128]),
                         dec[:, j0:j0 + 3, :], ALU.mult)
            stA[ci] = qs

        def stage_B(ci):
            par = ci % 2
            t0 = ci * 128
            qs = stA.pop(ci)
            scb_ = sc_b[par]
            P.copy(Sent, Sst[0], eng="pool")
            pyd = banks(1)
            for h in range(6):
                for d_ in range(2):
                    j = d_ * 6 + h
                    P.mm(pbank(pyd)[:, h * 64:(h + 1) * 64], scb_[:, j, :], xdt_b[par][d_][:, h * 64:(h + 1) * 64],
                         start=(d_ == 0), stop=(d_ == 1), skip_group_check=True)
            pyo = [banks(1), banks(1)]
            for d_ in range(2):
                src = Sent if d_ == 0 else Sbe[:, ci, :]
                for g in range(2):
                    P.mm(pbank(pyo[d_])[:, g * 192:(g + 1) * 192], CT[:, g, t0:t0 + 128], src[:, g * 192:(g + 1) * 192])
            y0, y1, y2 = yc
            for d_, yy in ((0, y0), (1, y1)):
                P.tt(yy.rearrange("p (h d) -> p h d", d=64),
                     pbank(pyo[d_])[:, 0:384].rearrange("p (h d) -> p h d", d=64),
                     qs[d_]["ecs"].unsqueeze(2).to_broadcast([128, 6, 64]), ALU.mult)
            P.tt(y0, y0, y1, ALU.add)
            P.tt(y0, pbank(pyd)[:, 0:384], y0, ALU.add)
            P.tt(y1.rearrange("p (h d) -> p h d", d=64), xtok_b[par].rearrange("p (h d) -> p h d", d=64),
                 lp["drow"].unsqueeze(2).to_broadcast([128, 6, 64]), ALU.mult)
            P.tt(y0, y0, y1, ALU.add)
            dma_dbg("yraw%d" % ci, y0)
            P.tt(y0, y0, zg[:, ci, :], ALU.mult)
            P.tt(y2, y0, y0, ALU.mult)
            P.op("dve", lambda e, y2=y2: e.reduce_sum(rs[:, 0:1], y2, axis=AX.X), reads=[y2], writes=[rs[:, 0:1]])
            P.act(rs[:, 1:2], rs[:, 0:1], AF.Ln, bias=EPS, scale=1.0 / 384)
            P.act(rs[:, 1:2], rs[:, 1:2], AF.Exp, scale=-0.5)
            P.stt(ynb, y0, rs[:, 1:2], lp["snw"], ALU.mult, ALU.mult)
            pb = banks(1)
            pv = psb[:, pb * 1024:(pb + 1) * 1024]
            for j in range(3):
                P.tr(pv[:, j * 128:(j + 1) * 128], ynb[:, j * 128:(j + 1) * 128], identb)
            P.copy(yT[:, ci, :, :], pv[:, 0:384].rearrange("p (j t) -> p j t", t=128), eng="act")
            chunk_state_update(ci, par, 0, qs[0])

        stage_A(0)
        for ci in range(16):
            if ci + 1 < 16:
                stage_A(ci + 1)
            stage_B(ci)
        A.reset(mL)
        if stop_after == "M2" and l == 0:
            break

        KA = A.alloc([128, 2, NKEY], BF16)
        KB = A.alloc([128, 4, NKEY], BF16)
        VX = A.alloc([128, NKT, 576], BF16)
        mK = A.mark()
        P.memset(KA, 0.0, eng="pool")
        P.memset(KB, 0.0, eng="pool")
        P.dma(KA[64:73, :, :], mk_d[:, :].unsqueeze(1).to_broadcast([9, 2, NKEY]), q="pool")
        P.dma(KB[32:41, :, :], mk_d[:, :].unsqueeze(1).to_broadcast([9, 4, NKEY]), q="pool")
        P.dma(KB[96:105, :, :], mk_d[:, :].unsqueeze(1).to_broadcast([9, 4, NKEY]), q="pool")
        VX4 = VX.rearrange("p k (a s d) -> p k a s d", s=3, d=64)
        P.memset(VX4[:, :, :, 1, :], 1.0, eng="pool")
        hT = A.alloc([128, KD, 512], BF16)
        rope = [A.alloc([128, 512], F32) for _ in range(4)]
        sqb = A.alloc([128, 512], BF16)
        lnv = A.alloc([128, 512], F32)
        kn = A.alloc([128, 512], F32)
        t1 = A.alloc([128, 512], F32)
        kr = A.alloc([128, 512], F32)
        ktr = [A.alloc([128, 512], F32) for _ in range(2)]
        vst = [A.alloc([128, 384], F32) for _ in range(2)]
        cin = A.alloc([128, 2, 256], F32)
        ktc = [0]

        def rope_apply(src, dst, perm, cosT, sinT):
            pb = banks(1)
            P.mm(pbank(pb), perm, src)
            P.tt(t1, src, cosT, ALU.mult)
            P.tt(dst, pbank(pb), sinT, ALU.mult)
            P.tt(dst, dst, t1, ALU.add)

        def head_norm(pt, wcol, dst):
            P.act(sqb, pt, AF.Square)
            pb = banks(1)
            P.mm(pbank(pb), bones, sqb)
            P.act(lnv, pbank(pb), AF.Ln, bias=EPS, scale=1.0 / 64)
            P.act(lnv, lnv, AF.Exp, scale=-0.5)
            P.stt(dst, pt, wcol, lnv, ALU.mult, ALU.mult)

        def out_tok(src, dst_d, c0, ncol):
            pb = banks(1)
            for t4 in range(4):
                P.tr(pbank(pb)[:, t4 * 128:(t4 + 1) * 128], src[:, t4 * 128:(t4 + 1) * 128], identf)
            kt_ = ktr[ktc[0] % 2]
            ktc[0] += 1
            P.copy(kt_, pbank(pb), eng="act")
            P.dma(dst_d[c0:c0 + 512, ncol:ncol + 128].rearrange("(t p) f -> p t f", p=128),
                  kt_.rearrange("p (t f) -> p t f", f=128))

        for kt2 in range(2):
            P.dma(cin[:, 0, 0:128], cak_d[l, kt2 * 128:(kt2 + 1) * 128, :])
            pb = banks(1)
            P.tr(pbank(pb)[:, 0:128], cin[:, 0, 0:128], identf)
            kc = T + kt2 * 128
            P.copy(KA[0:64, 0, kc:kc + 128], pbank(pb)[0:64, 0:128])
            P.copy(KA[0:64, 1, kc:kc + 128], pbank(pb)[64:128, 0:128])
            P.dma(cin[:, 1, :], cbk_d[l, kt2 * 128:(kt2 + 1) * 128, :])
            pb = banks(1)
            for j in range(2):
                P.tr(pbank(pb)[:, j * 128:(j + 1) * 128], cin[:, 1, j * 128:(j + 1) * 128], identf)
            for h in range(4):
                j = h // 2
                r0 = (h % 2) * 64
                P.copy(KB[0:32, h, kc:kc + 128], pbank(pb)[r0:r0 + 32, j * 128:(j + 1) * 128])
                P.copy(KB[64:96, h, kc:kc + 128], pbank(pb)[r0 + 32:r0 + 64, j * 128:(j + 1) * 128])
            P.dma(VX4[:, 16 + kt2, 0, 0:3:2, :], cav_d[l, kt2 * 128:(kt2 + 1) * 128, :].rearrange("p (s d) -> p s d", d=64), q="pool")
            for a_ in range(2):
                P.dma(VX4[:, 16 + kt2, 1 + a_, 0:3:2, :],
                      cbv_d[l, kt2 * 128:(kt2 + 1) * 128, a_ * 128:(a_ + 1) * 128].rearrange("p (s d) -> p s d", d=64), q="pool")

        for c in range(4):
            c0 = c * 512
            adanorm(l, 0, c0, 512, hT)
            for i in (0, 2):
                P.dma(rope[i], rope_d[i][:, c0:c0 + 512])
                P.dma(rope[i + 1], rope_d[i + 1][:, c0:c0 + 512])

            def cons_k(f, pt, c0=c0):
                if f == 0:
                    head_norm(pt, lp["akn"][:, 0:1], kn)
                    rope_apply(kn, kr, permA, rope[0], rope[1])
                    P.copy(KA[0:64, 0, c0:c0 + 512], kr[0:64, :], eng="pool")
                    P.copy(KA[0:64, 1, c0:c0 + 512], kr[64:128, :], eng="pool")
                    out_tok(kr, nak_o[l], c0, 0)
                else:
                    P.copy(kn, pt, eng="act")
                    rope_apply(kn, kr, permB, rope[2], rope[3])
                    for hh in range(2):
                        h = (f - 1) * 2 + hh
                        r0 = hh * 64
                        P.copy(KB[0:32, h, c0:c0 + 512], kr[r0:r0 + 32, :], eng="dve")
                        P.copy(KB[64:96, h, c0:c0 + 512], kr[r0 + 32:r0 + 64, :], eng="dve")
                    out_tok(kr, nbk_o[l], c0, (f - 1) * 128)

            proj_feat(l, C_AK, 1, hT, 512, cons_k)
            proj_feat(l, C_BK, 2, hT, 512, lambda f, pt: cons_k(f + 1, pt))
            wv_ = [load_wslab(l, C_AV, 256), load_wslab(l, C_AV + 256, 128)]
            for t4 in range(4):
                pb = banks(1)
                pt = pbank(pb)
                for k in range(KD):
                    P.mm(pt[:, 0:256], hT[:, k, t4 * 128:(t4 + 1) * 128], wv_[0][:, k, 0:256],
                         start=(k == 0), stop=(k == KD - 1))
                for k in range(KD):
                    P.mm(pt[:, 256:384], hT[:, k, t4 * 128:(t4 + 1) * 128], wv_[1][:, k, 0:128],
                         start=(k == 0), stop=(k == KD - 1), skip_group_check=True)
                vs = vst[t4 % 2]
                P.copy(vs, pt[:, 0:384], eng="act")
                tk = c * 4 + t4
                P.dma(nav_o[l, tk * 128:(tk + 1) * 128, :], vs[:, 0:128])
                P.dma(nbv_o[l, tk * 128:(tk + 1) * 128, :], vs[:, 128:384])
                P.copy(VX4[:, tk, :, 0:3:2, :], vs.rearrange("p (a s d) -> p a s d", s=2, d=64), eng="pool")
        A.reset(mK)
        if stop_after == "M3" and l == 0:
            break

        hT = A.alloc([128, KD, 512], BF16)
        sqb = A.alloc([128, 512], BF16)
        lnv = A.alloc([128, 512], F32)
        QA = A.alloc([128, 6, 512], BF16)
        QB = A.alloc([128, 4, 512], BF16)
        mixT = A.alloc([128, 5, 512], BF16)
        bo = A.alloc([128, 512], F32)
        mOv = A.mark()
        rope = [A.alloc([128, 512], F32) for _ in range(4)]
        kn = A.alloc([128, 512], F32)
        t1 = A.alloc([128, 512], F32)
        kr = A.alloc([128, 512], F32)
        A.reset(mOv)
        PT = [A.alloc([128, 1024], BF16) for _ in range(3)]
        rz = A.alloc([128, 2, 512], F32)
        ob = A.alloc([128, 2, 512], F32)
        A.reset(mOv)
        P.memset(QA, 0.0, eng="pool")
        P.memset(QB, 0.0, eng="pool")
        sA = 1.0 / 8.0
        sB = 32.0 ** -0.5
        ptc = [0]
        for c in range(4):
            c0 = c * 512
            adanorm(l, 0, c0, 512, hT)
            for i in range(4):
                P.dma(rope[i], rope_d[i][:, c0:c0 + 512])
            P.dma(QA[64:73, :, :], mq_d[:, c0:c0 + 512].unsqueeze(1).to_broadcast([9, 6, 512]), q="pool")
            P.dma(QB[32:41, :, :], mq_d[:, c0:c0 + 512].unsqueeze(1).to_broadcast([9, 4, 512]), q="pool")
            P.dma(QB[96:105, :, :], mq_d[:, c0:c0 + 512].unsqueeze(1).to_broadcast([9, 4, 512]), q="pool")

            def cons_qa(f, pt):
                head_norm(pt, lp["aqn"][:, 0:1], kn)
                rope_apply(kn, kr, permA, rope[0], rope[1])
                P.copy(QA[0:64, f, :], kr[0:64, :], eng="pool")
                P.copy(QA[0:64, f + 3, :], kr[64:128, :], eng="pool")

            def cons_qb(f, pt):
                P.copy(kn, pt, eng="act")
                rope_apply(kn, kr, permB, rope[2], rope[3])
                for hh in range(2):
                    h = f * 2 + hh
                    r0 = hh * 64
                    P.copy(QB[0:32, h, :], kr[r0:r0 + 32, :], eng="dve")
                    P.copy(QB[64:96, h, :], kr[r0 + 32:r0 + 64, :], eng="dve")

            proj_feat(l, C_AQ, 3, hT, 512, cons_qa)
            proj_feat(l, C_BQ, 2, hT, 512, cons_qb)

            groups = [("A", f) for f in range(3)] + [("B", h) for h in range(4)]
            def group_units(kind, idx):
                if kind == "A":
                    return ([(KA[0:73, 0, :], QA[0:73, idx, :], 0, 128),
                             (KA[0:73, 1, :], QA[0:73, idx + 3, :], 64, 192)], sA)
                h = idx
                vc = (1 + h // 2) * 192 + (0 if h % 2 == 0 else 64)
                return ([(KB[0:41, h, :], QB[0:41, h, :], vc, vc + 128),
                         (KB[64:105, h, :], QB[64:105, h, :], vc, vc + 128)], sB)

            gunits = [group_units(k_, i_) for (k_, i_) in groups]
            steps = [(gi, kt) for gi in range(len(groups)) for kt in range(NKT)]
            pts = {}
            pending = []

            def emit_S(si):
                gi, kt = steps[si]
                units, scl = gunits[gi]
                sb_ = 2 * (si % 2)
                for u, (Kt, Qt, v0, v1) in enumerate(units):
                    P.mm(pbank(sb_ + u), Kt[:, kt * 128:(kt + 1) * 128], Qt)
                pt_ = PT[si % 3]
                pts[si] = pt_
                P.act(pt_, pbank(sb_, 2), AF.Exp, scale=scl)

            def emit_AV(si):
                gi, kt = steps[si]
                units, scl = gunits[gi]
                ob_ = 4 + 2 * (gi % 2)
                pt_ = pts.pop(si)
                for u, (Kt, Qt, v0, v1) in enumerate(units):
                    P.mm(pbank(ob_ + u), VX[:, kt, v0:v1], pt_[:, u * 512:(u + 1) * 512],
                         start=(kt == 0), stop=(kt == NKT - 1))
                if kt == NKT - 1:
                    evac(gi, si)

            def evac(gi, si):
                kind, idx = groups[gi]
                ob_ = 4 + 2 * (gi % 2)
                if kind == "A":
                    O0 = pbank(ob_)
                    O1 = pbank(ob_ + 1)
                    P.op("dve", lambda e, O0=O0: e.reciprocal(rz[0:64, 0, :], O0[64:128, :]),
                         reads=[O0[64:128, :]], writes=[rz[0:64, 0, :]])
                    P.tt(mixT[0:64, idx, :], O0[0:64, :], rz[0:64, 0, :], ALU.mult)
                    P.op("dve", lambda e, O1=O1: e.reciprocal(rz[64:128, 1, :], O1[0:64, :]),
                         reads=[O1[0:64, :]], writes=[rz[64:128, 1, :]])
                    P.tt(mixT[64:128, idx, :], O1[64:128, :], rz[64:128, 1, :], ALU.mult)
                    return
                h = idx
                r0 = 0 if h % 2 == 0 else 64
                z0 = 64 - r0
                for u in range(2):
                    Ou = pbank(ob_ + u)
                    P.op("dve", lambda e, Ou=Ou, u=u, r0=r0, z0=z0: e.reciprocal(rz[r0:r0 + 64, u, :], Ou[z0:z0 + 64, :]),
                         reads=[Ou[z0:z0 + 64, :]], writes=[rz[r0:r0 + 64, u, :]])
                    P.tt(ob[r0:r0 + 64, u, :], Ou[r0:r0 + 64, :], rz[r0:r0 + 64, u, :], ALU.mult)
                P.stt(bo[r0:r0 + 64, :], ob[r0:r0 + 64, 1, :], lp["nlam"][r0:r0 + 64, 0:1], ob[r0:r0 + 64, 0, :],
                      ALU.mult, ALU.add)
                if h % 2 == 1:
                    def subln(h=h):
                        P.act(sqb, bo, AF.Square)
                        pb = 2 * (len(pts) % 2)
                        P.mm(pbank(pb), bones, sqb)
                        P.act(lnv, pbank(pb), AF.Ln, bias=EPS, scale=1.0 / 64)
                        P.act(lnv, lnv, AF.Exp, scale=-0.5)
                        P.stt(mixT[:, 3 + h // 2, :], bo, lp["bsub"][:, 0:1], lnv, ALU.mult, ALU.mult)
                    pending.append((si + 10, subln))

            NS = len(steps)
            emit_S(0)
            for si in range(1, NS):
                emit_S(si)
                emit_AV(si - 1)
                for item in list(pending):
                    if item[0] <= si:
                        pending.remove(item)
                        item[1]()
            emit_AV(NS - 1)
            for item in list(pending):
                pending.remove(item)
                item[1]()
            if ("mixT%d" % c) in dbg_o:
                tmpf = A.alloc([128, 5, 512], F32)
                P.copy(tmpf, mixT)
                P.dma(dbg_o["mixT%d" % c], tmpf)
            wov = w_out_d[l].rearrange("(k p) n -> p k n", p=128)
            for dp in range(4):
                wb = next_wsl()
                P.dma(wb[:, :, :], wov[:, :, dp * 256:(dp + 1) * 256], q="pool")
                for dd in range(2):
                    dt_ = dp * 2 + dd
                    pb = banks(1)
                    pt = pbank(pb)
                    for k in range(KD):
                        if k < 5:
                            rhs = mixT[:, k, :]
                        else:
                            rhs = yT[:, c * 4:(c + 1) * 4, k - 5, :]
                        P.mm(pt, wb[:, k, dd * 128:(dd + 1) * 128], rhs, start=(k == 0), stop=(k == KD - 1))
                    P.stt(xT[:, dt_, c0:c0 + 512], pt, mod[l][:, 16 + dt_:17 + dt_], xT[:, dt_, c0:c0 + 512],
                          ALU.mult, ALU.add)
        A.reset(mL)
        if stop_after == "M4" and l == 0:
            break

        A.reset(mLP)
        hT = A.alloc([128, KD, 1026], BF16)
        P.memset(hT, 0.0, eng="pool")
        hkeep = A.alloc([128, KD, 1], BF16)
        hmid = A.alloc([128, NFF, 1024], BF16)
        wd = [A.alloc([128, NFF, 128], BF16) for _ in range(2)]
        mOv = A.mark()
        Ub = [[A.alloc([128, 1026], F32) for _ in range(2)] for _ in range(2)]
        ab = [[A.alloc([128, 1024], F32) for _ in range(2)] for _ in range(2)]
        A.reset(mOv)
        upv = ffn_up_d[l].rearrange("(k p) n -> p k n", p=128)
        dnv = ffn_down_d[l].rearrange("(k p) n -> p k n", p=128)
        fc = [0]
        for hf in range(2):
            h0 = hf * 1024
            if hf == 0:
                adanorm(l, 1, 0, 1025, hT, col0=1)
                P.copy(hkeep, hT[:, :, 1024:1025], eng="pool")
            else:
                adanorm(l, 1, 1024, 1024, hT, col0=1)
                P.copy(hT[:, :, 0:1], hkeep, eng="pool")
            pend_f = []
            for s in range(11):
                wg = next_wsl()
                P.dma(wg[:, :, :], upv[:, :, s * 256:(s + 1) * 256], q="pool")
                wvv = next_wsl()
                P.dma(wvv[:, :, :], upv[:, :, D_FF + s * 256:D_FF + (s + 1) * 256], q="pool")
                for j in range(2):
                    i = s * 2 + j
                    par = i % 2
                    pts_ = []
                    for which, wb in ((0, wg), (1, wvv)):
                        pb = banks(3)
                        pt = pbank(pb, 3)
                        for (j0, w) in colgroups(1026):
                            for k in range(KD):
                                P.mm(pt[:, j0:j0 + w], wb[:, k, j * 128:(j + 1) * 128], hT[:, k, j0:j0 + w],
                                     start=(k == 0), stop=(k == KD - 1))
                        pts_.append(pt)
                    for which in range(2):
                        ft = i + which * NFF
                        conv_evac(Ub[par][which], ab[par][which], 1024, lp["fcw"][:, ft, :], lp["fcb"][:, ft:ft + 1], pts_[which])
                    for fn_ in pend_f:
                        fn_()
                    del pend_f[:]
                    for which in range(2):
                        ft = i + which * NFF
                        conv_tile(Ub[par][which], ab[par][which], 1024, lp["fcw"][:, ft, :], lp["fcb"][:, ft:ft + 1],
                                  lp["nfcw0"][:, ft:ft + 1], lp["nfcw2"][:, ft:ft + 1], h0, pts_[which], evac=False)

                    def fin(i=i, par=par):
                        P.act(ab[par][0], ab[par][0], AF.Silu)
                        P.tt(hmid[:, i, :], ab[par][1], ab[par][0], ALU.mult)
                    pend_f.append(fin)
            for fn_ in pend_f:
                fn_()
            del pend_f[:]
            for dt_ in range(8):
                wdb = wd[dt_ % 2]
                P.dma(wdb[:, :, :], dnv[:, :, dt_ * 128:(dt_ + 1) * 128], q="pool")
                for nch in range(2):
                    pb = banks(1)
                    pt = pbank(pb)
                    for i in range(NFF):
                        P.mm(pt, wdb[:, i, :], hmid[:, i, nch * 512:(nch + 1) * 512],
                             start=(i == 0), stop=(i == NFF - 1))
                    tsl = slice(h0 + nch * 512, h0 + (nch + 1) * 512)
                    P.stt(xT[:, dt_, tsl], pt, mod[l][:, 40 + dt_:41 + dt_], xT[:, dt_, tsl], ALU.mult, ALU.add)
        A.reset(mL)
        if stop_after == "F0" and l == 0:
            break

    A.reset(0)
    if stop_after is None or stop_after == "FIN":
        sq = [A.alloc([128, 512], BF16) for _ in range(2)]
        lnv = A.alloc([128, 512], F32)
        yn = A.alloc([128, KD, 512], F32)
        yo = [A.alloc([128, D], F32) for _ in range(2)]
        for c in range(4):
            c0 = c * 512
            pb = banks(1)
            for k in range(KD):
                P.act(sq[k % 2], xT[:, k, c0:c0 + 512], AF.Square)
                P.mm(pbank(pb), onesb, sq[k % 2], start=(k == 0), stop=(k == KD - 1))
            P.act(lnv, pbank(pb), AF.Ln, bias=EPS, scale=1.0 / D)
            P.act(lnv, lnv, AF.Exp, scale=-0.5)
            for k in range(KD):
                P.stt(yn[:, k, :], xT[:, k, c0:c0 + 512], fnw[:, k:k + 1], lnv, ALU.mult, ALU.mult)
            for t4 in range(4):
                yb = yo[t4 % 2]
                for hb in range(2):
                    pb2 = banks(1)
                    for k4 in range(4):
                        k = hb * 4 + k4
                        P.tr(pbank(pb2)[:, k4 * 128:(k4 + 1) * 128], yn[:, k, t4 * 128:(t4 + 1) * 128], identf)
                    P.copy(yb[:, hb * 512:(hb + 1) * 512], pbank(pb2), eng=("act" if hb else "dve"))
                tk = c * 4 + t4
                P.dma(y_o[tk * 128:(tk + 1) * 128, :], yb)
    else:
        pass
    if "xT" in dbg_o:
        P.dma(dbg_o["xT"], xT[:, :, :])
    P.finalize()
    P.arena_peak = A.peak
    return nc, P


def _rope_tables(L, d, grid_w=64, theta=10000.0):
    rows = L // grid_w
    row = np.repeat(np.arange(rows), grid_w).astype(np.float32)
    col = np.tile(np.arange(grid_w), rows).astype(np.float32)
    quarter = d // 4
    inv = (np.float32(theta) ** (-np.arange(quarter, dtype=np.float32) / np.float32(quarter))).astype(np.float32)
    ang_r = row[:, None] * inv[None, :]
    ang_c = col[:, None] * inv[None, :]
    ang = np.concatenate([ang_r, ang_r, ang_c, ang_c], axis=-1)
    return np.cos(ang).astype(np.float32), np.sin(ang).astype(np.float32)


def _perm_mat(d):
    q = d // 4
    Pm = np.zeros((128, 128), np.float32)
    for b0 in range(0, 128, d):
        for i in range(q):
            Pm[b0 + q + i, b0 + i] = -1.0
            Pm[b0 + i, b0 + q + i] = 1.0
            Pm[b0 + 3 * q + i, b0 + 2 * q + i] = -1.0
            Pm[b0 + 2 * q + i, b0 + 3 * q + i] = 1.0
    return Pm


def _consts():
    r = np.arange(128)
    c = {}
    c["identf"] = np.eye(128, dtype=np.float32)
    c["trif"] = (r[:, None] <= r[None, :]).astype(np.float32)
    c["trib"] = (r[:, None] >= r[None, :]).astype(np.float32)
    c["mnegf"] = np.where(r[None, :] < r[:, None], NEG, 0.0).astype(np.float32)
    c["mnegb"] = np.where(r[None, :] > r[:, None], NEG, 0.0).astype(np.float32)
    c["permA"] = _perm_mat(64)
    c["permB"] = _perm_mat(32)
    bo = np.zeros((128, 128), np.float32)
    bo[:64, :64] = 1.0
    bo[64:, 64:] = 1.0
    c["bones"] = bo
    return c


def _shared_weights(inp):
    f = lambda a: np.ascontiguousarray(np.asarray(a, dtype=np.float32))
    w = {}
    w["ada_w"] = f(inp["ada_w"])
    w["ada_b"] = f(np.asarray(inp["ada_b"]).reshape(NL, 48, 128).transpose(0, 2, 1))
    win = np.asarray(inp["w_in"], dtype=np.float32)
    o = np.cumsum([0, 384, 128, 128, 256, 256, 256, 384, 384, 256, 256, 12])
    aq, ak, av, bq, bk, bv, cx, cz, cB, cC, cdt = [win[:, :, o[i]:o[i + 1]] for i in range(11)]
    aqh = aq.reshape(NL, D, 6, 64)
    aqp = aqh[:, :, [0, 3, 1, 4, 2, 5], :].reshape(NL, D, 384)
    w["w_in"] = f(np.concatenate([aqp, ak, bq, bk, cx, cB, cC, av, bv, cz, cdt], axis=-1))
    tile2 = lambda v: f(np.concatenate([v, v], axis=-1)[:, :, None])
    w["aqn"] = tile2(np.asarray(inp["a_q_norm"]))
    w["akn"] = tile2(np.asarray(inp["a_k_norm"]))
    w["blam"] = f(np.asarray(inp["b_lambda"]).reshape(NL, 128))
    w["bsub"] = tile2(np.asarray(inp["b_subln"]))
    w["scw"] = f(np.asarray(inp["ssm_conv_w"]).reshape(NL, 3, 7, 128).transpose(0, 3, 2, 1))
    w["scb"] = f(np.asarray(inp["ssm_conv_b"]).reshape(NL, 7, 128).transpose(0, 2, 1))
    w["alog"] = f(np.asarray(inp["ssm_A_log"]).reshape(NL, 12))
    w["dtb"] = f(np.asarray(inp["ssm_dt_bias"]).reshape(NL, 12))
    w["ssmD"] = f(inp["ssm_D"])
    w["snw"] = f(inp["ssm_norm_w"])
    wo = np.asarray(inp["w_out"], dtype=np.float32)
    woa = wo[:, :384, :].reshape(NL, 6, 64, D)[:, [0, 3, 1, 4, 2, 5]].reshape(NL, 384, D)
    w["w_out"] = f(np.concatenate([woa, wo[:, 384:, :]], axis=1))
    w["ffn_up"] = f(inp["ffn_up"])
    w["fcw"] = f(np.asarray(inp["ffn_conv_w"]).reshape(NL, 3, 44, 128).transpose(0, 3, 2, 1))
    w["fcb"] = f(np.asarray(inp["ffn_conv_b"]).reshape(NL, 44, 128).transpose(0, 2, 1))
    w["ffn_down"] = f(inp["ffn_down"])
    w["fnw"] = f(np.asarray(inp["final_norm_w"]).reshape(8, 128).T)
    return w


def _core_inputs(inp, core, shared, consts, ropeS):
    f = lambda a: np.ascontiguousarray(np.asarray(a, dtype=np.float32))
    m = dict(shared)
    m.update(consts)
    is_prompt = core >= 4
    if not is_prompt:
        b = core
        m["x"] = f(inp["x_sample"][b])
        cvec = np.asarray(inp["c"])[b]
        m["cak"] = f(np.asarray(inp["cache_a_k"])[b].reshape(NL, 256, 128))
        m["cav"] = f(np.asarray(inp["cache_a_v"])[b].reshape(NL, 256, 128))
        m["cbk"] = f(np.asarray(inp["cache_b_k"])[b].reshape(NL, 256, 256))
        m["cbv"] = f(np.asarray(inp["cache_b_v"])[b].reshape(NL, 256, 256))
        m["ssm0"] = f(np.asarray(inp["state_ssm"])[b].reshape(NL, 2, 384, 128))
        mq = np.zeros((9, T), np.float32)
        mq[8] = 1.0
        mk = np.zeros((9, NKEY), np.float32)
        m["cosA"], m["sinA"], m["cosB"], m["sinB"] = ropeS
        m["pflag"] = np.zeros((128, 1), np.float32)
        m["seqf"] = np.ones((128, 16), np.float32)
    else:
        j = core - 4
        m["x"] = f(np.asarray(inp["x_prompt"])[8 * j:8 * j + 8].reshape(T, D))
        cvec = np.asarray(inp["c_ctx"])
        m["cak"] = np.zeros((NL, 256, 128), np.float32)
        m["cav"] = np.zeros((NL, 256, 128), np.float32)
        m["cbk"] = np.zeros((NL, 256, 256), np.float32)
        m["cbv"] = np.zeros((NL, 256, 256), np.float32)
        m["ssm0"] = np.zeros((NL, 2, 384, 128), np.float32)
        seq_q = np.arange(T) // 256
        mq = np.zeros((9, T), np.float32)
        mq[seq_q, np.arange(T)] = 1.0
        mq[8] = 1.0
        mk = np.zeros((9, NKEY), np.float32)
        mk[seq_q, np.arange(T)] = MASKV
        mk[8] = -MASKV
        one = np.ones((128, T), np.float32)
        zero = np.zeros((128, T), np.float32)
        m["cosA"], m["sinA"], m["cosB"], m["sinB"] = one, zero, one, zero
        m["pflag"] = np.ones((128, 1), np.float32)
        sf = np.ones((128, 16), np.float32)
        sf[:, 1::2] = 0.0
        m["seqf"] = sf
    m["cvec"] = f(np.asarray(cvec).reshape(8, 128).T)
    m["mq"] = mq
    m["mk"] = mk
    return m


def make_in_maps(inp):
    shared = _shared_weights(inp)
    consts = _consts()
    cA, sA_ = _rope_tables(T, 64)
    cB, sB_ = _rope_tables(T, 32)
    tA = lambda a: np.ascontiguousarray(np.tile(a.T, (2, 1)))
    tB = lambda a: np.ascontiguousarray(np.tile(a.T, (4, 1)))
    ropeS = (tA(cA), tA(sA_), tB(cB), tB(sB_))
    return [_core_inputs(inp, core, shared, consts, ropeS) for core in range(8)]


_NC_CACHE = {}


def kernel(**inputs):
    in_maps = make_in_maps(inputs)
    if "nc" not in _NC_CACHE:
        _NC_CACHE["nc"] = build()[0]
    nc = _NC_CACHE["nc"]
    res = run_bass_kernel_spmd(nc, in_maps, core_ids=list(range(8)))
    r = res.results
    B = 32
    y_sample = np.stack([r[b]["y"] for b in range(4)], axis=0).astype(np.float32)
    y_prompt = np.concatenate([r[4 + j]["y"].reshape(8, 256, D) for j in range(4)], axis=0).astype(np.float32)

    def gather(name, tail):
        parts = []
        for j in range(4):
            a = r[4 + j][name]
            a = a.reshape(NL, 8, 256, -1).transpose(1, 0, 2, 3)
            parts.append(a)
        a = np.concatenate(parts, axis=0)
        return np.ascontiguousarray(a.reshape((B, NL, 256) + tail)).astype(np.float32)

    new_a_k = gather("nak", (2, 64))
    new_a_v = gather("nav", (2, 64))
    new_b_k = gather("nbk", (4, 2, 32))
    new_b_v = gather("nbv", (4, 64))
    parts = []
    for j in range(4):
        a = r[4 + j]["nssm"]
        parts.append(a.transpose(1, 0, 2, 3, 4))
    new_ssm = np.ascontiguousarray(np.concatenate(parts, axis=0).reshape(B, NL, 2, 6, 64, 128)).astype(np.float32)
    return (y_prompt, y_sample, new_a_k, new_a_v, new_b_k, new_b_v, new_ssm)
```

```python
import math
import numpy as np
import concourse.bass as bass
import concourse.mybir as mybir
from concourse.bass_utils import run_bass_kernel_spmd

F32 = mybir.dt.float32
BF16 = mybir.dt.bfloat16
AF = mybir.ActivationFunctionType
ALU = mybir.AluOpType
AX = mybir.AxisListType
_DSZ = {F32: 4, BF16: 2}


def dsz(dt):
    return _DSZ[dt]


class _Op:
    __slots__ = ("eng", "fn", "deps", "gidx", "idx", "is_dma", "defer", "sig", "tok", "dsem", "dval")

    def __init__(self, eng, fn, deps, gidx, idx, is_dma, defer):
        self.eng = eng
        self.fn = fn
        self.deps = deps
        self.gidx = gidx
        self.idx = idx
        self.is_dma = is_dma
        self.defer = defer
        self.sig = False
        self.tok = None
        self.dsem = None
        self.dval = None


class Prog:
    ENGS = ("pe", "act", "dve", "pool", "sp")
    NDS = 8

    def __init__(self, nc):
        self.nc = nc
        self.ops = {e: [] for e in self.ENGS}
        self.all_ops = []
        self.last_w = {}
        self.readers = {}
        self.chunk = {}
        self.rowbytes = {}
        self._cm = []
        self.psum_names = set()

    def sbuf(self, name, shape, dtype, chunk=None):
        g = self.nc.sbuf_tensor(name, list(shape), dtype)
        t = g.__enter__()
        self._cm.append(g)
        rb = int(np.prod(shape[1:])) * dsz(dtype)
        self.rowbytes[name] = rb
        self.chunk[name] = chunk if chunk else rb
        return t

    def psum(self, name, shape, dtype=F32):
        g = self.nc.psum_tensor(name, list(shape), dtype)
        t = g.__enter__()
        self._cm.append(g)
        rb = int(np.prod(shape[1:])) * dsz(dtype)
        self.rowbytes[name] = rb
        self.chunk[name] = 2048
        self.psum_names.add(name)
        return t

    def res(self, ap):
        sp = str(ap.space)
        if "DRAM" in sp.upper():
            return []
        name = ap.tensor.name
        rb = self.rowbytes[name]
        es = dsz(ap.dtype)
        ch = self.chunk[name]
        base = (int(ap.offset) * es) % rb
        dims = [(abs(st), cnt) for (st, cnt) in ap.ap[1:] if cnt > 1]
        if not dims:
            return [(name, base // ch)]
        dims.sort()
        inner_st, inner_cnt = dims[0]
        outer = dims[1:]
        nout = 1
        for (_, c) in outer:
            nout *= c
        keys = set()
        if nout > 256:
            span = sum((c - 1) * s for (s, c) in dims)
            lo, hi = base, base + (span + 1) * es
            return [(name, c) for c in range(lo // ch, (hi - 1) // ch + 1)]
        offs = [0]
        for (s, c) in outer:
            offs = [o + i * s for o in offs for i in range(c)]
        ispan = ((inner_cnt - 1) * inner_st + 1) * es
        for o in offs:
            lo = base + o * es
            hi = lo + ispan
            for c in range(lo // ch, (hi - 1) // ch + 1):
                keys.add((name, c))
        return list(keys)

    limit = None

    def op(self, eng, fn, reads=(), writes=(), is_dma=False, defer=False):
        if self.limit is not None and len(self.all_ops) >= self.limit:
            return None
        rk = set()
        for a in reads:
            rk.update(self.res(a))
        wk = set()
        for a in writes:
            wk.update(self.res(a))
        deps = set()
        for r in rk:
            w = self.last_w.get(r)
            if w is not None:
                deps.add(w)
            if r[0] in self.psum_names:
                for t in self.readers.get(r, ()):
                    if t.eng != eng:
                        deps.add(t)
        for r in wk:
            w = self.last_w.get(r)
            if w is not None:
                deps.add(w)
            for t in self.readers.get(r, ()):
                deps.add(t)
        o = _Op(eng, fn, deps, len(self.all_ops), len(self.ops[eng]), is_dma, defer)
        self.ops[eng].append(o)
        self.all_ops.append(o)
        for r in rk:
            if r not in wk:
                self.readers.setdefault(r, []).append(o)
        for r in wk:
            self.last_w[r] = o
            self.readers[r] = []
        return o

    def mm(self, out, lhsT, rhs, start=True, stop=True, defer=None, **kw):
        if defer is None:
            defer = not stop
        return self.op("pe", lambda e: e.matmul(out, lhsT, rhs, start=start, stop=stop, **kw),
                       reads=[lhsT, rhs], writes=[out], defer=defer)

    def tr(self, out, in_, ident, defer=False):
        return self.op("pe", lambda e: e.transpose(out, in_, ident), reads=[in_, ident], writes=[out], defer=defer)

    def act(self, out, in_, func, bias=None, scale=None):
        kw = {}
        reads = [in_]
        if bias is not None:
            kw["bias"] = bias
            if not isinstance(bias, (int, float)):
                reads.append(bias)
        if scale is not None:
            kw["scale"] = scale
            if not isinstance(scale, (int, float)):
                reads.append(scale)
        return self.op("act", lambda e: e.activation(out, in_, func, **kw), reads=reads, writes=[out])

    def tt(self, out, in0, in1, op, eng="dve"):
        return self.op(eng, lambda e: e.tensor_tensor(out, in0, in1, op), reads=[in0, in1], writes=[out])

    def ts(self, out, in0, s1, s2, op0, op1=None, eng="dve"):
        reads = [in0]
        for s in (s1, s2):
            if s is not None and not isinstance(s, (int, float)):
                reads.append(s)
        if op1 is None:
            return self.op(eng, lambda e: e.tensor_scalar(out, in0, s1, s2, op0), reads=reads, writes=[out])
        return self.op(eng, lambda e: e.tensor_scalar(out, in0, s1, s2, op0, op1), reads=reads, writes=[out])

    def stt(self, out, in0, scalar, in1, op0, op1, eng="dve"):
        reads = [in0, in1]
        if not isinstance(scalar, (int, float)):
            reads.append(scalar)
        return self.op(eng, lambda e: e.scalar_tensor_tensor(out, in0, scalar, in1, op0, op1),
                       reads=reads, writes=[out])

    def copy(self, out, in_, eng="dve"):
        if eng == "act":
            return self.op("act", lambda e: e.copy(out, in_), reads=[in_], writes=[out])
        return self.op(eng, lambda e: e.tensor_copy(out, in_), reads=[in_], writes=[out])

    def memset(self, ap, val, eng="pool"):
        return self.op(eng, lambda e: e.memset(ap, val), writes=[ap])

    def dma(self, out, in_, q="sp"):
        return self.op(q, lambda e: e.dma_start(out=out, in_=in_), reads=[in_], writes=[out], is_dma=True)

    def finalize(self):
        nc = self.nc
        nxt = {}
        for e in self.ENGS:
            lst = self.ops[e]
            cur = None
            for k in range(len(lst) - 1, -1, -1):
                o = lst[k]
                if not o.is_dma and not o.defer:
                    cur = o
                nxt[o] = cur
        resolve = {}
        for o in self.all_ops:
            for d in o.deps:
                if d.is_dma or (o.eng == "pe" and d.eng == "pe"):
                    continue
                t = d
                if d.defer:
                    t2 = nxt[d]
                    if t2 is not None and t2.gidx < o.gidx:
                        t = t2
                resolve[(d, o)] = t
                t.sig = True
        sems = {e: nc.alloc_semaphore(name="c_" + e) for e in self.ENGS}
        dsems = {}
        for e in self.ENGS:
            if any(o.is_dma for o in self.ops[e]):
                dsems[e] = [nc.alloc_semaphore(name="d_%s_%d" % (e, i)) for i in range(self.NDS)]
        for e in self.ENGS:
            c = 0
            j = 0
            for o in self.ops[e]:
                if o.is_dma:
                    o.dsem = dsems[e][j % self.NDS]
                    o.dval = 16 * (j // self.NDS + 1)
                    j += 1
                elif o.sig:
                    c += 1
                    o.tok = c
        self.sig_counts = {e: sum(1 for o in self.ops[e] if o.sig) for e in self.ENGS}
        progs = self

        def emit(e, eng):
            waited = {}
            lst = progs.ops[e]
            for o in lst:
                need = {}
                for d in o.deps:
                    if e == "pe" and d.eng == "pe":
                        continue
                    if d.is_dma:
                        key, val = d.dsem, d.dval
                    else:
                        t = resolve[(d, o)]
                        key, val = sems[t.eng], t.tok
                    if need.get(key, 0) < val:
                        need[key] = val
                if o.is_dma and o.dval > 16:
                    key, val = o.dsem, o.dval - 16
                    if need.get(key, 0) < val:
                        need[key] = val
                for key, val in need.items():
                    if waited.get(key, 0) < val:
                        eng.wait_ge(key, val)
                        waited[key] = val
                ins = o.fn(eng)
                if o.is_dma:
                    ins.then_inc(o.dsem, 16)
                elif o.sig:
                    ins.then_inc(sems[e], 1)
            if e in dsems:
                last = {}
                for o in lst:
                    if o.is_dma:
                        last[o.dsem] = o.dval
                for key, val in last.items():
                    if waited.get(key, 0) < val:
                        eng.wait_ge(key, val)

        with nc.Block() as block:
            if self.ops["pe"]:
                @block.tensor
                def _(eng):
                    emit("pe", eng)
            if self.ops["act"]:
                @block.scalar
                def _(eng):
                    emit("act", eng)
            if self.ops["dve"]:
                @block.vector
                def _(eng):
                    emit("dve", eng)
            if self.ops["pool"]:
                @block.gpsimd
                def _(eng):
                    emit("pool", eng)
            if self.ops["sp"]:
                @block.sync
                def _(eng):
                    emit("sp", eng)
        return nc


T = 2048
D = 1024
KD = 8
NL = 2
NKEY = 2304
NKT = 18
EPS = 1e-6
D_FF = 2816
NFF = 22
C_AQ, C_AK, C_BQ, C_BK, C_CX, C_CB, C_CC = 0, 384, 512, 768, 1024, 1408, 1664
C_AV, C_BV, C_CZ, C_DT = 1920, 2048, 2304, 2688
MASKV = 2048.0
NEG = -30000.0


class Arena:
    def __init__(self, P, name, nbytes):
        self.P = P
        self.t = P.sbuf(name, [128, nbytes // 4], F32, chunk=256)
        self.n = nbytes
        self.ptr = 0
        self.peak = 0

    def mark(self):
        return self.ptr

    def reset(self, m=0):
        self.ptr = m

    def alloc(self, shape, dtype):
        nb = int(np.prod(shape[1:])) * dsz(dtype)
        nb = (nb + 255) // 256 * 256
        off = self.ptr
        self.ptr += nb
        self.peak = max(self.peak, self.ptr)
        assert self.ptr <= self.n, "arena overflow %d > %d" % (self.ptr, self.n)
        n_el = int(np.prod(shape[1:]))
        v = self.t[:, off // 4:(off + nb) // 4]
        if dtype != F32:
            v = v.bitcast(dtype)
        v = v[:, 0:n_el]
        if len(shape) == 3:
            v = v.rearrange("p (a b) -> p a b", b=shape[2])
        elif len(shape) == 4:
            v = v.rearrange("p (a b c) -> p a b c", b=shape[2], c=shape[3])
        return v


def colgroups(n):
    out = []
    j = 0
    while j < n:
        w = min(512, n - j)
        out.append((j, w))
        j += w
    return out


def build(stop_after=None, dbg=None, limit=None):
    dbg = dbg or {}
    nc = bass.Bass("TRN2", target_bir_lowering=False)
    P = Prog(nc)
    P.limit = limit

    def din(name, shape):
        return nc.dram_tensor(name, list(shape), F32, kind="ExternalInput").ap()

    def dout(name, shape):
        return nc.dram_tensor(name, list(shape), F32, kind="ExternalOutput").ap()

    x_d = din("x", [T, D])
    cvec_d = din("cvec", [128, 8])
    cak_d = din("cak", [NL, 256, 128])
    cav_d = din("cav", [NL, 256, 128])
    cbk_d = din("cbk", [NL, 256, 256])
    cbv_d = din("cbv", [NL, 256, 256])
    ssm0_d = din("ssm0", [NL, 2, 384, 128])
    mq_d = din("mq", [9, T])
    mk_d = din("mk", [9, NKEY])
    rope_d = [din(n, [128, T]) for n in ("cosA", "sinA", "cosB", "sinB")]
    pflag_d = din("pflag", [128, 1])
    seqf_d = din("seqf", [128, 16])
    identf_d = din("identf", [128, 128])
    trif_d = din("trif", [128, 128])
    trib_d = din("trib", [128, 128])
    mnegf_d = din("mnegf", [128, 128])
    mnegb_d = din("mnegb", [128, 128])
    permA_d = din("permA", [128, 128])
    permB_d = din("permB", [128, 128])
    bones_d = din("bones", [128, 128])
    ada_w_d = din("ada_w", [NL, D, 6 * D])
    ada_b_d = din("ada_b", [NL, 128, 48])
    w_in_d = din("w_in", [NL, D, 2700])
    aqn_d = din("aqn", [NL, 128, 1])
    akn_d = din("akn", [NL, 128, 1])
    blam_d = din("blam", [NL, 128])
    bsub_d = din("bsub", [NL, 128, 1])
    scw_d = din("scw", [NL, 128, 7, 3])
    scb_d = din("scb", [NL, 128, 7])
    alog_d = din("alog", [NL, 12])
    dtb_d = din("dtb", [NL, 12])
    ssmD_d = din("ssmD", [NL, 6])
    snw_d = din("snw", [NL, 384])
    w_out_d = din("w_out", [NL, D, D])
    ffn_up_d = din("ffn_up", [NL, D, 2 * D_FF])
    fcw_d = din("fcw", [NL, 128, 44, 3])
    fcb_d = din("fcb", [NL, 128, 44])
    ffn_down_d = din("ffn_down", [NL, D_FF, D])
    fnw_d = din("fnw", [128, 8])

    y_o = dout("y", [T, D])
    nak_o = dout("nak", [NL, T, 128])
    nav_o = dout("nav", [NL, T, 128])
    nbk_o = dout("nbk", [NL, T, 256])
    nbv_o = dout("nbv", [NL, T, 256])
    nssm_o = dout("nssm", [NL, 8, 2, 384, 128])
    dbg_o = {}
    for k, shp in dbg.items():
        dbg_o[k] = dout("dbg_" + k, shp)

    xT = P.sbuf("xT", [128, KD, T], F32, chunk=2048)
    cst = P.sbuf("cst", [128, 8, 128], F32, chunk=512)
    cstb = P.sbuf("cstb", [128, 6, 128], BF16, chunk=256)
    small = P.sbuf("small", [128, 512], F32, chunk=64)
    wsl = [P.sbuf("wsl%d" % i, [128, KD, 256], BF16, chunk=512) for i in range(4)]
    ARENA_BYTES = 116 * 1024
    A = Arena(P, "arena", ARENA_BYTES)
    ps = P.psum("ps", [128, 4096], F32)
    psb = ps[:, :].bitcast(BF16)

    identf = cst[:, 0, :]
    trif = cst[:, 1, :]
    trib = cst[:, 2, :]
    permA = cst[:, 3, :]
    permB = cst[:, 4, :]
    onesf = cst[:, 5, :]
    identb = cstb[:, 0, :]
    mnegf = cstb[:, 1, :]
    mnegb = cstb[:, 2, :]
    bones = cstb[:, 3, :]
    onesb = cstb[:, 4, :]
    tri = [trif, trib]
    mneg = [mnegf, mnegb]

    _sp = [0]

    def salloc(n):
        o = _sp[0]
        _sp[0] += n
        assert _sp[0] <= 512
        return small[:, o:o + n]

    mod = [salloc(48) for _ in range(NL)]
    opsc = [salloc(16) for _ in range(NL)]
    cv = salloc(8)
    pflag = salloc(1)
    seqf = salloc(16)
    fnw = salloc(8)

    bank_rr = [0]

    reserved = set()

    def banks(n):
        for _ in range(16):
            b = bank_rr[0]
            if b + n > 8:
                b = 0
            bank_rr[0] = (b + n) % 8
            if not any((b + i) in reserved for i in range(n)):
                return b
        raise RuntimeError("no free psum banks")

    def pbank(b, n=1):
        return ps[:, b * 512:(b + n) * 512]

    wsl_rr = [0]

    def next_wsl():
        w = wsl[wsl_rr[0] % 4]
        wsl_rr[0] += 1
        return w

    for i, dsrc in enumerate((identf_d, trif_d, trib_d, permA_d, permB_d)):
        P.dma(cst[:, i, :], dsrc[:, :])
    P.memset(cst[:, 5, :], 1.0)
    P.dma(cstb[:, 0, :], identf_d[:, :], q="pool")
    P.dma(cstb[:, 1, :], mnegf_d[:, :], q="pool")
    P.dma(cstb[:, 2, :], mnegb_d[:, :], q="pool")
    P.dma(cstb[:, 3, :], bones_d[:, :], q="pool")
    P.memset(cstb[:, 4, :], 1.0)
    P.dma(cv, cvec_d[:, :])
    P.dma(pflag, pflag_d[:, :])
    P.dma(seqf, seqf_d[:, :])
    P.dma(fnw, fnw_d[:, :])

    m0 = A.mark()
    scv = A.alloc([128, 8], BF16)
    adab = A.alloc([128, 48], F32)
    P.act(scv, cv, AF.Silu)
    aslab = [A.alloc([128, KD, 512], BF16) for _ in range(2)]
    for l in range(NL):
        pb = banks(1)
        pm = pbank(pb)
        awv = ada_w_d[l].rearrange("(k p) n -> p k n", p=128)
        for s in range(12):
            sl = aslab[s % 2]
            P.dma(sl[:, :, :], awv[:, :, s * 512:(s + 1) * 512], q="pool")
            for jj in range(4):
                j = s * 4 + jj
                for k in range(KD):
                    P.mm(pm[:, j:j + 1], sl[:, k, jj * 128:(jj + 1) * 128], scv[:, k:k + 1],
                         start=(k == 0), stop=(k == KD - 1))
        P.dma(adab, ada_b_d[l])
        P.tt(mod[l], pm[:, 0:48], adab, ALU.add)
        P.ts(opsc[l][:, 0:8], mod[l][:, 8:16], 1.0, None, ALU.add)
        P.ts(opsc[l][:, 8:16], mod[l][:, 32:40], 1.0, None, ALU.add)
    A.reset(m0)

    if "mod" in dbg_o:
        P.dma(dbg_o["mod"][:, 0:48], mod[0])
        P.dma(dbg_o["mod"][:, 48:96], mod[1])
    if stop_after == "S0":
        P.finalize()
        return nc, P
    m0 = A.mark()
    xin = [A.alloc([128, D], F32) for _ in range(2)]
    for tt_ in range(16):
        xi = xin[tt_ % 2]
        P.dma(xi, x_d[tt_ * 128:(tt_ + 1) * 128, :])
        for hb in range(2):
            pb = banks(1)
            for k4 in range(4):
                k = hb * 4 + k4
                P.tr(pbank(pb)[:, k4 * 128:(k4 + 1) * 128], xi[:, k * 128:(k + 1) * 128], identf)
            P.copy(xT[:, hb * 4:(hb + 1) * 4, tt_ * 128:(tt_ + 1) * 128],
                   pbank(pb).rearrange("p (a b) -> p a b", b=128), eng=("act" if hb else "dve"))
    A.reset(m0)

    if stop_after == "S1":
        if "xT" in dbg_o:
            P.dma(dbg_o["xT"], xT[:, :, :])
        P.finalize()
        return nc, P
    def adanorm(l, which, t0, n, hT, col0=0):
        m = A.mark()
        sq = [A.alloc([128, n], BF16) for _ in range(2)]
        lnv = A.alloc([128, n], F32)
        tmp = [A.alloc([128, n], F32) for _ in range(2)]
        nbk = (n + 511) // 512
        pb = banks(nbk)
        pss = pbank(pb, nbk)
        for k in range(KD):
            P.act(sq[k % 2], xT[:, k, t0:t0 + n], AF.Square)
            for (j0, w) in colgroups(n):
                P.mm(pss[:, j0:j0 + w], onesb, sq[k % 2][:, j0:j0 + w], start=(k == 0), stop=(k == KD - 1))
        P.act(lnv, pss[:, 0:n], AF.Ln, bias=EPS, scale=1.0 / D)
        P.act(lnv, lnv, AF.Exp, scale=-0.5)
        shb = 0 if which == 0 else 24
        for k in range(KD):
            P.tt(tmp[k % 2], xT[:, k, t0:t0 + n], lnv, ALU.mult)
            P.act(hT[:, k, col0:col0 + n], tmp[k % 2], AF.Identity,
                  bias=mod[l][:, shb + k:shb + k + 1], scale=opsc[l][:, which * 8 + k:which * 8 + k + 1])
        A.reset(m)

    def load_wslab(l, c0, ncols):
        wb = next_wsl()
        wv = w_in_d[l].rearrange("(k p) n -> p k n", p=128)
        P.dma(wb[:, :, 0:ncols], wv[:, :, c0:c0 + ncols], q="pool")
        return wb

    def proj_feat(l, c0, ntiles, hT, ntok, consume, hcol0=0):
        i = 0
        pend = None
        while i < ntiles:
            nt = min(2, ntiles - i)
            wb = load_wslab(l, c0 + i * 128, nt * 128)
            for j in range(nt):
                nbk = (ntok + 511) // 512
                pb = banks(nbk)
                pt = pbank(pb, nbk)
                for (j0, w) in colgroups(ntok):
                    for k in range(KD):
                        P.mm(pt[:, j0:j0 + w], wb[:, k, j * 128:(j + 1) * 128], hT[:, k, hcol0 + j0:hcol0 + j0 + w],
                             start=(k == 0), stop=(k == KD - 1))
                if pend is not None:
                    reserved.update(range(pb, pb + nbk))
                    consume(*pend)
                    reserved.difference_update(range(pb, pb + nbk))
                pend = (i + j, pt)
            i += nt
        if pend is not None:
            consume(*pend)

    def dma_dbg(name, src):
        if name in dbg_o:
            P.dma(dbg_o[name], src)

    def layer_params(l):
        lp = {}
        lp["aqn"] = A.alloc([128, 1], F32)
        lp["akn"] = A.alloc([128, 1], F32)
        lp["bsub"] = A.alloc([128, 1], F32)
        lp["scw"] = A.alloc([128, 7, 3], F32)
        lp["scb"] = A.alloc([128, 7], F32)
        lp["nscw0"] = A.alloc([128, 7], F32)
        lp["nscw2"] = A.alloc([128, 7], F32)
        lp["fcw"] = A.alloc([128, 44, 3], F32)
        lp["fcb"] = A.alloc([128, 44], F32)
        lp["nfcw0"] = A.alloc([128, 44], F32)
        lp["nfcw2"] = A.alloc([128, 44], F32)
        lp["arow"] = A.alloc([128, 12], F32)
        lp["dtb"] = A.alloc([128, 12], F32)
        lp["drow"] = A.alloc([128, 6], F32)
        lp["snw"] = A.alloc([128, 384], F32)
        lp["nlam"] = A.alloc([128, 1], F32)
        blam = A.alloc([128, 128], F32)
        lt = A.alloc([128, 4], F32)
        P.dma(lp["aqn"], aqn_d[l])
        P.dma(lp["akn"], akn_d[l])
        P.dma(lp["bsub"], bsub_d[l])
        P.dma(lp["scw"], scw_d[l])
        P.dma(lp["scb"], scb_d[l])
        P.dma(lp["fcw"], fcw_d[l])
        P.dma(lp["fcb"], fcb_d[l])
        P.dma(lp["arow"], alog_d[l:l + 1, :].partition_broadcast(128).rearrange("p a b -> p (a b)"))
        P.dma(lp["dtb"], dtb_d[l:l + 1, :].partition_broadcast(128).rearrange("p a b -> p (a b)"))
        P.dma(lp["drow"], ssmD_d[l:l + 1, :].partition_broadcast(128).rearrange("p a b -> p (a b)"))
        P.dma(lp["snw"], snw_d[l:l + 1, :].partition_broadcast(128).rearrange("p a b -> p (a b)"))
        P.dma(blam, blam_d[l:l + 1, :].partition_broadcast(128).rearrange("p a b -> p (a b)"))
        lam_init = 0.8 - 0.6 * math.exp(-0.3 * l)
        P.act(lp["arow"], lp["arow"], AF.Exp)
        P.ts(lp["arow"], lp["arow"], -1.0, None, ALU.mult)
        bl = blam.rearrange("p (a b) -> p a b", b=32)
        pr = A.alloc([128, 2, 32], F32)
        P.tt(pr[:, 0, :], bl[:, 0, :], bl[:, 1, :], ALU.mult)
        P.tt(pr[:, 1, :], bl[:, 2, :], bl[:, 3, :], ALU.mult)
        P.op("dve", lambda e: e.reduce_sum(lt[:, 0:2], pr, axis=AX.X), reads=[pr], writes=[lt[:, 0:2]])
        P.act(lt[:, 0:2], lt[:, 0:2], AF.Exp)
        P.tt(lt[:, 2:3], lt[:, 1:2], lt[:, 0:1], ALU.subtract)
        P.ts(lp["nlam"], lt[:, 2:3], -lam_init, None, ALU.add)
        P.ts(lp["bsub"], lp["bsub"], 1.0 - lam_init, None, ALU.mult)
        for (dst, src, tap, n) in ((lp["nscw0"], lp["scw"], 0, 7), (lp["nscw2"], lp["scw"], 2, 7),
                                   (lp["nfcw0"], lp["fcw"], 0, 44), (lp["nfcw2"], lp["fcw"], 2, 44)):
            P.ts(dst, src[:, :, tap], pflag[:, 0:1], -1.0, ALU.mult, ALU.mult)
        return lp

    def conv_evac(U, a, n, w3, bcol, pt):
        P.copy(U, pt[:, 0:n + 2], eng="act")
        P.act(a[:, 0:n], pt[:, 1:n + 1], AF.Identity, bias=bcol, scale=w3[:, 1:2])

    def conv_tile(U, a, n, w3, bcol, nw0, nw2, tstart, pt, evac=True):
        if evac:
            conv_evac(U, a, n, w3, bcol, pt)
        if tstart == 0:
            P.memset(U[:, 0:1], 0.0, eng="pool")
        if tstart + n == T:
            P.memset(U[:, n + 1:n + 2], 0.0, eng="pool")
        P.stt(a[:, 0:n], U[:, 0:n], w3[:, 0:1], a[:, 0:n], ALU.mult, ALU.add)
        P.stt(a[:, 0:n], U[:, 2:n + 2], w3[:, 2:3], a[:, 0:n], ALU.mult, ALU.add)
        starts = [b - tstart for b in range(256, T, 256) if tstart <= b < tstart + n]
        ends = [b - 1 - tstart for b in range(256, T, 256) if tstart <= b - 1 < tstart + n]
        if starts:
            s0, cnt = starts[0], len(starts)
            av = a[:, s0:s0 + (cnt - 1) * 256 + 1:256] if cnt > 1 else a[:, s0:s0 + 1]
            uv = U[:, s0:s0 + (cnt - 1) * 256 + 1:256] if cnt > 1 else U[:, s0:s0 + 1]
            P.stt(av, uv, nw0, av, ALU.mult, ALU.add)
        if ends:
            s0, cnt = ends[0], len(ends)
            av = a[:, s0:s0 + (cnt - 1) * 256 + 1:256] if cnt > 1 else a[:, s0:s0 + 1]
            uv = U[:, s0 + 2:s0 + 2 + (cnt - 1) * 256 + 1:256] if cnt > 1 else U[:, s0 + 2:s0 + 3]
            P.stt(av, uv, nw2, av, ALU.mult, ALU.add)

    for l in range(NL):
        A.reset(0)
        lp = layer_params(l)
        mLP = A.mark()
        yT = A.alloc([128, 16, 3, 128], BF16)
        mL = A.mark()
        if stop_after == "LP":
            if "lp" in dbg_o:
                P.dma(dbg_o["lp"][:, 0:12], lp["arow"])
                P.dma(dbg_o["nlam"], lp["nlam"])
                P.dma(dbg_o["lp"][:, 13:20], lp["nscw0"])
                P.dma(dbg_o["lp"][:, 20:26], lp["drow"])
            break

        xcT = A.alloc([128, 3, T], BF16)
        BT = A.alloc([128, 2, T], BF16)
        CT = A.alloc([128, 2, T], BF16)
        zg = A.alloc([128, 16, 384], BF16)
        dtt = A.alloc([128, 16, 12], F32)
        mS = A.mark()
        hT = A.alloc([128, KD, 514], BF16)
        P.memset(hT, 0.0, eng="pool")
        Ub = [A.alloc([128, 514], F32) for _ in range(2)]
        ab = [A.alloc([128, 512], F32) for _ in range(2)]
        dts = A.alloc([128, 4, 12], F32)
        cnt = [0]
        for c in range(4):
            c0 = c * 512
            tlo = max(c0 - 1, 0)
            thi = min(c0 + 513, T)
            adanorm(l, 0, tlo, thi - tlo, hT, col0=tlo - (c0 - 1))

            def cons_xbc(f, pt, c0=c0):
                U = Ub[cnt[0] % 2]
                a = ab[cnt[0] % 2]
                cnt[0] += 1
                conv_evac(U, a, 512, lp["scw"][:, f, :], lp["scb"][:, f:f + 1], pt)
                for fn_ in pend_silu:
                    fn_()
                del pend_silu[:]
                conv_tile(U, a, 512, lp["scw"][:, f, :], lp["scb"][:, f:f + 1],
                          lp["nscw0"][:, f:f + 1], lp["nscw2"][:, f:f + 1], c0, pt, evac=False)
                if f < 3:
                    dst = xcT[:, f, c0:c0 + 512]
                elif f < 5:
                    dst = BT[:, f - 3, c0:c0 + 512]
                else:
                    dst = CT[:, f - 5, c0:c0 + 512]
                pend_silu.append(lambda dst=dst, a=a: P.act(dst, a, AF.Silu))

            pend_silu = []
            proj_feat(l, C_CX, 7, hT, 514, cons_xbc)
            for fn_ in pend_silu:
                fn_()
            del pend_silu[:]
            wz = [load_wslab(l, C_CZ, 256), load_wslab(l, C_CZ + 256, 140)]
            pdt = pbank(banks(1))
            for t4 in range(4):
                pb = banks(1)
                pt = pbank(pb)
                lh = lambda k, t4=t4: hT[:, k, 1 + t4 * 128:1 + (t4 + 1) * 128]
                for k in range(KD):
                    P.mm(pt[:, 0:256], lh(k), wz[0][:, k, 0:256], start=(k == 0), stop=(k == KD - 1))
                for k in range(KD):
                    P.mm(pt[:, 256:384], lh(k), wz[1][:, k, 0:128], start=(k == 0), stop=(k == KD - 1),
                         skip_group_check=True)
                for k in range(KD):
                    P.mm(pdt[:, t4 * 16:t4 * 16 + 12], lh(k), wz[1][:, k, 128:140], start=(k == 0), stop=(k == KD - 1),
                         skip_group_check=True)
                P.act(zg[:, c * 4 + t4, :], pt[:, 0:384], AF.Silu)
            P.tt(dts, pdt[:, 0:64].rearrange("p (t j) -> p t j", j=16)[:, :, 0:12],
                 lp["dtb"].unsqueeze(1).to_broadcast([128, 4, 12]), ALU.add)
            P.act(dts, dts, AF.Exp)
            P.act(dtt[:, c * 4:(c + 1) * 4, :], dts, AF.Ln, bias=1.0)
        A.reset(mS)
        if "xcT" in dbg_o:
            tmpf = A.alloc([128, 3, T], F32)
            P.copy(tmpf, xcT)
            P.dma(dbg_o["xcT"], tmpf)
            A.reset(mS)
        if stop_after == "M1" and l == 0:
            break

        Sst = [A.alloc([128, 384], F32) for _ in range(2)]
        Sent = A.alloc([128, 384], BF16)
        Sbe = A.alloc([128, 16, 384], BF16)
        st_in = A.alloc([128, 3, 128], F32)
        for d_ in range(2):
            P.dma(st_in, ssm0_d[l, d_].rearrange("(j p) n -> p j n", p=128))
            pb = banks(1)
            for j in range(3):
                P.tr(pbank(pb)[:, j * 128:(j + 1) * 128], st_in[:, j, :], identf)
            P.copy(Sst[d_], pbank(pb)[:, 0:384])
        so = [A.alloc([128, 3, 128], F32) for _ in range(2)]
        xtok_b = [A.alloc([128, 384], BF16) for _ in range(2)]
        btok_b = [A.alloc([128, 256], BF16) for _ in range(2)]
        sm = [[A.alloc([128, 64], F32) for _ in range(2)] for _ in range(2)]
        xdt_b = [[A.alloc([128, 384], BF16) for _ in range(2)] for _ in range(2)]
        xdte_b = [[A.alloc([128, 384], BF16) for _ in range(2)] for _ in range(2)]
        dec_b = [A.alloc([128, 12, 128], F32) for _ in range(2)]
        sc_b = [A.alloc([128, 12, 128], BF16) for _ in range(2)]
        yc = [A.alloc([128, 384], F32) for _ in range(3)]
        ynb = A.alloc([128, 384], BF16)
        rs = A.alloc([128, 4], F32)
        scnt = [0]

        def prep_common(ci, par):
            t0 = ci * 128
            pb = banks(1)
            pv = psb[:, pb * 1024:(pb + 1) * 1024]
            for j in range(3):
                P.tr(pv[:, j * 128:(j + 1) * 128], xcT[:, j, t0:t0 + 128], identb)
            pb2 = banks(1)
            pv2 = psb[:, pb2 * 1024:(pb2 + 1) * 1024]
            for g in range(2):
                P.tr(pv2[:, g * 128:(g + 1) * 128], BT[:, g, t0:t0 + 128], identb)
            P.copy(xtok_b[par], pv[:, 0:384])
            P.copy(btok_b[par], pv2[:, 0:256], eng="act")

        def prep_dir(ci, par, d_, full):
            s_ = sm[par][d_]
            a = s_[:, 0:6]
            cs = s_[:, 6:12]
            ncs = s_[:, 12:18]
            dte = s_[:, 18:24]
            dtot = s_[:, 24:30]
            ecs = s_[:, 30:36]
            w_ = s_[:, 36:42]
            dtv = dtt[:, ci, d_ * 6:(d_ + 1) * 6]
            P.tt(a, dtv, lp["arow"][:, d_ * 6:(d_ + 1) * 6], ALU.mult)
            pb = banks(1)
            pt = pbank(pb)
            P.mm(pt[:, 0:6], tri[d_], a)
            P.mm(pt[:, 8:14], onesf, a)
            P.copy(cs, pt[:, 0:6])
            P.ts(ncs, cs, -1.0, None, ALU.mult)
            P.tt(dte, pt[:, 8:14], ncs, ALU.add)
            P.act(dte, dte, AF.Exp)
            P.act(dtot, pt[:, 8:14], AF.Exp)
            P.tt(w_, dtv, dte, ALU.mult)
            x3 = xtok_b[par].rearrange("p (h d) -> p h d", d=64)
            P.tt(xdte_b[par][d_].rearrange("p (h d) -> p h d", d=64), x3,
                 w_.unsqueeze(2).to_broadcast([128, 6, 64]), ALU.mult)
            if full:
                P.act(ecs, cs, AF.Exp)
                P.tt(xdt_b[par][d_].rearrange("p (h d) -> p h d", d=64), x3,
                     dtv.unsqueeze(2).to_broadcast([128, 6, 64]), ALU.mult)
            return dict(a=a, cs=cs, ncs=ncs, dte=dte, dtot=dtot, ecs=ecs)

        def chunk_state_update(ci, par, d_, q):
            pb = banks(1)
            pt = pbank(pb)
            for g in range(2):
                P.mm(pt[:, g * 192:(g + 1) * 192], btok_b[par][:, g * 128:(g + 1) * 128],
                     xdte_b[par][d_][:, g * 192:(g + 1) * 192])
            S3 = Sst[d_].rearrange("p (h d) -> p h d", d=64)
            P.tt(S3, S3, q["dtot"].unsqueeze(2).to_broadcast([128, 6, 64]), ALU.mult)
            P.tt(Sst[d_], Sst[d_], pt[:, 0:384], ALU.add)
            is_end = (ci % 2 == 1) if d_ == 0 else (ci % 2 == 0)
            if is_end:
                seq = ci // 2
                pb2 = banks(1)
                for j in range(3):
                    P.tr(pbank(pb2)[:, j * 128:(j + 1) * 128], Sst[d_][:, j * 128:(j + 1) * 128], identf)
                sob = so[scnt[0] % 2]
                scnt[0] += 1
                P.copy(sob, pbank(pb2)[:, 0:384].rearrange("p (j n) -> p j n", n=128), eng="act")
                P.dma(nssm_o[l, seq, d_].rearrange("(j p) n -> p j n", p=128), sob)
                bnd = ci if d_ == 0 else ci - 1
                if 0 <= bnd <= 14:
                    P.ts(Sst[d_], Sst[d_], seqf[:, bnd:bnd + 1], None, ALU.mult)

        qb_ = {}
        order = list(reversed(range(16)))
        prep_common(order[0], order[0] % 2)
        qb_[order[0]] = prep_dir(order[0], order[0] % 2, 1, False)
        for oi, ci in enumerate(order):
            par = ci % 2
            if oi + 1 < 16:
                cn = order[oi + 1]
                prep_common(cn, cn % 2)
                qb_[cn] = prep_dir(cn, cn % 2, 1, False)
            P.copy(Sbe[:, ci, :], Sst[1], eng="pool")
            chunk_state_update(ci, par, 1, qb_.pop(ci))

        stA = {}

        def stage_A(ci):
            par = ci % 2
            t0 = ci * 128
            prep_common(ci, par)
            qs = [prep_dir(ci, par, 0, True), prep_dir(ci, par, 1, True)]
            pcb = banks(1)
            for g in range(2):
                P.mm(pbank(pcb)[:, g * 128:(g + 1) * 128], BT[:, g, t0:t0 + 128], CT[:, g, t0:t0 + 128])
            pcs = banks(3)
            pcsv = pbank(pcs, 3).rearrange("p (j l) -> p j l", l=128)
            for d_ in range(2):
                for h in range(6):
                    j = d_ * 6 + h
                    P.mm(pcsv[:, j, :], qs[d_]["a"][:, h:h + 1].to_broadcast([128, 128]), tri[d_],
                         start=True, stop=False, skip_group_check=True)
                    P.mm(pcsv[:, j, :], identb, mneg[d_], start=False, stop=True, skip_group_check=True)
            dec = dec_b[par]
            scb_ = sc_b[par]
            for d_ in range(2):
                for h in range(6):
                    j = d_ * 6 + h
                    P.act(dec[:, j, :], pcsv[:, j, :], AF.Exp, bias=qs[d_]["ncs"][:, h:h + 1])
            for d_ in range(2):
                for g in range(2):
                    j0 = d_ * 6 + g * 3
                    P.tt(scb_[:, j0:j0 + 3, :],
                         pbank(pcb)[:, g * 128:(g + 1) * 128].unsqueeze(1).to_broadcast([128, 3, 128]),
                         dec[:, j0:j0 + 3, :], ALU.mult)
            stA[ci] = qs

        def stage_B(ci):
            par = ci % 2
            t0 = ci * 128
            qs = stA.pop(ci)
            scb_ = sc_b[par]
            P.copy(Sent, Sst[0], eng="pool")
            pyd = banks(1)
            for h in range(6):
                for d_ in range(2):
                    j = d_ * 6 + h
                    P.mm(pbank(pyd)[:, h * 64:(h + 1) * 64], scb_[:, j, :], xdt_b[par][d_][:, h * 64:(h + 1) * 64],
                         start=(d_ == 0), stop=(d_ == 1), skip_group_check=True)
            pyo = [banks(1), banks(1)]
            for d_ in range(2):
                src = Sent if d_ == 0 else Sbe[:, ci, :]
                for g in range(2):
                    P.mm(pbank(pyo[d_])[:, g * 192:(g + 1) * 192], CT[:, g, t0:t0 + 128], src[:, g * 192:(g + 1) * 192])
            y0, y1, y2 = yc
            for d_, yy in ((0, y0), (1, y1)):
                P.tt(yy.rearrange("p (h d) -> p h d", d=64),
                     pbank(pyo[d_])[:, 0:384].rearrange("p (h d) -> p h d", d=64),
                     qs[d_]["ecs"].unsqueeze(2).to_broadcast([128, 6, 64]), ALU.mult)
            P.tt(y0, y0, y1, ALU.add)
            P.tt(y0, pbank(pyd)[:, 0:384], y0, ALU.add)
            P.tt(y1.rearrange("p (h d) -> p h d", d=64), xtok_b[par].rearrange("p (h d) -> p h d", d=64),
                 lp["drow"].unsqueeze(2).to_broadcast([128, 6, 64]), ALU.mult)
            P.tt(y0, y0, y1, ALU.add)
            dma_dbg("yraw%d" % ci, y0)
            P.tt(y0, y0, zg[:, ci, :], ALU.mult)
            P.tt(y2, y0, y0, ALU.mult)
            P.op("dve", lambda e, y2=y2: e.reduce_sum(rs[:, 0:1], y2, axis=AX.X), reads=[y2], writes=[rs[:, 0:1]])
            P.act(rs[:, 1:2], rs[:, 0:1], AF.Ln, bias=EPS, scale=1.0 / 384)
            P.act(rs[:, 1:2], rs[:, 1:2], AF.Exp, scale=-0.5)
            P.stt(ynb, y0, rs[:, 1:2], lp["snw"], ALU.mult, ALU.mult)
            pb = banks(1)
            pv = psb[:, pb * 1024:(pb + 1) * 1024]
            for j in range(3):
                P.tr(pv[:, j * 128:(j + 1) * 128], ynb[:, j * 128:(j + 1) * 128], identb)
            P.copy(yT[:, ci, :, :], pv[:, 0:384].rearrange("p (j t) -> p j t", t=128), eng="act")
            chunk_state_update(ci, par, 0, qs[0])

        stage_A(0)
        for ci in range(16):
            if ci + 1 < 16:
                stage_A(ci + 1)
            stage_B(ci)
        A.reset(mL)
        if stop_after == "M2" and l == 0:
            break

        KA = A.alloc([128, 2, NKEY], BF16)
        KB = A.alloc([128, 4, NKEY], BF16)
        VX = A.alloc([128, NKT, 576], BF16)
        mK = A.mark()
        P.memset(KA, 0.0, eng="pool")
        P.memset(KB, 0.0, eng="pool")
        P.dma(KA[64:73, :, :], mk_d[:, :].unsqueeze(1).to_broadcast([9, 2, NKEY]), q="pool")
        P.dma(KB[32:41, :, :], mk_d[:, :].unsqueeze(1).to_broadcast([9, 4, NKEY]), q="pool")
        P.dma(KB[96:105, :, :], mk_d[:, :].unsqueeze(1).to_broadcast([9, 4, NKEY]), q="pool")
        VX4 = VX.rearrange("p k (a s d) -> p k a s d", s=3, d=64)
        P.memset(VX4[:, :, :, 1, :], 1.0, eng="pool")
        hT = A.alloc([128, KD, 512], BF16)
        rope = [A.alloc([128, 512], F32) for _ in range(4)]
        sqb = A.alloc([128, 512], BF16)
        lnv = A.alloc([128, 512], F32)
        kn = A.alloc([128, 512], F32)
        t1 = A.alloc([128, 512], F32)
        kr = A.alloc([128, 512], F32)
        ktr = [A.alloc([128, 512], F32) for _ in range(2)]
        vst = [A.alloc([128, 384], F32) for _ in range(2)]
        cin = A.alloc([128, 2, 256], F32)
        ktc = [0]

        def rope_apply(src, dst, perm, cosT, sinT):
            pb = banks(1)
            P.mm(pbank(pb), perm, src)
            P.tt(t1, src, cosT, ALU.mult)
            P.tt(dst, pbank(pb), sinT, ALU.mult)
            P.tt(dst, dst, t1, ALU.add)

        def head_norm(pt, wcol, dst):
            P.act(sqb, pt, AF.Square)
            pb = banks(1)
            P.mm(pbank(pb), bones, sqb)
            P.act(lnv, pbank(pb), AF.Ln, bias=EPS, scale=1.0 / 64)
            P.act(lnv, lnv, AF.Exp, scale=-0.5)
            P.stt(dst, pt, wcol, lnv, ALU.mult, ALU.mult)

        def out_tok(src, dst_d, c0, ncol):
            pb = banks(1)
            for t4 in range(4):
                P.tr(pbank(pb)[:, t4 * 128:(t4 + 1) * 128], src[:, t4 * 128:(t4 + 1) * 128], identf)
            kt_ = ktr[ktc[0] % 2]
            ktc[0] += 1
            P.copy(kt_, pbank(pb), eng="act")
            P.dma(dst_d[c0:c0 + 512, ncol:ncol + 128].rearrange("(t p) f -> p t f", p=128),
                  kt_.rearrange("p (t f) -> p t f", f=128))

        for kt2 in range(2):
            P.dma(cin[:, 0, 0:128], cak_d[l, kt2 * 128:(kt2 + 1) * 128, :])
            pb = banks(1)
            P.tr(pbank(pb)[:, 0:128], cin[:, 0, 0:128], identf)
            kc = T + kt2 * 128
            P.copy(KA[0:64, 0, kc:kc + 128], pbank(pb)[0:64, 0:128])
            P.copy(KA[0:64, 1, kc:kc + 128], pbank(pb)[64:128, 0:128])
            P.dma(cin[:, 1, :], cbk_d[l, kt2 * 128:(kt2 + 1) * 128, :])
            pb = banks(1)
            for j in range(2):
                P.tr(pbank(pb)[:, j * 128:(j + 1) * 128], cin[:, 1, j * 128:(j + 1) * 128], identf)
            for h in range(4):
                j = h // 2
                r0 = (h % 2) * 64
                P.copy(KB[0:32, h, kc:kc + 128], pbank(pb)[r0:r0 + 32, j * 128:(j + 1) * 128])
                P.copy(KB[64:96, h, kc:kc + 128], pbank(pb)[r0 + 32:r0 + 64, j * 128:(j + 1) * 128])
            P.dma(VX4[:, 16 + kt2, 0, 0:3:2, :], cav_d[l, kt2 * 128:(kt2 + 1) * 128, :].rearrange("p (s d) -> p s d", d=64), q="pool")
            for a_ in range(2):
                P.dma(VX4[:, 16 + kt2, 1 + a_, 0:3:2, :],
                      cbv_d[l, kt2 * 128:(kt2 + 1) * 128, a_ * 128:(a_ + 1) * 128].rearrange("p (s d) -> p s d", d=64), q="pool")

        for c in range(4):
            c0 = c * 512
            adanorm(l, 0, c0, 512, hT)
            for i in (0, 2):
                P.dma(rope[i], rope_d[i][:, c0:c0 + 512])
                P.dma(rope[i + 1], rope_d[i + 1][:, c0:c0 + 512])

            def cons_k(f, pt, c0=c0):
                if f == 0:
                    head_norm(pt, lp["akn"][:, 0:1], kn)
                    rope_apply(kn, kr, permA, rope[0], rope[1])
                    P.copy(KA[0:64, 0, c0:c0 + 512], kr[0:64, :], eng="pool")
                    P.copy(KA[0:64, 1, c0:c0 + 512], kr[64:128, :], eng="pool")
                    out_tok(kr, nak_o[l], c0, 0)
                else:
                    P.copy(kn, pt, eng="act")
                    rope_apply(kn, kr, permB, rope[2], rope[3])
                    for hh in range(2):
                        h = (f - 1) * 2 + hh
                        r0 = hh * 64
                        P.copy(KB[0:32, h, c0:c0 + 512], kr[r0:r0 + 32, :], eng="dve")
                        P.copy(KB[64:96, h, c0:c0 + 512], kr[r0 + 32:r0 + 64, :], eng="dve")
                    out_tok(kr, nbk_o[l], c0, (f - 1) * 128)

            proj_feat(l, C_AK, 1, hT, 512, cons_k)
            proj_feat(l, C_BK, 2, hT, 512, lambda f, pt: cons_k(f + 1, pt))
            wv_ = [load_wslab(l, C_AV, 256), load_wslab(l, C_AV + 256, 128)]
            for t4 in range(4):
                pb = banks(1)
                pt = pbank(pb)
                for k in range(KD):
                    P.mm(pt[:, 0:256], hT[:, k, t4 * 128:(t4 + 1) * 128], wv_[0][:, k, 0:256],
                         start=(k == 0), stop=(k == KD - 1))
                for k in range(KD):
                    P.mm(pt[:, 256:384], hT[:, k, t4 * 128:(t4 + 1) * 128], wv_[1][:, k, 0:128],
                         start=(k == 0), stop=(k == KD - 1), skip_group_check=True)
                vs = vst[t4 % 2]
                P.copy(vs, pt[:, 0:384], eng="act")
                tk = c * 4 + t4
                P.dma(nav_o[l, tk * 128:(tk + 1) * 128, :], vs[:, 0:128])
                P.dma(nbv_o[l, tk * 128:(tk + 1) * 128, :], vs[:, 128:384])
                P.copy(VX4[:, tk, :, 0:3:2, :], vs.rearrange("p (a s d) -> p a s d", s=2, d=64), eng="pool")
        A.reset(mK)
        if stop_after == "M3" and l == 0:
            break

        hT = A.alloc([128, KD, 512], BF16)
        sqb = A.alloc([128, 512], BF16)
        lnv = A.alloc([128, 512], F32)
        QA = A.alloc([128, 6, 512], BF16)
        QB = A.alloc([128, 4, 512], BF16)
        mixT = A.alloc([128, 5, 512], BF16)
        bo = A.alloc([128, 512], F32)
        mOv = A.mark()
        rope = [A.alloc([128, 512], F32) for _ in range(4)]
        kn = A.alloc([128, 512], F32)
        t1 = A.alloc([128, 512], F32)
        kr2 = [A.alloc([128, 512], F32) for _ in range(2)]
        krc = [0]
        A.reset(mOv)
        PT = [A.alloc([128, 1024], BF16) for _ in range(3)]
        rz = A.alloc([128, 2, 512], F32)
        ob = A.alloc([128, 2, 512], F32)
        A.reset(mOv)
        P.memset(QA, 0.0, eng="pool")
        P.memset(QB, 0.0, eng="pool")
        sA = 1.0 / 8.0
        sB = 32.0 ** -0.5
        ptc = [0]
        adanorm(l, 0, 0, 512, hT)
        for c in range(4):
            c0 = c * 512
            for i in range(4):
                P.dma(rope[i], rope_d[i][:, c0:c0 + 512])
            P.dma(QA[64:73, :, :], mq_d[:, c0:c0 + 512].unsqueeze(1).to_broadcast([9, 6, 512]), q="pool")
            P.dma(QB[32:41, :, :], mq_d[:, c0:c0 + 512].unsqueeze(1).to_broadcast([9, 4, 512]), q="pool")
            P.dma(QB[96:105, :, :], mq_d[:, c0:c0 + 512].unsqueeze(1).to_broadcast([9, 4, 512]), q="pool")

            def cons_qa(f, pt):
                head_norm(pt, lp["aqn"][:, 0:1], kn)
                kr = kr2[krc[0] % 2]
                krc[0] += 1
                rope_apply(kn, kr, permA, rope[0], rope[1])
                P.copy(QA[0:64, f, :], kr[0:64, :], eng="pool")
                P.copy(QA[0:64, f + 3, :], kr[64:128, :], eng="pool")

            def cons_qb(f, pt):
                P.copy(kn, pt, eng="act")
                kr = kr2[krc[0] % 2]
                krc[0] += 1
                rope_apply(kn, kr, permB, rope[2], rope[3])
                for hh in range(2):
                    h = f * 2 + hh
                    r0 = hh * 64
                    P.copy(QB[0:32, h, :], kr[r0:r0 + 32, :], eng="dve")
                    P.copy(QB[64:96, h, :], kr[r0 + 32:r0 + 64, :], eng="dve")

            proj_feat(l, C_AQ, 3, hT, 512, cons_qa)
            proj_feat(l, C_BQ, 2, hT, 512, cons_qb)

            groups = [("A", f) for f in range(3)] + [("B", h) for h in range(4)]
            def group_units(kind, idx):
                if kind == "A":
                    return ([(KA[0:73, 0, :], QA[0:73, idx, :], 0, 128),
                             (KA[0:73, 1, :], QA[0:73, idx + 3, :], 64, 192)], sA)
                h = idx
                vc = (1 + h // 2) * 192 + (0 if h % 2 == 0 else 64)
                return ([(KB[0:41, h, :], QB[0:41, h, :], vc, vc + 128),
                         (KB[64:105, h, :], QB[64:105, h, :], vc, vc + 128)], sB)

            gunits = [group_units(k_, i_) for (k_, i_) in groups]
            steps = [(gi, kt) for gi in range(len(groups)) for kt in range(NKT)]
            pts = {}
            pending = []

            def emit_S(si):
                gi, kt = steps[si]
                units, scl = gunits[gi]
                sb_ = 2 * (si % 2)
                for u, (Kt, Qt, v0, v1) in enumerate(units):
                    P.mm(pbank(sb_ + u), Kt[:, kt * 128:(kt + 1) * 128], Qt)
                pt_ = PT[si % 3]
                pts[si] = pt_
                P.act(pt_, pbank(sb_, 2), AF.Exp, scale=scl)

            def emit_AV(si):
                gi, kt = steps[si]
                units, scl = gunits[gi]
                ob_ = 4 + 2 * (gi % 2)
                pt_ = pts.pop(si)
                for u, (Kt, Qt, v0, v1) in enumerate(units):
                    P.mm(pbank(ob_ + u), VX[:, kt, v0:v1], pt_[:, u * 512:(u + 1) * 512],
                         start=(kt == 0), stop=(kt == NKT - 1))
                if kt == NKT - 1:
                    evac(gi, si)

            def evac(gi, si):
                kind, idx = groups[gi]
                ob_ = 4 + 2 * (gi % 2)
                if kind == "A":
                    O0 = pbank(ob_)
                    O1 = pbank(ob_ + 1)
                    P.op("dve", lambda e, O0=O0: e.reciprocal(rz[0:64, 0, :], O0[64:128, :]),
                         reads=[O0[64:128, :]], writes=[rz[0:64, 0, :]])
                    P.tt(mixT[0:64, idx, :], O0[0:64, :], rz[0:64, 0, :], ALU.mult)
                    P.op("dve", lambda e, O1=O1: e.reciprocal(rz[64:128, 1, :], O1[0:64, :]),
                         reads=[O1[0:64, :]], writes=[rz[64:128, 1, :]])
                    P.tt(mixT[64:128, idx, :], O1[64:128, :], rz[64:128, 1, :], ALU.mult)
                    return
                h = idx
                r0 = 0 if h % 2 == 0 else 64
                z0 = 64 - r0
                for u in range(2):
                    Ou = pbank(ob_ + u)
                    P.op("dve", lambda e, Ou=Ou, u=u, r0=r0, z0=z0: e.reciprocal(rz[r0:r0 + 64, u, :], Ou[z0:z0 + 64, :]),
                         reads=[Ou[z0:z0 + 64, :]], writes=[rz[r0:r0 + 64, u, :]])
                    P.tt(ob[r0:r0 + 64, u, :], Ou[r0:r0 + 64, :], rz[r0:r0 + 64, u, :], ALU.mult)
                P.stt(bo[r0:r0 + 64, :], ob[r0:r0 + 64, 1, :], lp["nlam"][r0:r0 + 64, 0:1], ob[r0:r0 + 64, 0, :],
                      ALU.mult, ALU.add)
                if h % 2 == 1:
                    def subln(h=h):
                        P.act(sqb, bo, AF.Square)
                        pb = 2 * (len(pts) % 2)
                        P.mm(pbank(pb), bones, sqb)
                        P.act(lnv, pbank(pb), AF.Ln, bias=EPS, scale=1.0 / 64)
                        P.act(lnv, lnv, AF.Exp, scale=-0.5)
                        P.stt(mixT[:, 3 + h // 2, :], bo, lp["bsub"][:, 0:1], lnv, ALU.mult, ALU.mult)
                    pending.append((si + 10, subln))

            NS = len(steps)
            emit_S(0)
            for si in range(1, NS):
                emit_S(si)
                emit_AV(si - 1)
                for item in list(pending):
                    if item[0] <= si:
                        pending.remove(item)
                        item[1]()
            emit_AV(NS - 1)
            for item in list(pending):
                pending.remove(item)
                item[1]()
            if ("mixT%d" % c) in dbg_o:
                tmpf = A.alloc([128, 5, 512], F32)
                P.copy(tmpf, mixT)
                P.dma(dbg_o["mixT%d" % c], tmpf)
            if c + 1 < 4:
                adanorm(l, 0, c0 + 512, 512, hT)
            wov = w_out_d[l].rearrange("(k p) n -> p k n", p=128)
            for dp in range(4):
                wb = next_wsl()
                P.dma(wb[:, :, :], wov[:, :, dp * 256:(dp + 1) * 256], q="pool")
                for dd in range(2):
                    dt_ = dp * 2 + dd
                    pb = banks(1)
                    pt = pbank(pb)
                    for k in range(KD):
                        if k < 5:
                            rhs = mixT[:, k, :]
                        else:
                            rhs = yT[:, c * 4:(c + 1) * 4, k - 5, :]
                        P.mm(pt, wb[:, k, dd * 128:(dd + 1) * 128], rhs, start=(k == 0), stop=(k == KD - 1))
                    P.stt(xT[:, dt_, c0:c0 + 512], pt, mod[l][:, 16 + dt_:17 + dt_], xT[:, dt_, c0:c0 + 512],
                          ALU.mult, ALU.add)
        A.reset(mL)
        if stop_after == "M4" and l == 0:
            break

        A.reset(mLP)
        hT = A.alloc([128, KD, 1026], BF16)
        P.memset(hT, 0.0, eng="pool")
        hkeep = A.alloc([128, KD, 1], BF16)
        hmid = A.alloc([128, NFF, 1024], BF16)
        wd = [A.alloc([128, NFF, 128], BF16) for _ in range(2)]
        mOv = A.mark()
        Ub = [[A.alloc([128, 1026], F32) for _ in range(2)] for _ in range(2)]
        ab = [[A.alloc([128, 1024], F32) for _ in range(2)] for _ in range(2)]
        A.reset(mOv)
        upv = ffn_up_d[l].rearrange("(k p) n -> p k n", p=128)
        dnv = ffn_down_d[l].rearrange("(k p) n -> p k n", p=128)
        fc = [0]
        for hf in range(2):
            h0 = hf * 1024
            if hf == 0:
                adanorm(l, 1, 0, 1025, hT, col0=1)
                P.copy(hkeep, hT[:, :, 1024:1025], eng="pool")
            else:
                adanorm(l, 1, 1024, 1024, hT, col0=1)
                P.copy(hT[:, :, 0:1], hkeep, eng="pool")
            pend_f = []
            for s in range(11):
                wg = next_wsl()
                P.dma(wg[:, :, :], upv[:, :, s * 256:(s + 1) * 256], q="pool")
                wvv = next_wsl()
                P.dma(wvv[:, :, :], upv[:, :, D_FF + s * 256:D_FF + (s + 1) * 256], q="pool")
                for j in range(2):
                    i = s * 2 + j
                    par = i % 2
                    pts_ = []
                    for which, wb in ((0, wg), (1, wvv)):
                        pb = banks(3)
                        pt = pbank(pb, 3)
                        for (j0, w) in colgroups(1026):
                            for k in range(KD):
                                P.mm(pt[:, j0:j0 + w], wb[:, k, j * 128:(j + 1) * 128], hT[:, k, j0:j0 + w],
                                     start=(k == 0), stop=(k == KD - 1))
                        pts_.append(pt)
                    for which in range(2):
                        ft = i + which * NFF
                        conv_evac(Ub[par][which], ab[par][which], 1024, lp["fcw"][:, ft, :], lp["fcb"][:, ft:ft + 1], pts_[which])
                    for fn_ in pend_f:
                        fn_()
                    del pend_f[:]
                    for which in range(2):
                        ft = i + which * NFF
                        conv_tile(Ub[par][which], ab[par][which], 1024, lp["fcw"][:, ft, :], lp["fcb"][:, ft:ft + 1],
                                  lp["nfcw0"][:, ft:ft + 1], lp["nfcw2"][:, ft:ft + 1], h0, pts_[which], evac=False)

                    def fin(i=i, par=par):
                        P.act(ab[par][0], ab[par][0], AF.Silu)
                        P.tt(hmid[:, i, :], ab[par][1], ab[par][0], ALU.mult)
                    pend_f.append(fin)
            for fn_ in pend_f:
                fn_()
            del pend_f[:]
            for dt_ in range(8):
                wdb = wd[dt_ % 2]
                P.dma(wdb[:, :, :], dnv[:, :, dt_ * 128:(dt_ + 1) * 128], q="pool")
                for nch in range(2):
                    pb = banks(1)
                    pt = pbank(pb)
                    for i in range(NFF):
                        P.mm(pt, wdb[:, i, :], hmid[:, i, nch * 512:(nch + 1) * 512],
                             start=(i == 0), stop=(i == NFF - 1))
                    tsl = slice(h0 + nch * 512, h0 + (nch + 1) * 512)
                    P.stt(xT[:, dt_, tsl], pt, mod[l][:, 40 + dt_:41 + dt_], xT[:, dt_, tsl], ALU.mult, ALU.add)
        A.reset(mL)
        if stop_after == "F0" and l == 0:
            break

    A.reset(0)
    if stop_after is None or stop_after == "FIN":
        sq = [A.alloc([128, 512], BF16) for _ in range(2)]
        lnv = A.alloc([128, 512], F32)
        yn = A.alloc([128, KD, 512], F32)
        yo = [A.alloc([128, D], F32) for _ in range(2)]
        for c in range(4):
            c0 = c * 512
            pb = banks(1)
            for k in range(KD):
                P.act(sq[k % 2], xT[:, k, c0:c0 + 512], AF.Square)
                P.mm(pbank(pb), onesb, sq[k % 2], start=(k == 0), stop=(k == KD - 1))
            P.act(lnv, pbank(pb), AF.Ln, bias=EPS, scale=1.0 / D)
            P.act(lnv, lnv, AF.Exp, scale=-0.5)
            for k in range(KD):
                P.stt(yn[:, k, :], xT[:, k, c0:c0 + 512], fnw[:, k:k + 1], lnv, ALU.mult, ALU.mult)
            for t4 in range(4):
                yb = yo[t4 % 2]
                for hb in range(2):
                    pb2 = banks(1)
                    for k4 in range(4):
                        k = hb * 4 + k4
                        P.tr(pbank(pb2)[:, k4 * 128:(k4 + 1) * 128], yn[:, k, t4 * 128:(t4 + 1) * 128], identf)
                    P.copy(yb[:, hb * 512:(hb + 1) * 512], pbank(pb2), eng=("act" if hb else "dve"))
                tk = c * 4 + t4
                P.dma(y_o[tk * 128:(tk + 1) * 128, :], yb)
    else:
        pass
    if "xT" in dbg_o:
        P.dma(dbg_o["xT"], xT[:, :, :])
    P.finalize()
    P.arena_peak = A.peak
    return nc, P


def _rope_tables(L, d, grid_w=64, theta=10000.0):
    rows = L // grid_w
    row = np.repeat(np.arange(rows), grid_w).astype(np.float32)
    col = np.tile(np.arange(grid_w), rows).astype(np.float32)
    quarter = d // 4
    inv = (np.float32(theta) ** (-np.arange(quarter, dtype=np.float32) / np.float32(quarter))).astype(np.float32)
    ang_r = row[:, None] * inv[None, :]
    ang_c = col[:, None] * inv[None, :]
    ang = np.concatenate([ang_r, ang_r, ang_c, ang_c], axis=-1)
    return np.cos(ang).astype(np.float32), np.sin(ang).astype(np.float32)


def _perm_mat(d):
    q = d // 4
    Pm = np.zeros((128, 128), np.float32)
    for b0 in range(0, 128, d):
        for i in range(q):
            Pm[b0 + q + i, b0 + i] = -1.0
            Pm[b0 + i, b0 + q + i] = 1.0
            Pm[b0 + 3 * q + i, b0 + 2 * q + i] = -1.0
            Pm[b0 + 2 * q + i, b0 + 3 * q + i] = 1.0
    return Pm


def _consts():
    r = np.arange(128)
    c = {}
    c["identf"] = np.eye(128, dtype=np.float32)
    c["trif"] = (r[:, None] <= r[None, :]).astype(np.float32)
    c["trib"] = (r[:, None] >= r[None, :]).astype(np.float32)
    c["mnegf"] = np.where(r[None, :] < r[:, None], NEG, 0.0).astype(np.float32)
    c["mnegb"] = np.where(r[None, :] > r[:, None], NEG, 0.0).astype(np.float32)
    c["permA"] = _perm_mat(64)
    c["permB"] = _perm_mat(32)
    bo = np.zeros((128, 128), np.float32)
    bo[:64, :64] = 1.0
    bo[64:, 64:] = 1.0
    c["bones"] = bo
    return c


def _shared_weights(inp):
    f = lambda a: np.ascontiguousarray(np.asarray(a, dtype=np.float32))
    w = {}
    w["ada_w"] = f(inp["ada_w"])
    w["ada_b"] = f(np.asarray(inp["ada_b"]).reshape(NL, 48, 128).transpose(0, 2, 1))
    win = np.asarray(inp["w_in"], dtype=np.float32)
    o = np.cumsum([0, 384, 128, 128, 256, 256, 256, 384, 384, 256, 256, 12])
    aq, ak, av, bq, bk, bv, cx, cz, cB, cC, cdt = [win[:, :, o[i]:o[i + 1]] for i in range(11)]
    aqh = aq.reshape(NL, D, 6, 64)
    aqp = aqh[:, :, [0, 3, 1, 4, 2, 5], :].reshape(NL, D, 384)
    w["w_in"] = f(np.concatenate([aqp, ak, bq, bk, cx, cB, cC, av, bv, cz, cdt], axis=-1))
    tile2 = lambda v: f(np.concatenate([v, v], axis=-1)[:, :, None])
    w["aqn"] = tile2(np.asarray(inp["a_q_norm"]))
    w["akn"] = tile2(np.asarray(inp["a_k_norm"]))
    w["blam"] = f(np.asarray(inp["b_lambda"]).reshape(NL, 128))
    w["bsub"] = tile2(np.asarray(inp["b_subln"]))
    w["scw"] = f(np.asarray(inp["ssm_conv_w"]).reshape(NL, 3, 7, 128).transpose(0, 3, 2, 1))
    w["scb"] = f(np.asarray(inp["ssm_conv_b"]).reshape(NL, 7, 128).transpose(0, 2, 1))
    w["alog"] = f(np.asarray(inp["ssm_A_log"]).reshape(NL, 12))
    w["dtb"] = f(np.asarray(inp["ssm_dt_bias"]).reshape(NL, 12))
    w["ssmD"] = f(inp["ssm_D"])
    w["snw"] = f(inp["ssm_norm_w"])
    wo = np.asarray(inp["w_out"], dtype=np.float32)
    woa = wo[:, :384, :].reshape(NL, 6, 64, D)[:, [0, 3, 1, 4, 2, 5]].reshape(NL, 384, D)
    w["w_out"] = f(np.concatenate([woa, wo[:, 384:, :]], axis=1))
    w["ffn_up"] = f(inp["ffn_up"])
    w["fcw"] = f(np.asarray(inp["ffn_conv_w"]).reshape(NL, 3, 44, 128).transpose(0, 3, 2, 1))
    w["fcb"] = f(np.asarray(inp["ffn_conv_b"]).reshape(NL, 44, 128).transpose(0, 2, 1))
    w["ffn_down"] = f(inp["ffn_down"])
    w["fnw"] = f(np.asarray(inp["final_norm_w"]).reshape(8, 128).T)
    return w


def _core_inputs(inp, core, shared, consts, ropeS):
    f = lambda a: np.ascontiguousarray(np.asarray(a, dtype=np.float32))
    m = dict(shared)
    m.update(consts)
    is_prompt = core >= 4
    if not is_prompt:
        b = core
        m["x"] = f(inp["x_sample"][b])
        cvec = np.asarray(inp["c"])[b]
        m["cak"] = f(np.asarray(inp["cache_a_k"])[b].reshape(NL, 256, 128))
        m["cav"] = f(np.asarray(inp["cache_a_v"])[b].reshape(NL, 256, 128))
        m["cbk"] = f(np.asarray(inp["cache_b_k"])[b].reshape(NL, 256, 256))
        m["cbv"] = f(np.asarray(inp["cache_b_v"])[b].reshape(NL, 256, 256))
        m["ssm0"] = f(np.asarray(inp["state_ssm"])[b].reshape(NL, 2, 384, 128))
        mq = np.zeros((9, T), np.float32)
        mq[8] = 1.0
        mk = np.zeros((9, NKEY), np.float32)
        m["cosA"], m["sinA"], m["cosB"], m["sinB"] = ropeS
        m["pflag"] = np.zeros((128, 1), np.float32)
        m["seqf"] = np.ones((128, 16), np.float32)
    else:
        j = core - 4
        m["x"] = f(np.asarray(inp["x_prompt"])[8 * j:8 * j + 8].reshape(T, D))
        cvec = np.asarray(inp["c_ctx"])
        m["cak"] = np.zeros((NL, 256, 128), np.float32)
        m["cav"] = np.zeros((NL, 256, 128), np.float32)
        m["cbk"] = np.zeros((NL, 256, 256), np.float32)
        m["cbv"] = np.zeros((NL, 256, 256), np.float32)
        m["ssm0"] = np.zeros((NL, 2, 384, 128), np.float32)
        seq_q = np.arange(T) // 256
        mq = np.zeros((9, T), np.float32)
        mq[seq_q, np.arange(T)] = 1.0
        mq[8] = 1.0
        mk = np.zeros((9, NKEY), np.float32)
        mk[seq_q, np.arange(T)] = MASKV
        mk[8] = -MASKV
        one = np.ones((128, T), np.float32)
        zero = np.zeros((128, T), np.float32)
        m["cosA"], m["sinA"], m["cosB"], m["sinB"] = one, zero, one, zero
        m["pflag"] = np.ones((128, 1), np.float32)
        sf = np.ones((128, 16), np.float32)
        sf[:, 1::2] = 0.0
        m["seqf"] = sf
    m["cvec"] = f(np.asarray(cvec).reshape(8, 128).T)
    m["mq"] = mq
    m["mk"] = mk
    return m


def make_in_maps(inp):
    shared = _shared_weights(inp)
    consts = _consts()
    cA, sA_ = _rope_tables(T, 64)
    cB, sB_ = _rope_tables(T, 32)
    tA = lambda a: np.ascontiguousarray(np.tile(a.T, (2, 1)))
    tB = lambda a: np.ascontiguousarray(np.tile(a.T, (4, 1)))
    ropeS = (tA(cA), tA(sA_), tB(cB), tB(sB_))
    return [_core_inputs(inp, core, shared, consts, ropeS) for core in range(8)]


_NC_CACHE = {}


def kernel(**inputs):
    in_maps = make_in_maps(inputs)
    if "nc" not in _NC_CACHE:
        _NC_CACHE["nc"] = build()[0]
    nc = _NC_CACHE["nc"]
    res = run_bass_kernel_spmd(nc, in_maps, core_ids=list(range(8)))
    r = res.results
    B = 32
    y_sample = np.stack([r[b]["y"] for b in range(4)], axis=0).astype(np.float32)
    y_prompt = np.concatenate([r[4 + j]["y"].reshape(8, 256, D) for j in range(4)], axis=0).astype(np.float32)

    def gather(name, tail):
        parts = []
        for j in range(4):
            a = r[4 + j][name]
            a = a.reshape(NL, 8, 256, -1).transpose(1, 0, 2, 3)
            parts.append(a)
        a = np.concatenate(parts, axis=0)
        return np.ascontiguousarray(a.reshape((B, NL, 256) + tail)).astype(np.float32)

    new_a_k = gather("nak", (2, 64))
    new_a_v = gather("nav", (2, 64))
    new_b_k = gather("nbk", (4, 2, 32))
    new_b_v = gather("nbv", (4, 64))
    parts = []
    for j in range(4):
        a = r[4 + j]["nssm"]
        parts.append(a.transpose(1, 0, 2, 3, 4))
    new_ssm = np.ascontiguousarray(np.concatenate(parts, axis=0).reshape(B, NL, 2, 6, 64, 128)).astype(np.float32)
    return (y_prompt, y_sample, new_a_k, new_a_v, new_b_k, new_b_v, new_ssm)
```
